# Optimizing a Trainium2 kernel written in Bass

```python
import math, functools
import jax, jax.numpy as jnp
from jax import lax
import numpy as np

D_MODEL = 1024
BATCH = 32
SEQ = 256
DEPTH = 2
DEC_BATCH = 8
DEC_SEQ = 4096
PAST_LEN = 256

GRID_W = 64
HEAD_DIM = 64
BLOCK = 128
ROPE_BASE = 10000.0
N_AB = (DEPTH + 1) // 2
N_CD = DEPTH // 2
A_HEADS = (D_MODEL // 2) // HEAD_DIM
A_KV_HEADS = A_HEADS // 4
A_REP = A_HEADS // A_KV_HEADS
WINDOW = 128
B_HEADS = (D_MODEL // 2) // HEAD_DIM
B_DK = HEAD_DIM
B_DV = HEAD_DIM
RET_CHUNK = 128
C_WIDTH = D_MODEL // 2
C_GROUP = 16
C_GROUPS = C_WIDTH // C_GROUP
C_STATE = 64
D_QK = HEAD_DIM
D_V = 2 * HEAD_DIM
D_HEADS = (D_MODEL // 2) // D_V
FFN_HIDDEN = -(-8 * D_MODEL // (3 * 256)) * 256
AB_SPLITS = (A_HEADS * HEAD_DIM, A_KV_HEADS * HEAD_DIM, A_KV_HEADS * HEAD_DIM,
             B_HEADS * B_DK, B_HEADS * B_DK, B_HEADS * B_DV, B_HEADS * B_DV)
AB_COLS = sum(AB_SPLITS)
AB_OUT = A_HEADS * HEAD_DIM + B_HEADS * B_DV
CD_SPLITS = (C_WIDTH, D_HEADS * 2 * D_QK, D_HEADS * 2 * D_QK, D_HEADS * D_V)
CD_COLS = sum(CD_SPLITS)
CD_OUT = C_WIDTH + D_HEADS * D_V
NEG_INF = -1e30

kernel_name = 'hybrid_diffusion_prefix_trunk_step'


def rms_norm(x, g, eps=1e-6):
    xf = x.astype(jnp.float32)
    y = xf * lax.rsqrt(jnp.mean(xf * xf, axis=-1, keepdims=True) + eps)
    return (y * g.astype(jnp.float32)).astype(x.dtype)


def split_cols(p, sizes):
    return jnp.split(p, np.cumsum(sizes)[:-1].tolist(), axis=-1)


def ada_modulation(cond, w, b):
    m = jax.nn.silu(cond) @ w + b
    return jnp.split(m[..., None, :], 6, axis=-1)


def swiglu(h, w1, w3, w2):
    return (jax.nn.silu(h @ w1) * (h @ w3)) @ w2


def axial_rope(n_tok, dim):
    rows = n_tok // GRID_W
    r = jnp.repeat(jnp.arange(rows, dtype=jnp.float32), GRID_W)
    col = jnp.tile(jnp.arange(GRID_W, dtype=jnp.float32), rows)
    n_freq = dim // 4
    inv = ROPE_BASE ** (-jnp.arange(n_freq, dtype=jnp.float32) / n_freq)
    ang = jnp.concatenate([r[:, None] * inv, col[:, None] * inv], axis=-1)
    return jnp.cos(ang), jnp.sin(ang)


def apply_rope(x, rope):
    cos, sin = rope
    half = x.shape[-1] // 2
    shp = (cos.shape[0],) + (1,) * (x.ndim - 3) + (half,)
    cos = cos.reshape(shp)
    sin = sin.reshape(shp)
    xf = x.astype(jnp.float32)
    x1, x2 = xf[..., :half], xf[..., half:]
    return jnp.concatenate([x1 * cos - x2 * sin, x2 * cos + x1 * sin], axis=-1).astype(x.dtype)


def over_query_blocks(fn, q):
    nb = q.shape[1] // BLOCK
    def one(bi):
        return fn(bi, lax.dynamic_slice_in_dim(q, bi * BLOCK, BLOCK, axis=1))
    out = lax.map(one, jnp.arange(nb))
    out = jnp.moveaxis(out, 0, 1)
    return out.reshape(out.shape[:1] + (nb * BLOCK,) + out.shape[3:])


def sink_attend(q, k, v, mask, sink):
    s = jnp.einsum('bqgrd,bkgd->bgrqk', q, k, preferred_element_type=jnp.float32) * (HEAD_DIM ** -0.5)
    if mask is not None:
        s = jnp.where(mask, s, NEG_INF)
    sk = sink.astype(jnp.float32)[None, :, :, None, None]
    m = jnp.maximum(jnp.max(s, axis=-1, keepdims=True), sk)
    p = jnp.exp(s - m)
    p = p / (jnp.sum(p, axis=-1, keepdims=True) + jnp.exp(sk - m))
    return jnp.einsum('bgrqk,bkgd->bqgrd', p.astype(v.dtype), v)


def window_attn_latent(q, k, v, ck, cv, sink):
    T = q.shape[1]
    Tc = ck.shape[1]
    span = BLOCK + 2 * WINDOW
    pad = ((0, 0), (WINDOW, WINDOW), (0, 0), (0, 0))
    kp = jnp.pad(k, pad)
    vp = jnp.pad(v, pad)
    def block(bi, qb):
        q0 = bi * BLOCK
        kb = lax.dynamic_slice_in_dim(kp, q0, span, axis=1)
        vb = lax.dynamic_slice_in_dim(vp, q0, span, axis=1)
        qpos = q0 + jnp.arange(BLOCK)
        kpos = q0 - WINDOW + jnp.arange(span)
        band = ((jnp.abs(qpos[:, None] - kpos[None, :]) <= WINDOW)
                & (kpos >= 0)[None, :] & (kpos < T)[None, :])
        mask = jnp.concatenate([jnp.ones((BLOCK, Tc), dtype=bool), band], axis=1)
        return sink_attend(qb, jnp.concatenate([ck, kb], axis=1),
                           jnp.concatenate([cv, vb], axis=1), mask, sink)
    return over_query_blocks(block, q)


def diff_attend(q, k, v, lam):
    s = jnp.einsum('bqhcd,bkhcd->bhcqk', q, k, preferred_element_type=jnp.float32) * (D_QK ** -0.5)
    p = jax.nn.softmax(s, axis=-1)
    w = p[:, :, 0] - lam * p[:, :, 1]
    return jnp.einsum('bhqk,bkhe->bqhe', w.astype(v.dtype), v)


def retention_scan(q, k, v, log_g, s0):
    B, T, H, dk = q.shape
    dv = v.shape[-1]
    L = RET_CHUNK
    n = T // L
    lg = log_g.astype(jnp.float32)
    idx = jnp.arange(L, dtype=jnp.float32)
    diff = idx[:, None] - idx[None, :]
    inner_decay = jnp.where(diff >= 0, jnp.exp(lg[:, None, None] * jnp.maximum(diff, 0.0)), 0.0)
    q_decay = jnp.exp(lg[None, :] * (idx[:, None] + 1.0))
    k_decay = jnp.exp(lg[None, :] * (L - 1.0 - idx[:, None]))
    chunk_decay = jnp.exp(lg * L)
    qc = q.reshape(B, n, L, H, dk)
    kc = k.reshape(B, n, L, H, dk)
    vc = v.reshape(B, n, L, H, dv).astype(jnp.float32)
    s = jnp.einsum('bnihd,bnjhd->bnhij', qc, kc, preferred_element_type=jnp.float32) * inner_decay
    inner = jnp.einsum('bnhij,bnjhe->bnihe', s, vc)
    kv = jnp.einsum('bnjhd,bnjhe->nbhde', kc * k_decay[:, :, None], vc)
    def step(state, kv_c):
        return chunk_decay[None, :, None, None] * state + kv_c, state
    s_last, s_prev = lax.scan(step, s0.astype(jnp.float32), kv)
    cross = jnp.einsum('bnihd,nbhde->bnihe', qc * q_decay[:, :, None], s_prev)
    return (inner + cross).reshape(B, T, H, dv), s_last


def retention_bidir(q, k, v, log_g, s0):
    of, sf = retention_scan(q, k, v, log_g[0], s0[:, 0])
    ob, sb = retention_scan(jnp.flip(q, 1), jnp.flip(k, 1), jnp.flip(v, 1), log_g[1], s0[:, 1])
    return of + jnp.flip(ob, 1), jnp.stack([sf, sb], axis=1)


def s5_zoh(lam_re, lam_im, log_dt):
    lr = lam_re.astype(jnp.float32)
    li = lam_im.astype(jnp.float32)
    dt = jnp.exp(log_dt.astype(jnp.float32))[:, None]
    mag = jnp.exp(lr * dt)
    a_re = mag * jnp.cos(li * dt)
    a_im = mag * jnp.sin(li * dt)
    den = lr * lr + li * li
    f_re = ((a_re - 1.0) * lr + a_im * li) / den
    f_im = (a_im * lr - (a_re - 1.0) * li) / den
    return a_re, a_im, f_re, f_im


def s5_combine(e1, e2):
    a1r, a1i, b1r, b1i = e1
    a2r, a2i, b2r, b2i = e2
    return (a1r * a2r - a1i * a2i, a1r * a2i + a1i * a2r,
            a2r * b1r - a2i * b1i + b2r, a2r * b1i + a2i * b1r + b2i)


def s5_bidir(u, lam_re, lam_im, log_dt, b_re, b_im, c_re, c_im, d_skip, h0_re, h0_im):
    B, T, _ = u.shape
    ug = u.reshape(B, T, C_GROUPS, C_GROUP).astype(jnp.float32)
    bu_re = jnp.einsum('btgc,gpc->btgp', ug, b_re.astype(jnp.float32))
    bu_im = jnp.einsum('btgc,gpc->btgp', ug, b_im.astype(jnp.float32))
    hs_re, hs_im, fin_re, fin_im = [], [], [], []
    for dr, reverse in ((0, False), (1, True)):
        a_re, a_im, f_re, f_im = s5_zoh(lam_re[dr], lam_im[dr], log_dt[dr])
        x_re = f_re * bu_re - f_im * bu_im
        x_im = f_re * bu_im + f_im * bu_re
        i_re = h0_re[:, dr].astype(jnp.float32)
        i_im = h0_im[:, dr].astype(jnp.float32)
        first = T - 1 if reverse else 0
        x_re = x_re.at[:, first].add(a_re * i_re - a_im * i_im)
        x_im = x_im.at[:, first].add(a_re * i_im + a_im * i_re)
        ar = jnp.broadcast_to(a_re, (1, T) + a_re.shape)
        ai = jnp.broadcast_to(a_im, (1, T) + a_im.shape)
        _, _, h_re, h_im = lax.associative_scan(s5_combine, (ar, ai, x_re, x_im), reverse=reverse, axis=1)
        last = 0 if reverse else T - 1
        fin_re.append(h_re[:, last])
        fin_im.append(h_im[:, last])
        hs_re.append(h_re)
        hs_im.append(h_im)
    h_re = hs_re[0] + hs_re[1]
    h_im = hs_im[0] + hs_im[1]
    y = (jnp.einsum('btgp,gcp->btgc', h_re, c_re.astype(jnp.float32))
         - jnp.einsum('btgp,gcp->btgc', h_im, c_im.astype(jnp.float32)))
    y = y.reshape(B, T, C_WIDTH) + d_skip.astype(jnp.float32) * u.astype(jnp.float32)
    return y.astype(u.dtype), jnp.stack(fin_re, axis=1), jnp.stack(fin_im, axis=1)


def ab_mixer(h, rope, ctx, w_in, w_out, q_g, k_g, sink, ret_decay, ret_g):
    B, T, _ = h.shape
    qa, ka, va, qb, kb, vb, gb = split_cols(h @ w_in, AB_SPLITS)
    qa = rms_norm(qa.reshape(B, T, A_KV_HEADS, A_REP, HEAD_DIM), q_g)
    ka = rms_norm(ka.reshape(B, T, A_KV_HEADS, HEAD_DIM), k_g)
    va = va.reshape(B, T, A_KV_HEADS, HEAD_DIM)
    qb = qb.reshape(B, T, B_HEADS, B_DK) * (B_DK ** -0.5)
    kb = kb.reshape(B, T, B_HEADS, B_DK)
    vb = vb.reshape(B, T, B_HEADS, B_DV)
    sink_r = sink.reshape(A_KV_HEADS, A_REP)
    log_g = jax.nn.log_sigmoid(ret_decay.astype(jnp.float32))
    if ctx is None:
        oa = over_query_blocks(lambda bi, qblk: sink_attend(qblk, ka, va, None, sink_r), qa)
        s0 = jnp.zeros((B, 2, B_HEADS, B_DK, B_DV), jnp.float32)
        ob, s_ret = retention_bidir(qb, kb, vb, log_g, s0)
        ctx_out = (ka, va, s_ret)
    else:
        ck, cv, s0 = ctx
        oa = window_attn_latent(apply_rope(qa, rope), apply_rope(ka, rope), va, ck, cv, sink_r)
        ob, _ = retention_bidir(qb, kb, vb, log_g, s0)
        ctx_out = None
    ob = rms_norm(ob, ret_g).astype(h.dtype) * jax.nn.silu(gb.reshape(B, T, B_HEADS, B_DV))
    out = jnp.concatenate([oa.reshape(B, T, -1), ob.reshape(B, T, -1)], axis=-1) @ w_out
    return out, ctx_out


def cd_mixer(h, rope, ctx, w_in, w_out, lam_re, lam_im, log_dt, b_re, b_im, c_re, c_im, d_skip,
             glu_w, glu_b, q_g, k_g, lam_p, subln_g, lam_init):
    B, T, _ = h.shape
    u, qd, kd, vd = split_cols(h @ w_in, CD_SPLITS)
    qd = rms_norm(qd.reshape(B, T, D_HEADS, 2, D_QK), q_g)
    kd = rms_norm(kd.reshape(B, T, D_HEADS, 2, D_QK), k_g)
    vd = vd.reshape(B, T, D_HEADS, D_V)
    lp = lam_p.astype(jnp.float32)
    lam = jnp.exp(jnp.sum(lp[0] * lp[1])) - jnp.exp(jnp.sum(lp[2] * lp[3])) + lam_init
    if ctx is None:
        h0 = jnp.zeros((B, 2, C_GROUPS, C_STATE), jnp.float32)
        y, fin_re, fin_im = s5_bidir(u, lam_re, lam_im, log_dt, b_re, b_im, c_re, c_im, d_skip, h0, h0)
        od = over_query_blocks(lambda bi, qblk: diff_attend(qblk, kd, vd, lam), qd)
        ctx_out = (fin_re, fin_im, kd, vd)
    else:
        h0_re, h0_im, ck, cv = ctx
        y, _, _ = s5_bidir(u, lam_re, lam_im, log_dt, b_re, b_im, c_re, c_im, d_skip, h0_re, h0_im)
        k_all = jnp.concatenate([ck, apply_rope(kd, rope)], axis=1)
        v_all = jnp.concatenate([cv, vd], axis=1)
        od = over_query_blocks(lambda bi, qblk: diff_attend(qblk, k_all, v_all, lam), apply_rope(qd, rope))
        ctx_out = None
    g = jax.nn.gelu(y)
    oc = g * jax.nn.sigmoid(g @ glu_w + glu_b)
    od = rms_norm(od, subln_g) * (1.0 - lam_init)
    out = jnp.concatenate([oc, od.reshape(B, T, -1)], axis=-1) @ w_out
    return out, ctx_out


def trunk_layer(x, cond, mixer, ada_w, ada_b, n1_g, n2_g, w1, w3, w2):
    sh1, sc1, g1, sh2, sc2, g2 = ada_modulation(cond, ada_w, ada_b)
    h = rms_norm(x, n1_g) * (1.0 + sc1) + sh1
    out, ctx_out = mixer(h)
    x = x + g1 * out
    h = rms_norm(x, n2_g) * (1.0 + sc2) + sh2
    x = x + g2 * swiglu(h, w1, w3, w2)
    return x, ctx_out


def setup_inputs(seed: int = 0) -> dict:
    key = jax.random.key(seed)
    keys = iter(jax.random.split(key, 64))
    f32 = jnp.float32
    def nrm(shape, scale=1.0):
        return scale * jax.random.normal(next(keys), shape, f32)
    ret_e = 5.0 + jnp.arange(B_HEADS, dtype=f32)
    ret_logit = jnp.log1p(-(2.0 ** -ret_e)) + ret_e * math.log(2.0)
    n_idx = jnp.arange(C_STATE, dtype=f32)
    return {
        'x_prompt': nrm((BATCH, SEQ, D_MODEL)),
        'x_sample': nrm((DEC_BATCH, DEC_SEQ, D_MODEL)),
        'c': nrm((DEC_BATCH, D_MODEL)),
        'c_ctx': nrm((D_MODEL,)),
        'cache_k_a': nrm((DEC_BATCH, N_AB, PAST_LEN, A_KV_HEADS, HEAD_DIM)),
        'cache_v_a': nrm((DEC_BATCH, N_AB, PAST_LEN, A_KV_HEADS, HEAD_DIM)),
        'state_ret': nrm((DEC_BATCH, N_AB, 2, B_HEADS, B_DK, B_DV)),
        'state_ssm_re': nrm((DEC_BATCH, N_CD, 2, C_GROUPS, C_STATE), 0.1),
        'state_ssm_im': nrm((DEC_BATCH, N_CD, 2, C_GROUPS, C_STATE), 0.1),
        'cache_k_d': nrm((DEC_BATCH, N_CD, PAST_LEN, D_HEADS, 2, D_QK)),
        'cache_v_d': nrm((DEC_BATCH, N_CD, PAST_LEN, D_HEADS, D_V)),
        'ada_w': nrm((DEPTH, D_MODEL, 6 * D_MODEL), 0.5 * D_MODEL ** -0.5),
        'ada_b': nrm((DEPTH, 6 * D_MODEL), 0.02),
        'norm1_g': 1.0 + nrm((DEPTH, D_MODEL), 0.02),
        'norm2_g': 1.0 + nrm((DEPTH, D_MODEL), 0.02),
        'ffn_w1': nrm((DEPTH, D_MODEL, FFN_HIDDEN), D_MODEL ** -0.5),
        'ffn_w3': nrm((DEPTH, D_MODEL, FFN_HIDDEN), D_MODEL ** -0.5),
        'ffn_w2': nrm((DEPTH, FFN_HIDDEN, D_MODEL), FFN_HIDDEN ** -0.5),
        'ab_w_in': nrm((N_AB, D_MODEL, AB_COLS), D_MODEL ** -0.5),
        'ab_w_out': nrm((N_AB, AB_OUT, D_MODEL), AB_OUT ** -0.5),
        'a_q_norm': 1.0 + nrm((N_AB, HEAD_DIM), 0.02),
        'a_k_norm': 1.0 + nrm((N_AB, HEAD_DIM), 0.02),
        'a_sink': nrm((N_AB, A_HEADS), 0.5),
        'ret_decay': ret_logit + nrm((N_AB, 2, B_HEADS), 0.1),
        'ret_norm': 1.0 + nrm((N_AB, B_DV), 0.02),
        'cd_w_in': nrm((N_CD, D_MODEL, CD_COLS), D_MODEL ** -0.5),
        'cd_w_out': nrm((N_CD, CD_OUT, D_MODEL), CD_OUT ** -0.5),
        'ssm_lambda_re': -0.5 + nrm((N_CD, 2, C_GROUPS, C_STATE), 0.01),
        'ssm_lambda_im': math.pi * n_idx + nrm((N_CD, 2, C_GROUPS, C_STATE), 0.01),
        'ssm_log_dt': jax.random.uniform(next(keys), (N_CD, 2, C_GROUPS), f32, math.log(1e-3), math.log(1e-1)),
        'ssm_b_re': nrm((N_CD, C_GROUPS, C_STATE, C_GROUP), (2 * C_GROUP) ** -0.5),
        'ssm_b_im': nrm((N_CD, C_GROUPS, C_STATE, C_GROUP), (2 * C_GROUP) ** -0.5),
        'ssm_c_re': nrm((N_CD, C_GROUPS, C_GROUP, C_STATE), (2 * C_STATE) ** -0.5),
        'ssm_c_im': nrm((N_CD, C_GROUPS, C_GROUP, C_STATE), (2 * C_STATE) ** -0.5),
        'ssm_d': nrm((N_CD, C_WIDTH)),
        'ssm_glu_w': nrm((N_CD, C_WIDTH, C_WIDTH), C_WIDTH ** -0.5),
        'ssm_glu_b': nrm((N_CD, C_WIDTH), 0.02),
        'd_q_norm': 1.0 + nrm((N_CD, D_QK), 0.02),
        'd_k_norm': 1.0 + nrm((N_CD, D_QK), 0.02),
        'd_lambda': nrm((N_CD, 4, D_QK), 0.1),
        'd_subln': 1.0 + nrm((N_CD, D_V), 0.02),
    }


def reference(x_prompt, x_sample, c, c_ctx, cache_k_a, cache_v_a, state_ret, state_ssm_re, state_ssm_im,
              cache_k_d, cache_v_d, ada_w, ada_b, norm1_g, norm2_g, ffn_w1, ffn_w3, ffn_w2,
              ab_w_in, ab_w_out, a_q_norm, a_k_norm, a_sink, ret_decay, ret_norm,
              cd_w_in, cd_w_out, ssm_lambda_re, ssm_lambda_im, ssm_log_dt, ssm_b_re, ssm_b_im,
              ssm_c_re, ssm_c_im, ssm_d, ssm_glu_w, ssm_glu_b, d_q_norm, d_k_norm, d_lambda, d_subln):
    rope = axial_rope(x_sample.shape[1], HEAD_DIM)
    yp, ys = x_prompt, x_sample
    new_ka, new_va, new_ret, new_sr, new_si, new_kd, new_vd = [], [], [], [], [], [], []
    for l in range(DEPTH):
        j = l // 2
        ffn_p = (ada_w[l], ada_b[l], norm1_g[l], norm2_g[l], ffn_w1[l], ffn_w3[l], ffn_w2[l])
        if l % 2 == 0:
            mix_p = (ab_w_in[j], ab_w_out[j], a_q_norm[j], a_k_norm[j], a_sink[j], ret_decay[j], ret_norm[j])
            yp, (k_a, v_a, s_r) = trunk_layer(yp, c_ctx, lambda h: ab_mixer(h, None, None, *mix_p), *ffn_p)
            ctx_l = (cache_k_a[:, j], cache_v_a[:, j], state_ret[:, j])
            ys, _ = trunk_layer(ys, c, lambda h: ab_mixer(h, rope, ctx_l, *mix_p), *ffn_p)
            new_ka.append(k_a)
            new_va.append(v_a)
            new_ret.append(s_r)
        else:
            lam_init = 0.8 - 0.6 * math.exp(-0.3 * l)
            mix_p = (cd_w_in[j], cd_w_out[j], ssm_lambda_re[j], ssm_lambda_im[j], ssm_log_dt[j],
                     ssm_b_re[j], ssm_b_im[j], ssm_c_re[j], ssm_c_im[j], ssm_d[j], ssm_glu_w[j], ssm_glu_b[j],
                     d_q_norm[j], d_k_norm[j], d_lambda[j], d_subln[j], lam_init)
            yp, (s_re, s_im, k_d, v_d) = trunk_layer(yp, c_ctx, lambda h: cd_mixer(h, None, None, *mix_p), *ffn_p)
            ctx_l = (state_ssm_re[:, j], state_ssm_im[:, j], cache_k_d[:, j], cache_v_d[:, j])
            ys, _ = trunk_layer(ys, c, lambda h: cd_mixer(h, rope, ctx_l, *mix_p), *ffn_p)
            new_sr.append(s_re)
            new_si.append(s_im)
            new_kd.append(k_d)
            new_vd.append(v_d)
    new_cache_k_a = jnp.stack(new_ka, axis=1)
    new_cache_v_a = jnp.stack(new_va, axis=1)
    new_state_ret = jnp.stack(new_ret, axis=1)
    new_state_ssm_re = jnp.stack(new_sr, axis=1)
    new_state_ssm_im = jnp.stack(new_si, axis=1)
    new_cache_k_d = jnp.stack(new_kd, axis=1)
    new_cache_v_d = jnp.stack(new_vd, axis=1)
    return (yp, ys, new_cache_k_a, new_cache_v_a, new_state_ret, new_state_ssm_re, new_state_ssm_im,
            new_cache_k_d, new_cache_v_d)
```

```python
import math
import os
import numpy as np
from contextlib import ExitStack
import concourse.bass as bass
import concourse.mybir as mybir
from concourse.bass_utils import run_bass_kernel_spmd

F32 = mybir.dt.float32
BF16 = mybir.dt.bfloat16
I32 = mybir.dt.int32
AF = mybir.ActivationFunctionType
ALU = mybir.AluOpType
AX = mybir.AxisListType

NCORES = 8
D = 1024
TS = 4096
TP = 1024
TOK = TS + TP
NT = TOK // 128
NTS = TS // 128
FF = 2816
EPS = 1e-6


class Ev:
    __slots__ = ("sem", "sid", "val")

    def __init__(self, sem, sid, val):
        self.sem, self.sid, self.val = sem, sid, val


class Buf:
    __slots__ = ("name", "w", "r", "excl")

    def __init__(self, name="", excl=False):
        self.name = name
        self.w = None
        self.r = {}
        self.excl = excl


def _bufs(xs):
    out = []
    for x in xs:
        if x is None:
            continue
        out.append(x.b if isinstance(x, T) else x)
    return out


class Eng:
    def __init__(self, S, e, name):
        self.S, self.e, self.name = S, e, name
        self.sem = S.new_sem("e_" + name)
        self.sid = id(self.sem)
        self.cnt = 0
        self.waited = {}
        self.pending = []
        self.dsems = []
        self.dvals = []
        self.dnext = 0

    def wait(self, ev):
        if ev is None or ev.sid == self.sid:
            return
        assert ev.val is not None, "waiting on unresolved pending event"
        if self.waited.get(ev.sid, 0) >= ev.val:
            return
        self.e.wait_ge(ev.sem, ev.val)
        self.waited[ev.sid] = ev.val

    def deps(self, reads, writes, nosync=False):
        for b in reads:
            w = b.w
            if w is not None and w.sid == self.sid and nosync:
                continue
            if w is not None and w.sid == self.sid and self.name != "pe":
                if self.waited.get(w.sid, 0) < w.val:
                    self.e.wait_ge(w.sem, w.val)
                    self.waited[w.sid] = w.val
            else:
                self.wait(w)
            if b.excl:
                for r in b.r.values():
                    self.wait(r)
        for b in writes:
            self.wait(b.w)
            for r in b.r.values():
                self.wait(r)

    def mark(self, ev, reads, writes):
        for b in reads:
            b.r[ev.sid] = ev
        for b in writes:
            b.w = ev
            b.r = {}

    def op(self, ins_fn, reads=(), writes=(), inc=True, selfsync=False, nosync=False):
        reads, writes = _bufs(reads), _bufs(writes)
        self.deps(reads, writes, nosync)
        ins = ins_fn()
        ev = Ev(self.sem, self.sid, None)
        self.pending.append(ev)
        self.mark(ev, reads, writes)
        if inc:
            self.cnt += 1
            ins.then_inc(self.sem, 1)
            for p in self.pending:
                p.val = self.cnt
            self.pending = []
            if selfsync:
                self.e.wait_ge(self.sem, self.cnt)
        return ins

    def dma(self, out, in_, reads=(), writes=(), **kw):
        reads, writes = _bufs(reads), _bufs(writes)
        S = self.S
        if not self.dsems:
            for i in range(S.dma_pool):
                self.dsems.append(S.new_sem("d_%s%d" % (self.name, i)))
                self.dvals.append(0)
        k = self.dnext
        self.dnext = (k + 1) % len(self.dsems)
        sem = self.dsems[k]
        if self.dvals[k] > 0:
            self.wait(Ev(sem, id(sem), self.dvals[k]))
        self.deps(reads, writes)
        self.dvals[k] += 16
        self.e.dma_start(out=out, in_=in_, **kw).then_inc(sem, 16)
        ev = Ev(sem, id(sem), self.dvals[k])
        self.mark(ev, reads, writes)
        return ev


class T:
    def __init__(self, t, name=""):
        self.t = t
        self.b = Buf(name)

    def __getitem__(self, k):
        return self.t[k]


class Sched:
    def __init__(self, nc, dma_pool=12):
        self.nc = nc
        self.stack = ExitStack()
        self.dma_pool = dma_pool
        self.pe = Eng(self, nc.tensor, "pe")
        self.act = Eng(self, nc.scalar, "act")
        self.dve = Eng(self, nc.vector, "dve")
        self.pool = Eng(self, nc.gpsimd, "pool")
        self.sp = Eng(self, nc.sync, "sp")
        self.engs = [self.pe, self.act, self.dve, self.pool, self.sp]
        self.uid = 0

    def new_sem(self, name):
        return self.stack.enter_context(self.nc.semaphore(name))

    def sb(self, name, shape, dtype, stack=None):
        self.uid += 1
        t = (stack or self.stack).enter_context(
            self.nc.sbuf_tensor("%s_%d" % (name, self.uid), list(shape), dtype))
        return T(t, name)

    def ps(self, name, shape, dtype=F32, stack=None):
        self.uid += 1
        t = (stack or self.stack).enter_context(
            self.nc.psum_tensor("%s_%d" % (name, self.uid), list(shape), dtype))
        tt_ = T(t, name)
        tt_.b.excl = True
        return tt_

    def all_events(self):
        evs = []
        for E in self.engs:
            assert not E.pending, "pending events on %s" % E.name
            if E.cnt:
                evs.append(Ev(E.sem, E.sid, E.cnt))
            for s, v in zip(E.dsems, E.dvals):
                if v:
                    evs.append(Ev(s, id(s), v))
        return evs

    def barrier(self):
        evs = self.all_events()
        for E in self.engs:
            for ev in evs:
                E.wait(ev)

    def finish(self):
        for ev in self.all_events():
            self.sp.wait(ev)


def AP(base, dims):
    return bass.AP(base.tensor, base.offset, [list(base.ap[0])] + [list(d) for d in dims])


class StopBuild(Exception):
    pass


class Builder:
    def __init__(self, debug=False, stop_after=None):
        self.debug = debug
        self.stop_after = stop_after
        self.nc = nc = bass.Bass("TRN2", target_bir_lowering=False)
        self.S = S = Sched(nc)
        self.dbg_names = []
        self.declare_io()
        self.setup_consts()
        done = False
        try:
            self.run_phases()
        except StopBuild:
            return
        S.finish()
        S.stack.close()

    def run_phases(self):
        S = self.S
        stop_after = self.stop_after
        for name, fn in [("ada", self.phase_ada), ("l0in", self.l0_in), ("l0a", self.l0_attn_a),
                         ("l0b", self.l0_ret_b), ("l0ffn", lambda: self.out_ffn(0)),
                         ("l1in", self.l1_in), ("l1d", self.l1_attn_d), ("l1c", self.l1_s5_c),
                         ("l1ffn", lambda: self.out_ffn(1))]:
            if os.environ.get("S5_ONLY") and name != "l1c":
                continue
            fn()
            S.barrier()
            if stop_after == name:
                break

    def din(self, name, shape, dtype=F32):
        return self.nc.dram_tensor(name, list(shape), dtype, kind="ExternalInput").ap()

    def dout(self, name, shape, dtype=F32):
        return self.nc.dram_tensor(name, list(shape), dtype, kind="ExternalOutput").ap()

    def scr(self, name, shape, dtype=BF16):
        if self.debug:
            self.dbg_names.append(name)
            return self.nc.dram_tensor(name, list(shape), dtype, kind="ExternalOutput").ap()
        return self.nc.dram_tensor(name, list(shape), dtype).ap()

    def declare_io(self):
        i = self.i = {}
        i["x"] = self.din("x", [TOK, D])
        i["cond"] = self.din("cond", [2, D])
        i["cache_k_a"] = self.din("cache_k_a", [256, 128])
        i["cache_v_a"] = self.din("cache_v_a", [256, 128])
        i["state_ret"] = self.din("state_ret", [2, 8, 64, 64])
        i["state_ssm_re"] = self.din("state_ssm_re", [2, 32, 64])
        i["state_ssm_im"] = self.din("state_ssm_im", [2, 32, 64])
        i["cache_k_d"] = self.din("cache_k_d", [256, 512])
        i["cache_v_d"] = self.din("cache_v_d", [256, 512])
        for nm, shp in [("ada_w", [2, D, 6 * D]), ("ada_b", [2, 6 * D]), ("norm1_g", [2, D]), ("norm2_g", [2, D]),
                        ("ffn_w1", [2, D, FF]), ("ffn_w3", [2, D, FF]), ("ffn_w2", [2, FF, D]),
                        ("ab_w_in", [1, D, 2816]), ("ab_w_out", [1, D, D]), ("a_q_norm", [1, 64]),
                        ("a_k_norm", [1, 64]), ("a_sink", [1, 8]), ("ret_decay", [1, 2, 8]), ("ret_norm", [1, 64]),
                        ("cd_w_in", [1, D, 2048]), ("cd_w_out", [1, D, D]),
                        ("ssm_lambda_re", [1, 2, 32, 64]), ("ssm_lambda_im", [1, 2, 32, 64]),
                        ("ssm_log_dt", [1, 2, 32]), ("ssm_b_re", [1, 32, 64, 16]), ("ssm_b_im", [1, 32, 64, 16]),
                        ("ssm_c_re", [1, 32, 16, 64]), ("ssm_c_im", [1, 32, 16, 64]), ("ssm_d", [1, 512]),
                        ("ssm_glu_w", [1, 512, 512]), ("ssm_glu_b", [1, 512]), ("d_q_norm", [1, 64]),
                        ("d_k_norm", [1, 64]), ("d_lambda", [1, 4, 64]), ("d_subln", [1, 128])]:
            i[nm] = self.din(nm, shp)
        o = self.o = {}
        o["y"] = self.dout("y", [TOK, D])
        o["nka"] = self.dout("nka", [TP, 128])
        o["nva"] = self.dout("nva", [TP, 128])
        o["nret"] = self.dout("nret", [4, 2, 8, 64, 64])
        o["nsr"] = self.dout("nsr", [4, 2, 32, 64])
        o["nsi"] = self.dout("nsi", [4, 2, 32, 64])
        o["nkd"] = self.dout("nkd", [TP, 512])
        o["nvd"] = self.dout("nvd", [TP, 512])
        s = self.s = {}
        s["mod"] = self.scr("mod", [2, 2, 6 * D], F32)
        s["QAT"] = self.scr("QAT", [64, 8, TOK])
        s["KAT"] = self.scr("KAT", [64, 2, TOK])
        s["VA"] = self.scr("VA", [TOK, 128])
        s["QBT"] = self.scr("QBT", [64, 8, TOK])
        s["KBT"] = self.scr("KBT", [64, 8, TOK])
        s["KB"] = self.scr("KB", [TOK, 512])
        s["VB"] = self.scr("VB", [TOK, 512])
        s["GB"] = self.scr("GB", [TOK, 512])
        s["OAT"] = self.scr("OAT", [64, 8, TOK])
        s["OBT"] = self.scr("OBT", [128, 4, TOK])
        s["UTM"] = self.scr("UTM", [TOK, 512])
        s["QDT"] = self.scr("QDT", [64, 8, TOK])
        s["KDT"] = self.scr("KDT", [64, 8, TOK])
        s["VD"] = self.scr("VD", [TOK, 512])
        s["OCT"] = self.scr("OCT", [128, 4, TOK])
        s["ODT"] = self.scr("ODT", [128, 4, TOK])
        s["X1"] = self.scr("X1", [TOK, D], F32)
        s["X2"] = self.scr("X2", [TOK, D], F32)
        self.sb_ = {k: Buf(k) for k in s}
        self.ob_ = {k: Buf(k) for k in o}

    def tt(self, E, out, in0, in1, op, reads, writes, nosync=False):
        e = E.e
        return E.op(lambda: e.tensor_tensor(out=out, in0=in0, in1=in1, op=op), reads, writes, nosync=nosync)

    def ts(self, E, out, in0, s1, s2, op0, op1, reads, writes, accum_out=None):
        e = E.e
        if accum_out is not None:
            return E.op(lambda: e.tensor_scalar(out=out, in0=in0, scalar1=s1, scalar2=s2, op0=op0, op1=op1,
                                                accum_out=accum_out), reads, writes, selfsync=True)
        if op1 is None:
            return E.op(lambda: e.tensor_scalar(out=out, in0=in0, scalar1=s1, scalar2=None, op0=op0), reads, writes)
        return E.op(lambda: e.tensor_scalar(out=out, in0=in0, scalar1=s1, scalar2=s2, op0=op0, op1=op1),
                    reads, writes)

    def stt(self, out, in0, scalar, in1, op0, op1, reads, writes, accum_out=None):
        nc = self.nc
        if accum_out is not None:
            return self.S.dve.op(lambda: nc.vector.scalar_tensor_tensor(out=out, in0=in0, scalar=scalar, in1=in1,
                                                                        op0=op0, op1=op1, accum_out=accum_out),
                                 reads, writes, selfsync=True)
        return self.S.dve.op(lambda: nc.vector.scalar_tensor_tensor(out=out, in0=in0, scalar=scalar, in1=in1,
                                                                    op0=op0, op1=op1), reads, writes)

    def stt_any(self, E, out, in0, scalar, in1, op0, op1, reads, writes):
        if E is self.S.dve:
            return self.stt(out, in0, scalar, in1, op0, op1, reads, writes)
        self.ts(E, out, in0, scalar, None, op0, None, reads, writes)
        return self.tt(E, out, out, in1, op1, list(reads) + list(writes), writes)

    def cp(self, E, out, in_, reads, writes, nosync=False):
        e = E.e
        if E is self.S.act:
            return E.op(lambda: e.copy(out=out, in_=in_), reads, writes, nosync=nosync)
        return E.op(lambda: e.tensor_copy(out=out, in_=in_), reads, writes, nosync=nosync)

    def actf(self, out, in_, func, reads, writes, scale=1.0, bias=None):
        nc = self.nc
        if bias is None:
            return self.S.act.op(lambda: nc.scalar.activation(out=out, in_=in_, func=func, scale=scale),
                                 reads, writes)
        return self.S.act.op(lambda: nc.scalar.activation(out=out, in_=in_, func=func, scale=scale, bias=bias),
                             reads, writes)

    def mm(self, out, lhsT, rhs, start, stop, reads, writes, inc=None):
        nc = self.nc
        if inc is None:
            inc = stop
        return self.S.pe.op(lambda: nc.tensor.matmul(out, lhsT=lhsT, rhs=rhs, start=start, stop=stop),
                            reads, writes, inc=inc)

    def tr(self, out, in_, ident, reads, writes, inc=True):
        nc = self.nc
        return self.S.pe.op(lambda: nc.tensor.transpose(out, in_, ident), reads, writes, inc=inc)

    def memset(self, E, ap, val, writes):
        e = E.e
        return E.op(lambda: e.memset(ap, val), (), writes)

    def bcast_load(self, dst, src_vec_ap, n):
        src = bass.AP(src_vec_ap.tensor, src_vec_ap.offset, [[0, 128], [1, n]])
        return src

    def setup_consts(self):
        S, nc = self.S, self.nc
        self.ident = S.sb("ident", [128, 128], BF16)
        self.memset(S.pool, self.ident[:], 0.0, [self.ident])
        S.pool.op(lambda: nc.gpsimd.affine_select(out=self.ident[:], in_=self.ident[:], compare_op=ALU.not_equal,
                                                  fill=1.0, base=0, pattern=[[-1, 128]], channel_multiplier=1),
                  [self.ident], [self.ident])
        self.ones_bf = S.sb("ones_bf", [128, 128], BF16)
        self.memset(S.pool, self.ones_bf[:], 1.0, [self.ones_bf])
        self.mhalf = S.sb("mhalf", [128, 8], F32)
        self.memset(S.pool, self.mhalf[:], -0.5, [self.mhalf])
        self._shiftb = S.sb("shiftb", [128, 1], F32)
        self.memset(S.pool, self._shiftb[:], (math.pi / 2.0) * (1.0 - 1e-5), [self._shiftb])
        self.s["ROPE"] = self.scr("ROPE", [2, 128, 32, 32], F32)
        self.sb_["ROPE"] = Buf("ROPE")
        self.build_rope()

    def sin_of(self, out, outT, ang, shift, shape, ph, reads, tmps=None):
        S = self.S
        two_pi = 2.0 * math.pi
        if tmps is not None:
            tot, nf, ni = tmps
        else:
            tot = S.sb("sr_tot", shape, F32, ph)
            nf = S.sb("sr_nf", shape, F32, ph)
            ni = S.sb("sr_ni", shape, I32, ph)
        SC = 1.0 - 1e-5
        big = int(np.prod(shape[1:])) >= 256
        self.ts(S.dve, nf[:], ang, shift, 1.0 / two_pi, ALU.add, ALU.mult, reads, [nf])
        self.cp(S.dve, ni[:], nf[:], [nf], [ni], nosync=big)
        self.cp(S.dve, nf[:], ni[:], [ni], [nf], nosync=big)
        self.stt(tot[:], nf[:], -two_pi, ang, ALU.mult, ALU.add, [nf] + list(reads), [tot])
        if shift == 0.0:
            self.actf(out, tot[:], AF.Sin, [tot], [outT], scale=SC)
        else:
            self.actf(out, tot[:], AF.Sin, [tot, self._shiftb], [outT], scale=SC, bias=self._shiftb[:])

    def build_rope(self):
        S, nc = self.S, self.nc
        with ExitStack() as ph:
            rowi = S.sb("rowi", [128, 32], I32, ph)
            coli = S.sb("coli", [128, 1], I32, ph)
            for half in range(2):
                sl = slice(64 * half, 64 * half + 64)
                S.pool.op(lambda: nc.gpsimd.iota(rowi[sl, :], pattern=[[2, 32]], base=half, channel_multiplier=0),
                          (), [rowi])
                S.pool.op(lambda: nc.gpsimd.iota(coli[sl, :], pattern=[[1, 1]], base=0, channel_multiplier=1),
                          (), [coli])
            rowf = S.sb("rowf", [128, 32], F32, ph)
            colf = S.sb("colf", [128, 1], F32, ph)
            self.cp(S.dve, rowf[:], rowi[:], [rowi], [rowf])
            self.cp(S.dve, colf[:], coli[:], [coli], [colf])
            ang = S.sb("ang", [128, 32, 32], F32, ph)
            for f in range(16):
                inv = 10000.0 ** (-f / 16.0)
                self.ts(S.dve, ang[:, :, f], rowf[:], inv, None, ALU.mult, None, [rowf], [ang])
                self.ts(S.dve, ang[:, :, 16 + f], AP(colf[:], [[0, 32]]), inv, None, ALU.mult, None, [colf], [ang])
            tab = S.sb("ropetab", [128, 32, 32], F32, ph)
            for i, shift in enumerate((math.pi / 2.0, 0.0)):
                self.sin_of(tab[:], tab, ang[:], shift, [128, 32, 32], ph, [ang])
                S.sp.dma(self.s["ROPE"][i], tab[:], reads=[tab], writes=[self.sb_["ROPE"]])
            S.barrier()

    def load_rope(self, dst_cos, dst_sin, tile):
        S = self.S
        S.sp.dma(dst_cos[:], self.s["ROPE"][0, :, tile, :], reads=[self.sb_["ROPE"]], writes=[dst_cos])
        S.sp.dma(dst_sin[:], self.s["ROPE"][1, :, tile, :], reads=[self.sb_["ROPE"]], writes=[dst_sin])

    def load_bcast(self, dst, vec_ap, n, E=None):
        E = E or self.S.sp
        src = bass.AP(vec_ap.tensor, vec_ap.offset, [[0, dst.t.shape[0]], [1, n]])
        return src

    def bc(self, vec_ap, parts, n):
        return bass.AP(vec_ap.tensor, vec_ap.offset, [[0, parts], [1, n]])

    def load_w_steps(self, dst, src_rows_ap, ncols, kchunks, stage, rows=128, engs=None):
        S = self.S
        engs = engs or [S.act, S.dve]
        steps = []
        for k in range(kchunks):
            def step(k=k):
                st = stage[k % len(stage)]
                S.sp.dma(st[0:rows, 0:ncols], src_rows_ap[k * rows:(k + 1) * rows, :], writes=[st])
                self.cp(engs[k % len(engs)], dst[0:rows, k, 0:ncols], st[0:rows, 0:ncols], [st], [dst])
            steps.append(step)
        return steps

    def load_w_bf16(self, dst, src_rows_ap, ncols, kchunks, stage, rows=128, engs=None):
        for st in self.load_w_steps(dst, src_rows_ap, ncols, kchunks, stage, rows, engs):
            st()

    def phase_ada(self):
        S, nc, i, s = self.S, self.nc, self.i, self.s
        with ExitStack() as ph:
            ct = S.sb("ct", [128, 2, 8], F32, ph)
            for c in range(2):
                for k in range(8):
                    src = bass.AP(i["cond"].tensor, c * D + k * 128, [[1, 128], [1, 1]])
                    S.sp.dma(ct[:, c, k:k + 1], src, writes=[ct])
            cs = S.sb("cs", [128, 8, 2], BF16, ph)
            self.actf(cs[:].rearrange("p k c -> p c k"), ct[:], AF.Silu, [ct], [cs])
            wts = [S.sb("adaw", [128, 8, 1536], BF16, ph) for _ in range(2)]
            wstage = [S.sb("adast", [128, 1536], F32, ph) for _ in range(3)]
            pss = [S.ps("adaps", [2, 512], F32, ph) for _ in range(2)]
            bias = S.sb("adab", [2, 6 * D], F32, ph)
            msb = S.sb("adam", [2, 6 * D], F32, ph)
            q = 0
            for l in range(2):
                S.sp.dma(bias[:], self.bc(i["ada_b"][l], 2, 6 * D), writes=[bias])
                for part in range(4):
                    wt = wts[q % 2]
                    q += 1
                    self.load_w_bf16(wt, i["ada_w"][l][:, part * 1536:(part + 1) * 1536], 1536, 8, wstage,
                                     engs=[S.act, S.dve])
                    for cb in range(3):
                        pst = pss[cb % 2]
                        for k in range(8):
                            self.mm(pst[:, :], cs[:, k, :], wt[:, k, cb * 512:(cb + 1) * 512], k == 0, k == 7,
                                    [cs, wt], [pst])
                        c0 = part * 1536 + cb * 512
                        self.tt(S.dve, msb[:, c0:c0 + 512], pst[:, :], bias[:, c0:c0 + 512], ALU.add,
                                [pst, bias], [msb])
                S.sp.dma(s["mod"][l], msb[:], reads=[msb], writes=[self.sb_["mod"]])

    def load_mod(self, dst, l, cond, which):
        self.S.sp.dma(dst[:], self.bc(self.s["mod"][l, cond, which * D:(which + 1) * D], 128, D),
                      reads=[self.sb_["mod"]], writes=[dst])

    def make_norm_scale(self, G, SH, l, cond, norm_g_ap, which_sc, which_sh, tmp):
        S = self.S
        self.load_mod(tmp, l, cond, which_sc)
        S.sp.dma(G[:], self.bc(norm_g_ap, 128, D), writes=[G])
        self.stt(G[:], tmp[:], 1.0, G[:], ALU.add, ALU.mult, [tmp, G], [G])
        self.load_mod(SH, l, cond, which_sh)

    def rms_rstd(self, rstd, ss, n, width):
        S, nc = self.S, self.nc
        self.ts(S.dve, ss[:, 0:width], ss[:, 0:width], 1.0 / n, EPS, ALU.mult, ALU.add, [ss], [ss])
        S.pool.op(lambda: nc.gpsimd.tensor_tensor(out=rstd[:, 0:width], in0=ss[:, 0:width],
                                                  in1=self.mhalf[:, 0:width], op=ALU.pow),
                  [ss, self.mhalf], [rstd])

    def norm_tile(self, xt, G, SH, hf, hb, ss, rstd):
        S = self.S
        self.stt(hf[:], xt[:], 1.0, xt[:], ALU.mult, ALU.mult, [xt], [hf, ss], accum_out=ss[:, 0:1])
        self.rms_rstd(rstd, ss, float(D), 1)
        self.stt(hf[:], xt[:], rstd[:, 0:1], G[:], ALU.mult, ALU.mult, [xt, rstd, G], [hf])
        self.tt(S.dve, hb[:], hf[:], SH[:], ALU.add, [hf, SH], [hb])

    def norm_tile_g(self, xt, G, SH, hf, hb, ss, rstd):
        S, nc = self.S, self.nc
        self.stt(hf[:], xt[:], 1.0, xt[:], ALU.mult, ALU.mult, [xt], [hf, ss], accum_out=ss[:, 0:1])
        self.ts(S.dve, ss[:, 0:1], ss[:, 0:1], 1.0 / float(D), EPS, ALU.mult, ALU.add, [ss], [ss])
        yield
        S.pool.op(lambda: nc.gpsimd.tensor_tensor(out=rstd[:, 0:1], in0=ss[:, 0:1], in1=self.mhalf[:, 0:1],
                                                  op=ALU.pow), [ss, self.mhalf], [rstd])
        yield
        self.stt(hf[:], xt[:], rstd[:, 0:1], G[:], ALU.mult, ALU.mult, [xt, rstd, G], [hf])
        self.tt(S.dve, hb[:], hf[:], SH[:], ALU.add, [hf, SH], [hb], nosync=True)

    def head_norm_g(self, pb, H, gvec, out_f, sqt, ss, rstd):
        S, nc = self.S, self.nc
        W = H * 64
        self.actf(sqt[:, 0:W], pb[:, 0:W], AF.Square, [pb], [sqt])
        yield
        S.dve.op(lambda: nc.vector.tensor_reduce(out=ss[:, 0:H], in_=sqt[:, 0:W].rearrange("p (h d) -> p h d", h=H),
                                                 axis=AX.X, op=ALU.add), [sqt], [ss])
        self.ts(S.dve, ss[:, 0:H], ss[:, 0:H], 1.0 / 64.0, EPS, ALU.mult, ALU.add, [ss], [ss])
        yield
        S.pool.op(lambda: nc.gpsimd.tensor_tensor(out=rstd[:, 0:H], in0=ss[:, 0:H], in1=self.mhalf[:, 0:H],
                                                  op=ALU.pow), [ss, self.mhalf], [rstd])
        yield
        o3 = out_f[:, 0:W].rearrange("p (h d) -> p h d", h=H)
        self.tt(S.dve, o3, pb[:, 0:W].rearrange("p (h d) -> p h d", h=H), AP(rstd[:, 0:H], [[1, H], [0, 64]]),
                ALU.mult, [pb, rstd], [out_f])
        self.tt(S.dve, o3, o3, AP(gvec[:], [[0, H], [1, 64]]), ALU.mult, [out_f, gvec], [out_f], nosync=(H >= 8))

    def transpose_to(self, pT, hT_dst_fn, hb, nchunks, evac):
        for c in range(nchunks):
            self.tr(pT[:, c * 128:(c + 1) * 128], hb[:, c * 128:(c + 1) * 128], self.ident[:], [hb, self.ident],
                    [pT], inc=(c == nchunks - 1))
        evac()

    def head_norm(self, pb, H, gvec, out_f, sqt, ss, rstd):
        S, nc = self.S, self.nc
        W = H * 64
        self.actf(sqt[:, 0:W], pb[:, 0:W], AF.Square, [pb], [sqt])
        S.dve.op(lambda: nc.vector.tensor_reduce(out=ss[:, 0:H], in_=sqt[:, 0:W].rearrange("p (h d) -> p h d", h=H),
                                                 axis=AX.X, op=ALU.add), [sqt], [ss])
        self.rms_rstd(rstd, ss, 64.0, H)
        o3 = out_f[:, 0:W].rearrange("p (h d) -> p h d", h=H)
        self.tt(S.dve, o3, pb[:, 0:W].rearrange("p (h d) -> p h d", h=H), AP(rstd[:, 0:H], [[1, H], [0, 64]]),
                ALU.mult, [pb, rstd], [out_f])
        self.tt(S.dve, o3, o3, AP(gvec[:], [[0, H], [1, 64]]), ALU.mult, [out_f, gvec], [out_f])

    def rope_apply(self, out_b, x_f, H, cos, sin, t1, t2):
        S = self.S
        x3 = x_f[:, 0:H * 64].rearrange("p (h d) -> p h d", h=H)
        o3 = out_b[:, 0:H * 64].rearrange("p (h d) -> p h d", h=H)
        a3 = t1[:, 0:H * 32].rearrange("p (h d) -> p h d", h=H)
        b3 = t2[:, 0:H * 32].rearrange("p (h d) -> p h d", h=H)
        cb = AP(cos[:], [[0, H], [1, 32]])
        sb = AP(sin[:], [[0, H], [1, 32]])
        x1, x2 = x3[:, :, 0:32], x3[:, :, 32:64]
        self.tt(S.dve, a3, x1, cb, ALU.mult, [x_f, cos], [t1], nosync=(H >= 8))
        self.tt(S.dve, b3, x2, sb, ALU.mult, [x_f, sin], [t2], nosync=(H >= 8))
        self.tt(S.dve, o3[:, :, 0:32], a3, b3, ALU.subtract, [t1, t2], [out_b], nosync=(H >= 8))
        self.tt(S.dve, a3, x2, cb, ALU.mult, [x_f, cos], [t1], nosync=(H >= 8))
        self.tt(S.dve, b3, x1, sb, ALU.mult, [x_f, sin], [t2], nosync=(H >= 8))
        self.tt(S.dve, o3[:, :, 32:64], a3, b3, ALU.add, [t1, t2], [out_b], nosync=(H >= 8))

    def l0_in(self):
        S, nc, i, s, o = self.S, self.nc, self.i, self.s, self.o
        with ExitStack() as ph:
            W = S.sb("win", [128, 8, 2816], BF16, ph)
            wstage = [S.sb("winst", [128, 2816], F32, ph) for _ in range(2)]
            self.load_w_bf16(W, i["ab_w_in"][0], 2816, 8, wstage)
            G = S.sb("G", [128, D], F32, ph)
            SH = S.sb("SH", [128, D], F32, ph)
            qg = S.sb("qg", [128, 64], F32, ph)
            kg = S.sb("kg", [128, 64], F32, ph)
            S.sp.dma(qg[:], self.bc(i["a_q_norm"][0], 128, 64), writes=[qg])
            S.sp.dma(kg[:], self.bc(i["a_k_norm"][0], 128, 64), writes=[kg])
            xts = [S.sb("xt", [128, D], F32, ph) for _ in range(2)]
            hf = [S.sb("hf", [128, D], F32, ph) for _ in range(2)]
            hbs = [S.sb("hb", [128, D], BF16, ph) for _ in range(2)]
            hTs = [S.sb("hT", [128, 8, 128], BF16, ph) for _ in range(2)]
            ss = [S.sb("ss", [128, 8], F32, ph) for _ in range(2)]
            rstd = [S.sb("rstd", [128, 8], F32, ph) for _ in range(2)]
            ssh = [S.sb("ssh", [128, 8], F32, ph) for _ in range(2)]
            rsh = [S.sb("rsh", [128, 8], F32, ph) for _ in range(2)]
            sqt = [S.sb("sqt", [128, 512], F32, ph) for _ in range(2)]
            qf = [S.sb("qf", [128, 512], F32, ph) for _ in range(2)]
            kf = [S.sb("kf", [128, 128], F32, ph) for _ in range(2)]
            vf = [S.sb("vf", [128, 128], F32, ph) for _ in range(2)]
            t1 = [S.sb("t1", [128, 256], F32, ph) for _ in range(2)]
            t2 = [S.sb("t2", [128, 256], F32, ph) for _ in range(2)]
            cos = [S.sb("cos", [128, 32], F32, ph) for _ in range(2)]
            sin = [S.sb("sin", [128, 32], F32, ph) for _ in range(2)]
            qab = [S.sb("qab", [128, 512], BF16, ph) for _ in range(2)]
            kab = [S.sb("kab", [128, 128], BF16, ph) for _ in range(2)]
            vab = [S.sb("vab", [128, 128], BF16, ph) for _ in range(2)]
            qbb = [S.sb("qbb", [128, 512], BF16, ph) for _ in range(2)]
            kbb = [S.sb("kbb", [128, 512], BF16, ph) for _ in range(2)]
            vbb = [S.sb("vbb", [128, 512], BF16, ph) for _ in range(2)]
            gbb = [S.sb("gbb", [128, 512], BF16, ph) for _ in range(2)]
            stQA = [S.sb("stQA", [64, 8, 512], BF16, ph) for _ in range(2)]
            stKA = [S.sb("stKA", [64, 2, 512], BF16, ph) for _ in range(2)]
            stQB = [S.sb("stQB", [64, 8, 512], BF16, ph) for _ in range(2)]
            stKB = [S.sb("stKB", [64, 8, 512], BF16, ph) for _ in range(2)]
            pT = [S.ps("pT", [128, 1024], BF16, ph) for _ in range(2)]
            pbs = [S.ps("pb", [128, 512], F32, ph) for _ in range(4)]
            pOs = [S.ps("pO", [128, 1024], BF16, ph) for _ in range(2)]
            tmpm = S.sb("tmpm", [128, D], F32, ph)
            cnt = {"pb": 0, "po": 0, "cond": -1}

            def tile_gen(t):
                p = t % 2
                cond = 0 if t < NTS else 1
                sub, sup, r0 = t % 4, t // 4, t * 128
                xt, hb, hT = xts[p], hbs[p], hTs[p]
                if cond != cnt["cond"]:
                    self.make_norm_scale(G, SH, 0, cond, i["norm1_g"][0], 1, 0, tmpm)
                    cnt["cond"] = cond
                S.sp.dma(xt[:], i["x"][r0:r0 + 128, :], writes=[xt])
                if cond == 0:
                    self.load_rope(cos[p], sin[p], t)
                yield from self.norm_tile_g(xt, G, SH, hf[p], hb, ss[p], rstd[p])
                yield
                self.transpose_to(pT[p], None, hb, 8,
                                  lambda: self.cp(S.act, hT[:].rearrange("p k t -> p (k t)"), pT[p][:, :], [pT[p]], [hT]))
                yield

                def proj(c0, ncols):
                    pb = pbs[cnt["pb"] % 4]
                    cnt["pb"] += 1
                    for k in range(8):
                        self.mm(pb[:, 0:ncols], hT[:, k, :], W[:, k, c0:c0 + ncols], k == 0, k == 7, [hT, W], [pb])
                    return pb

                def to_featmajor(src_b, H, stage):
                    pO = pOs[cnt["po"] % 2]
                    cnt["po"] += 1
                    for h in range(H):
                        self.tr(pO[0:64, h * 128:(h + 1) * 128], src_b[:, h * 64:(h + 1) * 64], self.ident[:],
                                [src_b, self.ident], [pO], inc=(h == H - 1))
                    yield
                    self.cp(S.act, stage[:, :, sub * 128:(sub + 1) * 128],
                            pO[0:64, 0:H * 128].rearrange("p (h t) -> p h t", h=H), [pO], [stage])

                pb = proj(0, 512)
                yield
                yield from self.head_norm_g(pb, 8, qg, qf[p], sqt[p], ssh[p], rsh[p])
                yield
                qa = qab[p]
                if cond == 0:
                    self.rope_apply(qa, qf[p], 8, cos[p], sin[p], t1[p], t2[p])
                else:
                    self.cp(S.dve, qa[:], qf[p][:], [qf[p]], [qa])
                yield
                yield from to_featmajor(qa, 8, stQA[sup % 2])
                pb = proj(512, 256)
                yield
                yield from self.head_norm_g(pb, 2, kg, kf[p], sqt[p], ssh[p], rsh[p])
                yield
                ka = kab[p]
                va = vab[p]
                if cond == 0:
                    self.rope_apply(ka, kf[p], 2, cos[p], sin[p], t1[p], t2[p])
                    self.cp(S.act, va[:], pb[:, 128:256], [pb], [va])
                else:
                    self.cp(S.dve, ka[:], kf[p][:], [kf[p]], [ka])
                    self.cp(S.act, vf[p][:], pb[:, 128:256], [pb], [vf[p]])
                    self.cp(S.dve, va[:], vf[p][:], [vf[p]], [va])
                    pr = r0 - TS
                    S.pool.dma(o["nka"][pr:pr + 128, :], kf[p][:], reads=[kf[p]], writes=[self.ob_["nka"]])
                    S.pool.dma(o["nva"][pr:pr + 128, :], vf[p][:], reads=[vf[p]], writes=[self.ob_["nva"]])
                yield
                yield from to_featmajor(ka, 2, stKA[sup % 2])
                S.pool.dma(s["VA"][r0:r0 + 128, :], va[:], reads=[va], writes=[self.sb_["VA"]])
                pb = proj(768, 512)
                yield
                qb = qbb[p]
                self.actf(qb[:], pb[:, :], AF.Copy, [pb], [qb], scale=0.125)
                yield from to_featmajor(qb, 8, stQB[sup % 2])
                pb = proj(1280, 512)
                yield
                kb = kbb[p]
                self.cp(S.act, kb[:], pb[:, :], [pb], [kb])
                yield from to_featmajor(kb, 8, stKB[sup % 2])
                S.pool.dma(s["KB"][r0:r0 + 128, :], kb[:], reads=[kb], writes=[self.sb_["KB"]])
                pb = proj(1792, 512)
                yield
                vb = vbb[p]
                self.cp(S.dve, vb[:], pb[:, :], [pb], [vb])
                S.pool.dma(s["VB"][r0:r0 + 128, :], vb[:], reads=[vb], writes=[self.sb_["VB"]])
                pb = proj(2304, 512)
                yield
                gb = gbb[p]
                self.actf(gb[:], pb[:, :], AF.Silu, [pb], [gb])
                S.pool.dma(s["GB"][r0:r0 + 128, :], gb[:], reads=[gb], writes=[self.sb_["GB"]])
                if sub == 3:
                    c0 = sup * 512
                    for nm, st in (("QAT", stQA), ("KAT", stKA), ("QBT", stQB), ("KBT", stKB)):
                        S.act.dma(s[nm][:, :, c0:c0 + 512], st[sup % 2][:], reads=[st[sup % 2]],
                                  writes=[self.sb_[nm]])

            for t0 in range(0, NT, 2):
                alive = [tile_gen(t0), tile_gen(t0 + 1)]
                while alive:
                    for g_ in list(alive):
                        try:
                            next(g_)
                        except StopIteration:
                            alive.remove(g_)

    def l0_attn_a(self):
        S, nc, i, s, o = self.S, self.nc, self.i, self.s, self.o
        NEG = -30000.0
        with ExitStack() as ph:
            esk = S.sb("esk", [64, 8], F32, ph)
            S.sp.dma(esk[:], self.bc(i["a_sink"][0], 64, 8), writes=[esk])
            self.actf(esk[:], esk[:], AF.Exp, [esk], [esk])
            mprev = S.sb("mprev", [128, 4, 128], BF16, ph)
            mnext = S.sb("mnext", [128, 4, 128], BF16, ph)
            for m, pat, cm in ((mprev, [[0, 4], [-1, 128]], 1), (mnext, [[0, 4], [1, 128]], -1)):
                self.memset(S.pool, m[:], 0.0, [m])
                S.pool.op(lambda: nc.gpsimd.affine_select(out=m[:], in_=m[:], compare_op=ALU.is_ge, fill=NEG,
                                                          base=0, pattern=pat, channel_multiplier=cm), [m], [m])
            kc = S.sb("kc", [128, 2, 128], BF16, ph)
            vc = S.sb("vc", [128, 2, 128], BF16, ph)
            S.pool.dma(kc[:], i["cache_k_a"].rearrange("(n p) d -> p n d", p=128), writes=[kc])
            S.pool.dma(vc[:], i["cache_v_a"].rearrange("(n p) d -> p n d", p=128), writes=[vc])
            pO = S.ps("pOc", [128, 1024], BF16, ph)
            kcT = S.sb("kcT", [64, 2, 256], BF16, ph)
            for g in range(2):
                for n in range(2):
                    self.tr(pO[0:64, (g * 2 + n) * 128:(g * 2 + n + 1) * 128], kc[:, n, g * 64:(g + 1) * 64],
                            self.ident[:], [kc, self.ident], [pO], inc=(g == 1 and n == 1))
            self.cp(S.act, kcT[:].rearrange("p g t -> p (g t)"), pO[0:64, 0:512], [pO], [kcT])
            pS = [S.ps("pS", [128, 512], F32, ph) for _ in range(3)]
            pOa = [S.ps("pOa", [64, 512], F32, ph) for _ in range(2)]
            pZ = [S.ps("pZ", [64, 512], F32, ph) for _ in range(2)]
            PTs = [S.sb("PT", [128, 512], BF16, ph) for _ in range(3)]
            den = S.sb("den", [64, 512], F32, ph)
            rec = S.sb("rec", [64, 512], F32, ph)
            KT = S.sb("KT", [64, TS], BF16, ph)
            QT = S.sb("QT", [64, 4, TS], BF16, ph)
            Vt = S.sb("Vt", [128, 32, 64], BF16, ph)
            stg = [S.sb("stgA", [64, 4, 512], BF16, ph) for _ in range(2)]
            cnt = {"s": 0, "o": 0}

            def block(g, tiles, q_rhs, out_ap, out_T):
                ob = cnt["o"] % 2
                cnt["o"] += 1
                nt = len(tiles)
                rb = []
                for ti in range(nt + 1):
                    if ti < nt:
                        kl, vl, msk, rd = tiles[ti]
                        r = cnt["s"] % 3
                        cnt["s"] += 1
                        rb.append(r)
                        self.mm(pS[r][:, :], kl, q_rhs, True, msk is None, rd, [pS[r]])
                        if msk is not None:
                            self.mm(pS[r][:, :], self.ident[:], msk[:].rearrange("p r t -> p (r t)"), False, True,
                                    [self.ident, msk], [pS[r]])
                        self.actf(PTs[r][:], pS[r][:, :], AF.Exp, [pS[r]], [PTs[r]], scale=0.125)
                    if ti > 0:
                        tp_ = ti - 1
                        kl, vl, msk, rd = tiles[tp_]
                        r = rb[tp_]
                        last = tp_ == nt - 1
                        self.mm(pOa[ob][:, :], vl, PTs[r][:], tp_ == 0, last, rd + [PTs[r]], [pOa[ob]])
                        self.mm(pZ[ob][:, :], self.ones_bf[:, 0:64], PTs[r][:], tp_ == 0, last,
                                [self.ones_bf, PTs[r]], [pZ[ob]])
                self.tt(S.dve, den[:].rearrange("p (r t) -> p r t", r=4),
                        pZ[ob][:, :].rearrange("p (r t) -> p r t", r=4),
                        AP(esk[:, 4 * g:4 * g + 4], [[1, 4], [0, 128]]), ALU.add, [pZ[ob], esk], [den])
                S.dve.op(lambda: nc.vector.reciprocal(out=rec[:], in_=den[:]), [den], [rec])
                self.tt(S.dve, out_ap, pOa[ob][:, :].rearrange("p (r t) -> p r t", r=4),
                        rec[:].rearrange("p (r t) -> p r t", r=4), ALU.mult, [pOa[ob], rec], [out_T])

            for g in range(2):
                S.sp.dma(KT[:], s["KAT"][:, g, 0:TS], reads=[self.sb_["KAT"]], writes=[KT])
                S.sp.dma(QT[:], s["QAT"][:, 4 * g:4 * g + 4, 0:TS], reads=[self.sb_["QAT"]], writes=[QT])
                for part in range(4):
                    S.sp.dma(Vt[:, part * 8:(part + 1) * 8, :],
                             s["VA"][part * 1024:(part + 1) * 1024, g * 64:(g + 1) * 64].rearrange("(n p) d -> p n d", p=128),
                             reads=[self.sb_["VA"]], writes=[Vt])
                for bi in range(NTS):
                    tiles = [(kcT[:, g, n * 128:(n + 1) * 128], vc[:, n, g * 64:(g + 1) * 64], None, [kcT, vc, QT])
                             for n in range(2)]
                    for kt, msk in ((bi - 1, mprev), (bi, None), (bi + 1, mnext)):
                        if 0 <= kt < NTS:
                            tiles.append((KT[:, kt * 128:(kt + 1) * 128], Vt[:, kt, :], msk, [KT, Vt, QT]))
                    st = stg[(bi // 4) % 2]
                    block(g, tiles, QT[:, :, bi * 128:(bi + 1) * 128],
                          st[:, :, (bi % 4) * 128:(bi % 4 + 1) * 128], st)
                    if bi % 4 == 3:
                        c0 = (bi // 4) * 512
                        S.pool.dma(s["OAT"][:, 4 * g:4 * g + 4, c0:c0 + 512], st[:], reads=[st],
                                 writes=[self.sb_["OAT"]])
            KTp = S.sb("KTp", [64, 256], BF16, ph)
            QTp = S.sb("QTp", [64, 4, 256], BF16, ph)
            Vtp = S.sb("Vtp", [128, 2, 64], BF16, ph)
            q = 0
            for sq in range(4):
                base = TS + 256 * sq
                for g in range(2):
                    S.sp.dma(KTp[:], s["KAT"][:, g, base:base + 256], reads=[self.sb_["KAT"]], writes=[KTp])
                    S.sp.dma(QTp[:], s["QAT"][:, 4 * g:4 * g + 4, base:base + 256], reads=[self.sb_["QAT"]],
                             writes=[QTp])
                    S.sp.dma(Vtp[:], s["VA"][base:base + 256, g * 64:(g + 1) * 64].rearrange("(n p) d -> p n d", p=128),
                             reads=[self.sb_["VA"]], writes=[Vtp])
                    st = stg[q % 2]
                    q += 1
                    for qb in range(2):
                        tiles = [(KTp[:, n * 128:(n + 1) * 128], Vtp[:, n, :], None, [KTp, Vtp, QTp])
                                 for n in range(2)]
                        block(g, tiles, QTp[:, :, qb * 128:(qb + 1) * 128],
                              st[:, :, qb * 128:(qb + 1) * 128], st)
                    S.pool.dma(s["OAT"][:, 4 * g:4 * g + 4, base:base + 256], st[:, :, 0:256], reads=[st],
                             writes=[self.sb_["OAT"]])

    def l0_ret_b(self):
        S, nc, i, s, o = self.S, self.nc, self.i, self.s, self.o
        with ExitStack() as ph:
            lg = S.sb("lg", [128, 16], F32, ph)
            S.sp.dma(lg[:], self.bc(i["ret_decay"][0], 128, 16), writes=[lg])
            self.actf(lg[:], lg[:], AF.Exp, [lg], [lg], scale=-1.0)
            self.actf(lg[:], lg[:], AF.Ln, [lg], [lg], scale=1.0, bias=1.0)
            self.ts(S.dve, lg[:], lg[:], -1.0, None, ALU.mult, None, [lg], [lg])
            di = S.sb("di", [128, 128], I32, ph)
            S.pool.op(lambda: nc.gpsimd.iota(di[:], pattern=[[1, 128]], base=0, channel_multiplier=-1), (), [di])
            diff = S.sb("diff", [128, 128], F32, ph)
            self.cp(S.dve, diff[:], di[:], [di], [diff])
            posd = S.sb("posd", [128, 128], F32, ph)
            negd = S.sb("negd", [128, 128], F32, ph)
            mge = S.sb("mge", [128, 128], F32, ph)
            mle = S.sb("mle", [128, 128], F32, ph)
            self.ts(S.dve, posd[:], diff[:], 0.0, None, ALU.max, None, [diff], [posd])
            self.ts(S.dve, negd[:], diff[:], -1.0, 0.0, ALU.mult, ALU.max, [diff], [negd])
            self.ts(S.dve, mge[:], diff[:], 0.0, None, ALU.is_ge, None, [diff], [mge])
            self.ts(S.dve, mle[:], diff[:], 0.0, None, ALU.is_le, None, [diff], [mle])
            Dm = S.sb("Dm", [128, 8, 128], F32, ph)
            ta = S.sb("ta", [128, 128], F32, ph)
            tb = S.sb("tb", [128, 128], F32, ph)
            for h in range(8):
                self.actf(ta[:], posd[:], AF.Exp, [posd, lg], [ta], scale=lg[:, h:h + 1])
                self.actf(tb[:], negd[:], AF.Exp, [negd, lg], [tb], scale=lg[:, 8 + h:9 + h])
                self.tt(S.dve, ta[:], ta[:], mge[:], ALU.mult, [ta, mge], [ta])
                self.tt(S.dve, tb[:], tb[:], mle[:], ALU.mult, [tb, mle], [tb])
                self.tt(S.dve, Dm[:, h, :], ta[:], tb[:], ALU.add, [ta, tb], [Dm])
            jc = S.sb("jc", [128, 2], F32, ph)
            self.ts(S.dve, jc[:, 0:1], diff[:, 0:1], -1.0, None, ALU.mult, None, [diff], [jc])
            self.ts(S.dve, jc[:, 1:2], diff[:, 0:1], 1.0, 127.0, ALU.mult, ALU.add, [diff], [jc])
            kdf = S.sb("kdf", [128, 8], F32, ph)
            kdb = S.sb("kdb", [128, 8], F32, ph)
            self.actf(kdf[:], lg[:, 0:8], AF.Exp, [lg, jc], [kdf], scale=jc[:, 1:2])
            self.actf(kdb[:], lg[:, 8:16], AF.Exp, [lg, jc], [kdb], scale=jc[:, 0:1])
            ip1 = S.sb("ip1", [64, 128], F32, ph)
            imi = S.sb("imi", [64, 128], F32, ph)
            self.ts(S.dve, ip1[:], diff[0:64, :], jc[0:64, 0:1], 1.0, ALU.add, ALU.add, [diff, jc], [ip1])
            self.ts(S.dve, imi[:], ip1[:], -1.0, 129.0, ALU.mult, ALU.add, [ip1], [imi])
            qdf = S.sb("qdf", [64, 8, 128], F32, ph)
            qdb = S.sb("qdb", [64, 8, 128], F32, ph)
            for h in range(8):
                self.actf(qdf[:, h, :], ip1[:], AF.Exp, [ip1, lg], [qdf], scale=lg[0:64, h:h + 1])
                self.actf(qdb[:, h, :], imi[:], AF.Exp, [imi, lg], [qdb], scale=lg[0:64, 8 + h:9 + h])
            Gf = S.sb("Gf", [64, 8, 64], F32, ph)
            Gb = S.sb("Gb", [64, 8, 64], F32, ph)
            self.actf(Gf[:], AP(lg[0:64, 0:8], [[1, 8], [0, 64]]), AF.Exp, [lg], [Gf], scale=128.0)
            self.actf(Gb[:], AP(lg[0:64, 8:16], [[1, 8], [0, 64]]), AF.Exp, [lg], [Gb], scale=128.0)
            retg = S.sb("retg", [128, 64], F32, ph)
            S.sp.dma(retg[:], self.bc(i["ret_norm"][0], 128, 64), writes=[retg])

            SAll = [S.sb("SAll", [64, NT, 512], BF16, ph) for _ in range(2)]
            St = S.sb("St", [64, 512], F32, ph)
            KBt = [S.sb("KBt", [128, 512], BF16, ph) for _ in range(2)]
            VBt = [S.sb("VBt", [128, 512], BF16, ph) for _ in range(2)]
            Kd = [S.sb("Kd", [128, 512], BF16, ph) for _ in range(2)]
            pKV = [S.ps("pKV", [64, 512], F32, ph) for _ in range(2)]

            seqs = [(0, NTS, None)] + [(NTS + 2 * q, 2, q) for q in range(4)]
            it = 0
            for dr in range(2):
                kdec = kdf if dr == 0 else kdb
                Gd = Gf if dr == 0 else Gb
                for (c0, ncs, pq) in seqs:
                    if pq is None:
                        S.sp.dma(St[:].rearrange("d (h v) -> d h v", h=8),
                                 i["state_ret"][dr].rearrange("h d v -> d h v"), writes=[St])
                    else:
                        self.memset(S.pool, St[:], 0.0, [St])
                    order = range(c0, c0 + ncs) if dr == 0 else range(c0 + ncs - 1, c0 - 1, -1)
                    for n in order:
                        kb, vb, kd, pk = KBt[it % 2], VBt[it % 2], Kd[it % 2], pKV[it % 2]
                        it += 1
                        r0 = n * 128
                        S.sp.dma(kb[:], s["KB"][r0:r0 + 128, :], reads=[self.sb_["KB"]], writes=[kb])
                        S.sp.dma(vb[:], s["VB"][r0:r0 + 128, :], reads=[self.sb_["VB"]], writes=[vb])
                        self.cp(S.act, SAll[dr][:, n, :], St[:], [St], [SAll[dr]])
                        self.tt(S.dve, kd[:].rearrange("p (h d) -> p h d", h=8),
                                kb[:].rearrange("p (h d) -> p h d", h=8), AP(kdec[:], [[1, 8], [0, 64]]), ALU.mult,
                                [kb, kdec], [kd])
                        for h in range(8):
                            self.mm(pk[:, h * 64:(h + 1) * 64], kd[:, h * 64:(h + 1) * 64], vb[:, h * 64:(h + 1) * 64],
                                    True, True, [kd, vb], [pk], inc=(h == 7))
                        self.tt(S.dve, St[:], St[:], Gd[:].rearrange("d h v -> d (h v)"), ALU.mult, [St, Gd], [St])
                        self.tt(S.dve, St[:], pk[:, :], St[:], ALU.add, [pk, St], [St])
                    if pq is not None:
                        S.pool.dma(o["nret"][pq, dr].rearrange("h d v -> d h v"),
                                   St[:].rearrange("d (h v) -> d h v", h=8), reads=[St], writes=[self.ob_["nret"]])
            QTt = [S.sb("QTt", [64, 8, 128], BF16, ph) for _ in range(2)]
            KTt = [S.sb("KTt", [64, 8, 128], BF16, ph) for _ in range(2)]
            GBt = [S.sb("GBt", [128, 512], BF16, ph) for _ in range(2)]
            SD = [S.sb("SD", [128, 8, 128], BF16, ph) for _ in range(2)]
            Qdf = [S.sb("Qdf", [64, 8, 128], BF16, ph) for _ in range(2)]
            Qdb = [S.sb("Qdb", [64, 8, 128], BF16, ph) for _ in range(2)]
            pST = [S.ps("pST", [128, 1024], F32, ph) for _ in range(2)]
            pOb = S.ps("pOb", [128, 512], F32, ph)
            pTo = S.ps("pTo", [128, 512], BF16, ph)
            sqt = S.sb("sqtb", [128, 512], F32, ph)
            onf = S.sb("onf", [128, 512], F32, ph)
            obb = S.sb("obb", [128, 512], BF16, ph)
            ss = S.sb("ssb", [128, 8], F32, ph)
            rstd = S.sb("rstdb", [128, 8], F32, ph)
            stg = [S.sb("stgB", [128, 4, 512], BF16, ph) for _ in range(2)]
            for n in range(NT):
                r0 = n * 128
                a = n % 2
                S.sp.dma(QTt[a][:], s["QBT"][:, :, r0:r0 + 128], reads=[self.sb_["QBT"]], writes=[QTt[a]])
                S.sp.dma(KTt[a][:], s["KBT"][:, :, r0:r0 + 128], reads=[self.sb_["KBT"]], writes=[KTt[a]])
                S.sp.dma(VBt[a][:], s["VB"][r0:r0 + 128, :], reads=[self.sb_["VB"]], writes=[VBt[a]])
                S.sp.dma(GBt[a][:], s["GB"][r0:r0 + 128, :], reads=[self.sb_["GB"]], writes=[GBt[a]])
                for h in range(8):
                    self.mm(pST[a][:, h * 128:(h + 1) * 128], KTt[a][:, h, :], QTt[a][:, h, :], True, True,
                            [KTt[a], QTt[a]], [pST[a]], inc=(h == 7))
                for hh in range(2):
                    self.tt(S.dve, SD[a][:, 4 * hh:4 * hh + 4, :],
                            pST[a][:, hh * 512:(hh + 1) * 512].rearrange("p (h t) -> p h t", h=4),
                            Dm[:, 4 * hh:4 * hh + 4, :], ALU.mult, [pST[a], Dm], [SD[a]])
                self.tt(S.dve, Qdf[a][:], QTt[a][:], qdf[:], ALU.mult, [QTt[a], qdf], [Qdf[a]])
                self.tt(S.dve, Qdb[a][:], QTt[a][:], qdb[:], ALU.mult, [QTt[a], qdb], [Qdb[a]])
                for h in range(8):
                    hs = slice(h * 64, (h + 1) * 64)
                    self.mm(pOb[:, hs], SD[a][:, h, :], VBt[a][:, hs], True, False, [SD[a], VBt[a]], [pOb], inc=False)
                    self.mm(pOb[:, hs], Qdf[a][:, h, :], SAll[0][:, n, hs], False, False, [Qdf[a], SAll[0]], [pOb],
                            inc=False)
                    self.mm(pOb[:, hs], Qdb[a][:, h, :], SAll[1][:, n, hs], False, True, [Qdb[a], SAll[1]], [pOb],
                            inc=(h == 7))
                self.actf(sqt[:], pOb[:, :], AF.Square, [pOb], [sqt])
                S.dve.op(lambda: nc.vector.tensor_reduce(out=ss[:, 0:8], in_=sqt[:].rearrange("p (h d) -> p h d", h=8),
                                                         axis=AX.X, op=ALU.add), [sqt], [ss])
                self.rms_rstd(rstd, ss, 64.0, 8)
                o3 = onf[:].rearrange("p (h d) -> p h d", h=8)
                self.tt(S.dve, o3, pOb[:, :].rearrange("p (h d) -> p h d", h=8), AP(rstd[:, 0:8], [[1, 8], [0, 64]]),
                        ALU.mult, [pOb, rstd], [onf])
                self.tt(S.dve, o3, o3, AP(retg[:], [[0, 8], [1, 64]]), ALU.mult, [onf, retg], [onf])
                self.tt(S.dve, obb[:], onf[:], GBt[a][:], ALU.mult, [onf, GBt[a]], [obb])
                sub, sup = n % 4, n // 4
                for c in range(4):
                    self.tr(pTo[:, c * 128:(c + 1) * 128], obb[:, c * 128:(c + 1) * 128], self.ident[:],
                            [obb, self.ident], [pTo], inc=(c == 3))
                st = stg[sup % 2]
                self.cp(S.act, st[:, :, sub * 128:(sub + 1) * 128], pTo[:, :].rearrange("p (c t) -> p c t", c=4),
                        [pTo], [st])
                if sub == 3:
                    S.act.dma(s["OBT"][:, :, sup * 512:(sup + 1) * 512], st[:], reads=[st], writes=[self.sb_["OBT"]])

    def out_ffn(self, l):
        S, i = self.S, self.i
        NCH = FF // 128
        with ExitStack() as ph:
            W1 = S.sb("W1", [128, 8, FF], BF16, ph)
            W3 = S.sb("W3", [128, 8, FF], BF16, ph)
            W2 = S.sb("W2", [128, NCH, D], BF16, ph)

            def prefetch(stage):
                return (self.load_w_steps(W1, i["ffn_w1"][l], FF, 8, stage)
                        + self.load_w_steps(W3, i["ffn_w3"][l], FF, 8, stage)
                        + self.load_w_steps(W2, i["ffn_w2"][l], D, NCH, stage))
            self.out_proj(l, prefetch)
            S.barrier()
            self.ffn(l, (W1, W3, W2))
            S.barrier()

    def out_proj(self, l, prefetch=None):
        S, nc, i, s, o = self.S, self.nc, self.i, self.s, self.o
        if l == 0:
            wsrc, xin = i["ab_w_out"][0], i["x"]
            srcs = [("OAT", 64, 8, 0), ("OBT", 128, 4, 512)]
            xin_b = None
        else:
            wsrc, xin = i["cd_w_out"][0], s["X2"]
            srcs = [("OCT", 128, 4, 0), ("ODT", 128, 4, 512)]
            xin_b = self.sb_["X2"]
        with ExitStack() as ph:
            Ws = []
            wstage = [S.sb("wost", [128, FF], F32, ph) for _ in range(2)]
            for (nm, K, nch, row0) in srcs:
                Wt = S.sb("wo_" + nm, [K, nch, D], BF16, ph)
                self.load_w_bf16(Wt, wsrc[row0:row0 + K * nch, :], D, nch, wstage, rows=K)
                Ws.append(Wt)
            pf_steps = prefetch(wstage) if prefetch is not None else []
            gate = S.sb("gate1", [128, D], F32, ph)
            lts = [[S.sb("lt_" + nm, [K, nch, 128], BF16, ph) for _ in range(2)] for (nm, K, nch, row0) in srcs]
            xts = [S.sb("xo", [128, D], F32, ph) for _ in range(2)]
            tmp = S.sb("tmpo", [128, 512], F32, ph)
            pss = [S.ps("pso", [128, 512], F32, ph) for _ in range(3)]
            cur = -1
            pi = 0
            for t in range(NT):
                cond = 0 if t < NTS else 1
                if cond != cur:
                    self.load_mod(gate, l, cond, 2)
                    cur = cond
                r0 = t * 128
                xt = xts[t % 2]
                S.sp.dma(xt[:], xin[r0:r0 + 128, :], reads=[xin_b], writes=[xt])
                if pf_steps:
                    pf_steps.pop(0)()
                for si, (nm, K, nch, row0) in enumerate(srcs):
                    S.sp.dma(lts[si][t % 2][:], s[nm][:, :, r0:r0 + 128], reads=[self.sb_[nm]], writes=[lts[si][t % 2]])
                for cb in range(2):
                    ps = pss[pi % 3]
                    pi += 1
                    chain = [(lts[si][t % 2], Ws[si], c) for si, (nm, K, nch, row0) in enumerate(srcs) for c in range(nch)]
                    for ci, (lt, Wt, c) in enumerate(chain):
                        self.mm(ps[:, :], lt[:, c, :], Wt[:, c, cb * 512:(cb + 1) * 512], ci == 0, ci == len(chain) - 1,
                                [lt, Wt], [ps])
                    self.tt(S.dve, tmp[:], ps[:, :], gate[:, cb * 512:(cb + 1) * 512], ALU.mult, [ps, gate], [tmp])
                    self.tt(S.dve, xt[:, cb * 512:(cb + 1) * 512], tmp[:], xt[:, cb * 512:(cb + 1) * 512], ALU.add,
                            [tmp, xt], [xt])
                S.act.dma(s["X1"][r0:r0 + 128, :], xt[:], reads=[xt], writes=[self.sb_["X1"]])
            while pf_steps:
                pf_steps.pop(0)()

    def ffn(self, l, Ws):
        S, nc, i, s, o = self.S, self.nc, self.i, self.s, self.o
        dst = s["X2"] if l == 0 else o["y"]
        dst_b = self.sb_["X2"] if l == 0 else self.ob_["y"]
        NCH = FF // 128
        with ExitStack() as ph:
            W1, W3, W2 = Ws
            G = S.sb("G2", [128, D], F32, ph)
            SH = S.sb("SH2", [128, D], F32, ph)
            gate = S.sb("gate2", [128, D], F32, ph)
            xts = [S.sb("xf", [128, D], F32, ph) for _ in range(2)]
            hf = S.sb("hf2", [128, D], F32, ph)
            hb = S.sb("hb2", [128, D], BF16, ph)
            hT = S.sb("hT2", [128, 8, 512], BF16, ph)
            GT = S.sb("GT", [128, NCH, 512], BF16, ph)
            sas = [S.sb("sa", [128, 512], F32, ph) for _ in range(2)]
            tmp = S.sb("tmpf", [128, 512], F32, ph)
            ss = S.sb("ss2", [128, 8], F32, ph)
            rstd = S.sb("rstd2", [128, 8], F32, ph)
            pT = S.ps("pT2", [128, 1024], BF16, ph)
            pA = [S.ps("pA", [128, 512], F32, ph) for _ in range(2)]
            pB = [S.ps("pB", [128, 512], F32, ph) for _ in range(2)]
            pY = [S.ps("pY", [128, 512], F32, ph) for _ in range(2)]
            cur = -1
            xi = 0
            yi = 0
            for blk in range(TOK // 512):
                cond = 0 if blk < TS // 512 else 1
                if cond != cur:
                    self.make_norm_scale(G, SH, l, cond, i["norm2_g"][l], 4, 3, gate)
                    self.load_mod(gate, l, cond, 5)
                    cur = cond
                for sub in range(4):
                    r0 = blk * 512 + sub * 128
                    xt = xts[xi % 2]
                    xi += 1
                    S.sp.dma(xt[:], s["X1"][r0:r0 + 128, :], reads=[self.sb_["X1"]], writes=[xt])
                    self.norm_tile(xt, G, SH, hf, hb, ss, rstd)
                    self.transpose_to(pT, None, hb, 8,
                                      lambda: self.cp(S.act, hT[:, :, sub * 128:(sub + 1) * 128],
                                                      pT[:, :].rearrange("p (k t) -> p k t", k=8), [pT], [hT]))
                for hc in range(NCH):
                    a = hc % 2
                    cs = slice(hc * 128, (hc + 1) * 128)
                    for k in range(8):
                        self.mm(pA[a][:, :], W1[:, k, cs], hT[:, k, :], k == 0, k == 7, [W1, hT], [pA[a]])
                    for k in range(8):
                        self.mm(pB[a][:, :], W3[:, k, cs], hT[:, k, :], k == 0, k == 7, [W3, hT], [pB[a]])
                    self.actf(sas[a][:], pA[a][:, :], AF.Silu, [pA[a]], [sas[a]])
                    self.tt(S.dve, GT[:, hc, :], pB[a][:, :], sas[a][:], ALU.mult, [pB[a], sas[a]], [GT])
                for sub in range(4):
                    r0 = blk * 512 + sub * 128
                    xt = xts[xi % 2]
                    xi += 1
                    S.sp.dma(xt[:], s["X1"][r0:r0 + 128, :], reads=[self.sb_["X1"]], writes=[xt])
                    for cb in range(2):
                        py = pY[yi % 2]
                        yi += 1
                        for hc in range(NCH):
                            self.mm(py[:, :], GT[:, hc, sub * 128:(sub + 1) * 128], W2[:, hc, cb * 512:(cb + 1) * 512],
                                    hc == 0, hc == NCH - 1, [GT, W2], [py])
                        self.tt(S.dve, tmp[:], py[:, :], gate[:, cb * 512:(cb + 1) * 512], ALU.mult, [py, gate], [tmp])
                        self.tt(S.dve, xt[:, cb * 512:(cb + 1) * 512], tmp[:], xt[:, cb * 512:(cb + 1) * 512],
                                ALU.add, [tmp, xt], [xt])
                    S.pool.dma(dst[r0:r0 + 128, :], xt[:], reads=[xt], writes=[dst_b])

    def l1_in(self):
        S, nc, i, s, o = self.S, self.nc, self.i, self.s, self.o
        with ExitStack() as ph:
            W = S.sb("win1", [128, 8, 2048], BF16, ph)
            wstage = [S.sb("winst", [128, 2048], F32, ph) for _ in range(2)]
            self.load_w_bf16(W, i["cd_w_in"][0], 2048, 8, wstage)
            G = S.sb("G", [128, D], F32, ph)
            SH = S.sb("SH", [128, D], F32, ph)
            qg = S.sb("qg", [128, 64], F32, ph)
            kg = S.sb("kg", [128, 64], F32, ph)
            S.sp.dma(qg[:], self.bc(i["d_q_norm"][0], 128, 64), writes=[qg])
            S.sp.dma(kg[:], self.bc(i["d_k_norm"][0], 128, 64), writes=[kg])
            xts = [S.sb("xt", [128, D], F32, ph) for _ in range(2)]
            hf = [S.sb("hf", [128, D], F32, ph) for _ in range(2)]
            hbs = [S.sb("hb", [128, D], BF16, ph) for _ in range(2)]
            hTs = [S.sb("hT", [128, 8, 128], BF16, ph) for _ in range(2)]
            ss = [S.sb("ss", [128, 8], F32, ph) for _ in range(2)]
            rstd = [S.sb("rstd", [128, 8], F32, ph) for _ in range(2)]
            ssh = [S.sb("ssh", [128, 8], F32, ph) for _ in range(2)]
            rsh = [S.sb("rsh", [128, 8], F32, ph) for _ in range(2)]
            sqt = [S.sb("sqt", [128, 512], F32, ph) for _ in range(2)]
            qf = [S.sb("qf", [128, 512], F32, ph) for _ in range(2)]
            kf = [S.sb("kf", [128, 512], F32, ph) for _ in range(2)]
            vf = [S.sb("vf", [128, 512], F32, ph) for _ in range(2)]
            t1 = [S.sb("t1", [128, 256], F32, ph) for _ in range(2)]
            t2 = [S.sb("t2", [128, 256], F32, ph) for _ in range(2)]
            cos = [S.sb("cos", [128, 32], F32, ph) for _ in range(2)]
            sin = [S.sb("sin", [128, 32], F32, ph) for _ in range(2)]
            ub = [S.sb("ub", [128, 512], BF16, ph) for _ in range(2)]
            qdb = [S.sb("qdb", [128, 512], BF16, ph) for _ in range(2)]
            kdb = [S.sb("kdb", [128, 512], BF16, ph) for _ in range(2)]
            vdb = [S.sb("vdb", [128, 512], BF16, ph) for _ in range(2)]
            stQ = [S.sb("stQ", [64, 8, 512], BF16, ph) for _ in range(2)]
            stK = [S.sb("stK", [64, 8, 512], BF16, ph) for _ in range(2)]
            pT = [S.ps("pT", [128, 1024], BF16, ph) for _ in range(2)]
            pbs = [S.ps("pb", [128, 512], F32, ph) for _ in range(4)]
            pOs = [S.ps("pO", [128, 1024], BF16, ph) for _ in range(2)]
            tmpm = S.sb("tmpm", [128, D], F32, ph)
            cnt = {"pb": 0, "po": 0, "cond": -1}

            def tile_gen(t):
                p = t % 2
                cond = 0 if t < NTS else 1
                sub, sup, r0 = t % 4, t // 4, t * 128
                xt, hb, hT = xts[p], hbs[p], hTs[p]
                if cond != cnt["cond"]:
                    self.make_norm_scale(G, SH, 1, cond, i["norm1_g"][1], 1, 0, tmpm)
                    cnt["cond"] = cond
                S.sp.dma(xt[:], s["X2"][r0:r0 + 128, :], reads=[self.sb_["X2"]], writes=[xt])
                if cond == 0:
                    self.load_rope(cos[p], sin[p], t)
                yield from self.norm_tile_g(xt, G, SH, hf[p], hb, ss[p], rstd[p])
                yield
                self.transpose_to(pT[p], None, hb, 8,
                                  lambda: self.cp(S.act, hT[:].rearrange("p k t -> p (k t)"), pT[p][:, :], [pT[p]], [hT]))
                yield

                def proj(c0, ncols):
                    pb = pbs[cnt["pb"] % 4]
                    cnt["pb"] += 1
                    for k in range(8):
                        self.mm(pb[:, 0:ncols], hT[:, k, :], W[:, k, c0:c0 + ncols], k == 0, k == 7, [hT, W], [pb])
                    return pb

                def to_featmajor(src_b, H, stage):
                    pO = pOs[cnt["po"] % 2]
                    cnt["po"] += 1
                    for h in range(H):
                        self.tr(pO[0:64, h * 128:(h + 1) * 128], src_b[:, h * 64:(h + 1) * 64], self.ident[:],
                                [src_b, self.ident], [pO], inc=(h == H - 1))
                    yield
                    self.cp(S.act, stage[:, :, sub * 128:(sub + 1) * 128],
                            pO[0:64, 0:H * 128].rearrange("p (h t) -> p h t", h=H), [pO], [stage])

                pb = proj(0, 512)
                yield
                u = ub[p]
                self.cp(S.act, u[:], pb[:, :], [pb], [u])
                S.pool.dma(s["UTM"][r0:r0 + 128, :], u[:], reads=[u], writes=[self.sb_["UTM"]])
                pb = proj(512, 512)
                yield
                yield from self.head_norm_g(pb, 8, qg, qf[p], sqt[p], ssh[p], rsh[p])
                yield
                qd = qdb[p]
                if cond == 0:
                    self.rope_apply(qd, qf[p], 8, cos[p], sin[p], t1[p], t2[p])
                else:
                    self.cp(S.dve, qd[:], qf[p][:], [qf[p]], [qd])
                yield
                yield from to_featmajor(qd, 8, stQ[sup % 2])
                pb = proj(1024, 512)
                yield
                yield from self.head_norm_g(pb, 8, kg, kf[p], sqt[p], ssh[p], rsh[p])
                yield
                kd = kdb[p]
                if cond == 0:
                    self.rope_apply(kd, kf[p], 8, cos[p], sin[p], t1[p], t2[p])
                else:
                    self.cp(S.dve, kd[:], kf[p][:], [kf[p]], [kd])
                    pr = r0 - TS
                    S.pool.dma(o["nkd"][pr:pr + 128, :], kf[p][:], reads=[kf[p]], writes=[self.ob_["nkd"]])
                yield
                yield from to_featmajor(kd, 8, stK[sup % 2])
                pb = proj(1536, 512)
                yield
                vd = vdb[p]
                if cond == 0:
                    self.cp(S.act, vd[:], pb[:, :], [pb], [vd])
                else:
                    self.cp(S.act, vf[p][:], pb[:, :], [pb], [vf[p]])
                    self.cp(S.dve, vd[:], vf[p][:], [vf[p]], [vd])
                    pr = r0 - TS
                    S.pool.dma(o["nvd"][pr:pr + 128, :], vf[p][:], reads=[vf[p]], writes=[self.ob_["nvd"]])
                S.pool.dma(s["VD"][r0:r0 + 128, :], vd[:], reads=[vd], writes=[self.sb_["VD"]])
                if sub == 3:
                    c0 = sup * 512
                    for nm, st in (("QDT", stQ), ("KDT", stK)):
                        S.act.dma(s[nm][:, :, c0:c0 + 512], st[sup % 2][:], reads=[st[sup % 2]],
                                  writes=[self.sb_[nm]])

            for t0 in range(0, NT, 2):
                alive = [tile_gen(t0), tile_gen(t0 + 1)]
                while alive:
                    for g_ in list(alive):
                        try:
                            next(g_)
                        except StopIteration:
                            alive.remove(g_)

    def l1_attn_d(self):
        S, nc, i, s, o = self.S, self.nc, self.i, self.s, self.o
        lam_init = 0.8 - 0.6 * math.exp(-0.3 * 1)
        NKT = 2 + NTS
        with ExitStack() as ph:
            lp = S.sb("lp", [128, 4, 64], F32, ph)
            S.sp.dma(lp[:].rearrange("p a d -> p (a d)"), self.bc(i["d_lambda"][0], 128, 256), writes=[lp])
            pr = S.sb("lpr", [128, 2, 64], F32, ph)
            self.tt(S.dve, pr[:, 0, :], lp[:, 0, :], lp[:, 1, :], ALU.mult, [lp], [pr])
            self.tt(S.dve, pr[:, 1, :], lp[:, 2, :], lp[:, 3, :], ALU.mult, [lp], [pr])
            l2 = S.sb("l2", [128, 2], F32, ph)
            S.dve.op(lambda: nc.vector.tensor_reduce(out=l2[:], in_=pr[:], axis=AX.X, op=ALU.add), [pr], [l2])
            self.actf(l2[:], l2[:], AF.Exp, [l2], [l2])
            nlam = S.sb("nlam", [128, 1], F32, ph)
            self.stt(nlam[:], l2[:, 0:1], -1.0, l2[:, 1:2], ALU.mult, ALU.add, [l2], [nlam])
            self.ts(S.dve, nlam[:], nlam[:], -lam_init, None, ALU.add, None, [nlam], [nlam])
            sgc = S.sb("sgc", [128, 1], F32, ph)
            S.sp.dma(sgc[:], bass.AP(i["d_subln"].tensor, 0, [[1, 128], [1, 1]]), writes=[sgc])
            self.ts(S.dve, sgc[:], sgc[:], 1.0 - lam_init, None, ALU.mult, None, [sgc], [sgc])
            kcT = S.sb("kcTd", [128, 4, 256], BF16, ph)
            pS = [[S.ps("pSd", [128, 512], F32, ph) for _ in range(2)] for _ in range(2)]
            pO = [S.ps("pOd", [128, 512], F32, ph) for _ in range(2)]
            pZ = [S.ps("pZd", [128, 512], F32, ph) for _ in range(2)]
            PT = [[S.sb("PTd", [128, 512], BF16, ph) for _ in range(2)] for _ in range(2)]
            QT = S.sb("QTd", [128, TS], BF16, ph)
            KT = S.sb("KTd", [128, TS], BF16, ph)
            Vt = S.sb("Vtd", [128, NKT, 128], BF16, ph)
            rz = [S.sb("rzd", [128, 512], F32, ph) for _ in range(2)]
            zacc = [S.sb("zacc", [128, 512], F32, ph) for _ in range(2)]
            zacc2 = S.sb("zacc2", [128, 512], F32, ph)
            zhi = [S.sb("zhi", [128, 512], BF16, ph) for _ in range(2)]
            zlo = [S.sb("zlo", [128, 512], BF16, ph) for _ in range(2)]
            tq = [S.sb("tqd", [128, 512], F32, ph) for _ in range(2)]
            od = S.sb("odd", [128, 512], F32, ph)
            sqb = S.sb("sqbd", [128, 512], BF16, ph)
            rs = S.sb("rsd", [128, 512], F32, ph)
            stg = [S.sb("stgD", [128, 512], BF16, ph) for _ in range(2)]
            cnt = {"it": 0, "st": 0}
            kcf = S.sb("kcf", [128, 2, 512], F32, ph)
            identf = S.sb("identfD", [128, 128], F32, ph)
            self.cp(S.dve, identf[:], self.ident[:], [self.ident], [identf])
            S.sp.dma(kcf[:], i["cache_k_d"].rearrange("(n p) d -> p n d", p=128), writes=[kcf])
            for n in range(2):
                for h in range(4):
                    self.tr(pO[0][:, h * 128:(h + 1) * 128], kcf[:, n, h * 128:(h + 1) * 128], identf[:],
                            [kcf, identf], [pO[0]], inc=(h == 3))
                self.cp(S.act, kcT[:, :, n * 128:(n + 1) * 128], pO[0][:, :].rearrange("p (h t) -> p h t", h=4),
                        [pO[0]], [kcT])

            def attend(qT, q0, nq, ktiles, rd, store):
                nk = len(ktiles)
                bufs = []
                for kti in range(nk + 1):
                    if kti < nk:
                        kt_, va = ktiles[kti]
                        a = cnt["it"] % 2
                        cnt["it"] += 1
                        bufs.append(a)
                        for c in range(2):
                            hs = slice(64 * c, 64 * c + 64)
                            self.mm(pS[c][a][:, 0:nq], kt_[hs, :], qT[hs, q0:q0 + nq], True, True, rd, [pS[c][a]])
                        for c in range(2):
                            self.actf(PT[c][a][:, 0:nq], pS[c][a][:, 0:nq], AF.Exp, [pS[c][a]], [PT[c][a]], scale=0.125)
                    if kti > 0:
                        kp = kti - 1
                        a = bufs[kp]
                        va = ktiles[kp][1]
                        first, last = kp == 0, kp == nk - 1
                        for c in range(2):
                            self.mm(pO[c][:, 0:nq], va, PT[c][a][:, 0:nq], first, last, rd + [PT[c][a]], [pO[c]],
                                    inc=True)
                            if c == 0:
                                if first:
                                    self.cp(S.dve, zacc[0][:, 0:nq], PT[0][a][:, 0:nq], [PT[0][a]], [zacc[0]])
                                else:
                                    self.tt(S.dve, zacc[0][:, 0:nq], zacc[0][:, 0:nq], PT[0][a][:, 0:nq], ALU.add,
                                            [zacc[0], PT[0][a]], [zacc[0]], nosync=True)
                            else:
                                self.mm(pZ[1][:, 0:nq], self.ones_bf[:], PT[1][a][:, 0:nq], first, last,
                                        [self.ones_bf, PT[1][a]], [pZ[1]], inc=True)
                for c in range(1):
                    self.cp(S.pool, zhi[c][:, 0:nq], zacc[c][:, 0:nq], [zacc[c]], [zhi[c]])
                    self.tt(S.pool, zlo[c][:, 0:nq], zacc[c][:, 0:nq], zhi[c][:, 0:nq], ALU.subtract,
                            [zacc[c], zhi[c]], [zlo[c]])
                    self.mm(pZ[c][:, 0:nq], self.ones_bf[:], zhi[c][:, 0:nq], True, False, [self.ones_bf, zhi[c]],
                            [pZ[c]], inc=False)
                    self.mm(pZ[c][:, 0:nq], self.ones_bf[:], zlo[c][:, 0:nq], False, True, [self.ones_bf, zlo[c]],
                            [pZ[c]], inc=True)
                st = stg[cnt["st"] % 2]
                cnt["st"] += 1
                for c in range(2):
                    S.dve.op(lambda: nc.vector.reciprocal(out=rz[c][:, 0:nq], in_=pZ[c][:, 0:nq]), [pZ[c]], [rz[c]])
                    self.tt(S.dve, tq[c][:, 0:nq], pO[c][:, 0:nq], rz[c][:, 0:nq], ALU.mult, [pO[c], rz[c]], [tq[c]])
                self.stt(od[:, 0:nq], tq[1][:, 0:nq], nlam[:, 0:1], tq[0][:, 0:nq], ALU.mult, ALU.add,
                         [tq[0], tq[1], nlam], [od])
                self.tt(S.pool, sqb[:, 0:nq], od[:, 0:nq], od[:, 0:nq], ALU.mult, [od], [sqb])
                a = cnt["it"] % 2
                self.mm(pS[0][a][:, 0:nq], self.ones_bf[:], sqb[:, 0:nq], True, True, [self.ones_bf, sqb], [pS[0][a]])
                self.ts(S.dve, rs[:, 0:nq], pS[0][a][:, 0:nq], 1.0 / 128.0, EPS, ALU.mult, ALU.add, [pS[0][a]], [rs])
                self.actf(rs[:, 0:nq], rs[:, 0:nq], AF.Ln, [rs], [rs])
                self.actf(rs[:, 0:nq], rs[:, 0:nq], AF.Exp, [rs], [rs], scale=-0.5)
                self.stt(st[:, 0:nq], od[:, 0:nq], sgc[:, 0:1], rs[:, 0:nq], ALU.mult, ALU.mult, [od, sgc, rs], [st])
                store(st)

            for h in range(4):
                for c in range(2):
                    S.sp.dma(QT[64 * c:64 * c + 64, :], s["QDT"][:, 2 * h + c, 0:TS], reads=[self.sb_["QDT"]], writes=[QT])
                    S.sp.dma(KT[64 * c:64 * c + 64, :], s["KDT"][:, 2 * h + c, 0:TS], reads=[self.sb_["KDT"]], writes=[KT])
                S.pool.dma(Vt[:, 0:2, :], i["cache_v_d"][:, h * 128:(h + 1) * 128].rearrange("(n p) d -> p n d", p=128),
                           writes=[Vt])
                for part in range(4):
                    S.sp.dma(Vt[:, 2 + part * 8:2 + (part + 1) * 8, :],
                             s["VD"][part * 1024:(part + 1) * 1024, h * 128:(h + 1) * 128].rearrange("(n p) d -> p n d", p=128),
                             reads=[self.sb_["VD"]], writes=[Vt])
                ktiles = [(kcT[:, h, n * 128:(n + 1) * 128], Vt[:, n, :]) for n in range(2)]
                ktiles += [(KT[:, n * 128:(n + 1) * 128], Vt[:, 2 + n, :]) for n in range(NTS)]
                for qs in range(TS // 512):
                    def store(st, h=h, qs=qs):
                        S.pool.dma(s["ODT"][:, h, qs * 512:(qs + 1) * 512], st[:], reads=[st], writes=[self.sb_["ODT"]])
                    attend(QT, qs * 512, 512, ktiles, [QT, KT, Vt, kcT], store)
            QTp = S.sb("QTdp", [128, 256], BF16, ph)
            KTp = S.sb("KTdp", [128, 256], BF16, ph)
            Vtp = S.sb("Vtdp", [128, 2, 128], BF16, ph)
            for sqi in range(4):
                base = TS + 256 * sqi
                for h in range(4):
                    for c in range(2):
                        S.sp.dma(QTp[64 * c:64 * c + 64, :], s["QDT"][:, 2 * h + c, base:base + 256],
                                 reads=[self.sb_["QDT"]], writes=[QTp])
                        S.sp.dma(KTp[64 * c:64 * c + 64, :], s["KDT"][:, 2 * h + c, base:base + 256],
                                 reads=[self.sb_["KDT"]], writes=[KTp])
                    S.sp.dma(Vtp[:], s["VD"][base:base + 256, h * 128:(h + 1) * 128].rearrange("(n p) d -> p n d", p=128),
                             reads=[self.sb_["VD"]], writes=[Vtp])
                    ktiles = [(KTp[:, n * 128:(n + 1) * 128], Vtp[:, n, :]) for n in range(2)]

                    def store(st, h=h, base=base):
                        S.pool.dma(s["ODT"][:, h, base:base + 256], st[:, 0:256], reads=[st], writes=[self.sb_["ODT"]])
                    attend(QTp, 0, 256, ktiles, [QTp, KTp, Vtp], store)

    def l1_s5_c(self):
        S, nc, i, s, o = self.S, self.nc, self.i, self.s, self.o

        def stop(tag):
            if os.environ.get("S5_STOP") == tag:
                S.barrier()
                S.finish()
                self.leak = ExitStack()
                self.leak.push(lambda *a: True)
                raise StopBuild()
        NU = 640
        dvp = [S.dve, S.pool]
        with ExitStack() as ph:
            identf = S.sb("identf", [128, 128], F32, ph)
            self.cp(S.dve, identf[:], self.ident[:], [self.ident], [identf])
            VRe = [S.sb("VRe", [128, 32, 64], BF16, ph) for _ in range(2)]
            VIm = [S.sb("VIm", [128, 32, 64], BF16, ph) for _ in range(2)]
            CRe = [S.sb("CRe", [128, 16, 128], BF16, ph) for _ in range(2)]
            CIm = [S.sb("CIm", [128, 16, 128], BF16, ph) for _ in range(2)]
            Mg = S.sb("Mg", [128, 32, 128], BF16, ph)
            R8 = S.sb("R8", [128, 32], F32, ph)
            TH8 = S.sb("TH8", [128, 32], F32, ph)
            H0 = [S.sb("H0", [128, 32], F32, ph) for _ in range(2)]
            mask8 = S.sb("mask8", [128, 8], BF16, ph)
            Sel = S.sb("Sel", [128, 16], BF16, ph)
            maskI = S.sb("maskI", [128, 8], BF16, ph)
            Esel = S.sb("Esel", [128, 16], BF16, ph)
            FIN = [S.sb("FIN", [128, 4, 32], F32, ph) for _ in range(2)]
            self.memset(S.pool, mask8[:], 0.0, [mask8])
            self.memset(S.pool, Esel[:], 0.0, [Esel])
            for n in range(16):
                S.pool.op(lambda: nc.gpsimd.affine_select(out=mask8[:], in_=mask8[:], compare_op=ALU.not_equal, fill=1.0,
                                                          base=-8 * n, pattern=[[-1, 8]], channel_multiplier=1),
                          [mask8], [mask8])
            for n in range(8):
                S.pool.op(lambda: nc.gpsimd.affine_select(out=Esel[:], in_=Esel[:], compare_op=ALU.not_equal, fill=1.0,
                                                          base=-16 * n, pattern=[[-1, 16]], channel_multiplier=1),
                          [Esel], [Esel])
            for (m_, step, hi) in ((Sel, 8, 7), (maskI, 16, 15)):
                n_ = m_.t.shape[1]
                self.memset(S.pool, m_[:], 1.0, [m_])
                S.pool.op(lambda: nc.gpsimd.affine_select(out=m_[:], in_=m_[:], compare_op=ALU.is_ge, fill=0.0, base=0,
                                                          pattern=[[-step, n_]], channel_multiplier=1), [m_], [m_])
                S.pool.op(lambda: nc.gpsimd.affine_select(out=m_[:], in_=m_[:], compare_op=ALU.is_ge, fill=0.0, base=hi,
                                                          pattern=[[step, n_]], channel_multiplier=-1), [m_], [m_])
            stop("masks")
            pT4 = S.ps("pT4", [128, 512], F32, ph)
            with ExitStack() as st:
                raw = S.sb("raw", [32, 4, 128], F32, st)
                for k, nm in enumerate(("ssm_lambda_re", "ssm_lambda_im")):
                    S.sp.dma(raw[:, k, :], bass.AP(i[nm].tensor, 0, [[128, 32], [1, 128]]), writes=[raw])
                for k, nm in enumerate(("state_ssm_re", "state_ssm_im")):
                    S.sp.dma(raw[:, 2 + k, :], bass.AP(i[nm].tensor, 0, [[128, 32], [1, 128]]), writes=[raw])
                LR = S.sb("LR", [128, 32], F32, st)
                LI = S.sb("LI", [128, 32], F32, st)
                for k, dstt in enumerate((LR, LI, H0[0], H0[1])):
                    self.tr(pT4[:, k * 32:(k + 1) * 32], raw[:, k, :], identf[0:32, 0:32], [raw, identf], [pT4])
                    self.cp(S.act, dstt[:], pT4[:, k * 32:(k + 1) * 32], [pT4], [dstt])
                stop("lam")
                DT = S.sb("DT", [128, 32], F32, st)
                ldt = S.sb("ldt", [128, 64], F32, st)
                S.sp.dma(ldt[:], self.bc(i["ssm_log_dt"][0], 128, 64), writes=[ldt])
                for m in range(2):
                    self.actf(DT[64 * m:64 * m + 64, :], AP(ldt[64 * m:64 * m + 64, m:m + 1], [[2, 32]]), AF.Exp,
                              [ldt], [DT])
                LRDT = S.sb("LRDT", [128, 32], F32, st)
                TH = S.sb("TH", [128, 32], F32, st)
                self.tt(S.dve, LRDT[:], LR[:], DT[:], ALU.mult, [LR, DT], [LRDT])
                self.tt(S.dve, TH[:], LI[:], DT[:], ALU.mult, [LI, DT], [TH])
                self.ts(S.dve, TH8[:], TH[:], 8.0, None, ALU.mult, None, [TH], [TH8])
                stop("dt")
                MAG = S.sb("MAG", [128, 16, 32], F32, st)
                ANG = S.sb("ANG", [128, 16, 32], F32, st)
                PWr = S.sb("PWr", [128, 16, 32], F32, st)
                PWi = S.sb("PWi", [128, 16, 32], F32, st)
                for kk in range(16):
                    self.actf(MAG[:, kk, :], LRDT[:], AF.Exp, [LRDT], [MAG], scale=float(kk - 7))
                    self.ts(S.dve, ANG[:, kk, :], TH[:], float(kk - 7), None, ALU.mult, None, [TH], [ANG])
                self.cp(S.act, R8[:], MAG[:, 15, :], [MAG], [R8])
                self.sin_of(PWr[:].rearrange("p k c -> p (k c)"), PWr, ANG[:].rearrange("p k c -> p (k c)"),
                            math.pi / 2.0, [128, 512], st, [ANG])
                self.sin_of(PWi[:].rearrange("p k c -> p (k c)"), PWi, ANG[:].rearrange("p k c -> p (k c)"),
                            0.0, [128, 512], st, [ANG])
                self.tt(S.dve, PWr[:], PWr[:], MAG[:], ALU.mult, [PWr, MAG], [PWr])
                self.tt(S.dve, PWi[:], PWi[:], MAG[:], ALU.mult, [PWi, MAG], [PWi])
                stop("pw")
                den = S.sb("den5", [128, 32], F32, st)
                am1 = S.sb("am1", [128, 32], F32, st)
                fr = S.sb("fr", [128, 32], F32, st)
                fi = S.sb("fi", [128, 32], F32, st)
                tq = S.sb("tq", [128, 32], F32, st)
                self.tt(S.dve, den[:], LR[:], LR[:], ALU.mult, [LR], [den])
                self.tt(S.dve, tq[:], LI[:], LI[:], ALU.mult, [LI], [tq])
                self.tt(S.dve, den[:], den[:], tq[:], ALU.add, [den, tq], [den])
                S.dve.op(lambda: nc.vector.reciprocal(out=den[:], in_=den[:]), [den], [den])
                self.ts(S.dve, am1[:], PWr[:, 8, :], -1.0, None, ALU.add, None, [PWr], [am1])
                self.tt(S.dve, fr[:], am1[:], LR[:], ALU.mult, [am1, LR], [fr])
                self.tt(S.dve, tq[:], PWi[:, 8, :], LI[:], ALU.mult, [PWi, LI], [tq])
                self.tt(S.dve, fr[:], fr[:], tq[:], ALU.add, [fr, tq], [fr])
                self.tt(S.dve, fr[:], fr[:], den[:], ALU.mult, [fr, den], [fr])
                self.tt(S.dve, fi[:], PWi[:, 8, :], LR[:], ALU.mult, [PWi, LR], [fi])
                self.tt(S.dve, tq[:], am1[:], LI[:], ALU.mult, [am1, LI], [tq])
                self.tt(S.dve, fi[:], fi[:], tq[:], ALU.subtract, [fi, tq], [fi])
                self.tt(S.dve, fi[:], fi[:], den[:], ALU.mult, [fi, den], [fi])
                stop("f")
                Bre = S.sb("Bre", [128, 16, 16], F32, st)
                Bim = S.sb("Bim", [128, 16, 16], F32, st)
                for nm, dstt in (("ssm_b_re", Bre), ("ssm_b_im", Bim)):
                    for part in range(4):
                        S.sp.dma(dstt[:, part * 4:(part + 1) * 4, :],
                                 bass.AP(i[nm].tensor, part * 4 * 2048, [[16, 128], [2048, 4], [1, 16]]), writes=[dstt])
                FBr = [S.sb("FBr", [128, 16, 16], F32, st) for _ in range(2)]
                FBi = [S.sb("FBi", [128, 16, 16], F32, st) for _ in range(2)]
                tw = [S.sb("tw", [128, 16, 16], F32, st) for _ in range(4)]
                for dr in range(2):
                    frb = AP(fr[:, dr * 16:(dr + 1) * 16], [[1, 16], [0, 16]])
                    fib = AP(fi[:, dr * 16:(dr + 1) * 16], [[1, 16], [0, 16]])
                    self.tt(S.dve, tw[0][:], Bre[:], frb, ALU.mult, [Bre, fr], [tw[0]])
                    self.tt(S.dve, tw[1][:], Bim[:], fib, ALU.mult, [Bim, fi], [tw[1]])
                    self.tt(S.dve, FBr[dr][:], tw[0][:], tw[1][:], ALU.subtract, [tw[0], tw[1]], [FBr[dr]])
                    self.tt(S.dve, tw[0][:], Bim[:], frb, ALU.mult, [Bim, fr], [tw[0]])
                    self.tt(S.dve, tw[1][:], Bre[:], fib, ALU.mult, [Bre, fi], [tw[1]])
                    self.tt(S.dve, FBi[dr][:], tw[0][:], tw[1][:], ALU.add, [tw[0], tw[1]], [FBi[dr]])
                stop("b")
                CT = [S.sb("CT", [128, 16, 16], F32, st) for _ in range(2)]
                craw = S.sb("craw", [128, 4, 2, 64], F32, st)
                for k, nm in enumerate(("ssm_c_re", "ssm_c_im")):
                    for blk in range(2):
                        S.sp.dma(craw[:, :, blk, :], bass.AP(i[nm].tensor, 0, [[64, 128], [8192, 4], [1, 64]]),
                                 reads=[CT[0], CT[1]], writes=[craw])
                    for t4 in range(4):
                        self.tr(pT4[:, t4 * 128:(t4 + 1) * 128], craw[:, t4, :, :].rearrange("p b c -> p (b c)"),
                                identf[:], [craw, identf], [pT4], inc=(t4 == 3))
                    for m in range(2):
                        src = AP(pT4[64 * m:64 * m + 64, 16 * m:16 * m + 16], [[128, 4], [32, 4], [1, 16]])
                        self.cp(S.act, CT[k][64 * m:64 * m + 64, :, :].rearrange("p (a b) c -> p a b c", a=4), src,
                                [pT4], [CT[k]])
                stop("c")
                cnt = {"e": 0}

                def cpow(outr, outi, Ar, Ai, dr, efn, neg_im):
                    for x in range(8):
                        kk = efn(x) + 7
                        pr_ = AP(PWr[:, kk, dr * 16:(dr + 1) * 16], [[1, 16], [0, 16]])
                        pi_ = AP(PWi[:, kk, dr * 16:(dr + 1) * 16], [[1, 16], [0, 16]])
                        E = S.dve
                        cnt["e"] += 1
                        a, b = (tw[0], tw[1]) if E is S.dve else (tw[2], tw[3])
                        self.tt(E, a[:], Ar[:], pr_, ALU.mult, [Ar, PWr], [a])
                        self.tt(E, b[:], Ai[:], pi_, ALU.mult, [Ai, PWi], [b])
                        self.tt(E, outr[:, :, x, :], a[:], b[:], ALU.subtract, [a, b], [outr])
                        self.tt(E, a[:], Ar[:], pi_, ALU.mult, [Ar, PWi], [a])
                        self.tt(E, b[:], Ai[:], pr_, ALU.mult, [Ai, PWr], [b])
                        if neg_im:
                            self.stt_any(E, outi[:, :, x, :], a[:], -1.0, b[:], ALU.mult, ALU.subtract, [a, b], [outi])
                        else:
                            self.tt(E, outi[:, :, x, :], a[:], b[:], ALU.add, [a, b], [outi])

                XTr = [S.sb("XTr", [128, 16, 8, 16], F32, st) for _ in range(2)]
                XTi = [S.sb("XTi", [128, 16, 8, 16], F32, st) for _ in range(2)]
                cpow(XTr[0], XTi[0], FBr[0], FBi[0], 0, lambda j: 7 - j, False)
                cpow(XTr[1], XTi[1], FBr[1], FBi[1], 1, lambda j: j, False)
                YMr = [S.sb("YMr", [128, 16, 8, 16], F32, st) for _ in range(2)]
                YMi = [S.sb("YMi", [128, 16, 8, 16], F32, st) for _ in range(2)]
                cpow(YMr[0], YMi[0], CT[0], CT[1], 0, lambda x: x - 7, True)
                cpow(YMr[1], YMi[1], CT[0], CT[1], 1, lambda x: -x, True)
                YCr = S.sb("YCr", [128, 16, 8, 16], F32, st)
                YCi = S.sb("YCi", [128, 16, 8, 16], F32, st)
                for dr, efn in ((0, lambda x: x + 1), (1, lambda x: 8 - x)):
                    cpow(YCr, YCi, CT[0], CT[1], dr, efn, True)
                    self.cp(S.act, CRe[dr][:], YCr[:].rearrange("p q x c -> p q (x c)"), [YCr], [CRe[dr]])
                    self.cp(S.act, CIm[dr][:], YCi[:].rearrange("p q x c -> p q (x c)"), [YCi], [CIm[dr]])
                stop("cpow")
                pT5 = S.ps("pT5", [128, 512], F32, st)
                for dr in range(2):
                    for (src_, dst_) in ((XTr[dr], VRe[dr]), (XTi[dr], VIm[dr])):
                        for q8 in range(2):
                            for m, bank in ((0, pT4), (1, pT5)):
                                for qq in range(8):
                                    q = q8 * 8 + qq
                                    self.tr(bank[:, qq * 64:(qq + 1) * 64],
                                            src_[64 * m:64 * m + 64, q, :, :].rearrange("p x c -> p (x c)"),
                                            identf[64 * m:64 * m + 64, 64 * m:64 * m + 64], [src_, identf], [bank],
                                            inc=(qq == 7))
                            for m, bank in ((0, pT4), (1, pT5)):
                                self.cp(S.act, AP(dst_[:, q8 * 16 + m, 0:1], [[128, 8], [1, 64]]),
                                        bank[:, :].rearrange("p (g c) -> p g c", g=8), [bank], [dst_])
                stop("vtr")
                maskF = S.sb("maskF", [128, 8, 16], F32, st)
                maskB = S.sb("maskB", [128, 8, 16], F32, st)
                self.memset(S.pool, maskF[:], 1.0, [maskF])
                self.memset(S.pool, maskB[:], 1.0, [maskB])
                S.pool.op(lambda: nc.gpsimd.affine_select(out=maskF[:], in_=maskF[:], compare_op=ALU.is_ge, fill=0.0,
                                                          base=15, pattern=[[16, 8], [0, 16]], channel_multiplier=-1),
                          [maskF], [maskF])
                S.pool.op(lambda: nc.gpsimd.affine_select(out=maskB[:], in_=maskB[:], compare_op=ALU.is_ge, fill=0.0,
                                                          base=0, pattern=[[-16, 8], [0, 16]], channel_multiplier=1),
                          [maskB], [maskB])
                Dcol = S.sb("Dcol", [128, 32], F32, st)
                for j in range(8):
                    S.sp.dma(Dcol[16 * j:16 * j + 16, :], bass.AP(i["ssm_d"].tensor, 0, [[1, 16], [16, 32]]),
                             writes=[Dcol], allow_slow_non_contiguous=True)
                pMa = [S.ps("pMa", [128, 128], F32, st) for _ in range(2)]
                pMb = [S.ps("pMb", [128, 128], F32, st) for _ in range(2)]
                ma = S.sb("ma", [128, 128], F32, st)
                mb = S.sb("mb", [128, 128], F32, st)
                for g in range(32):
                    q, m = g // 2, g % 2
                    hs = slice(64 * m, 64 * m + 64)
                    for (pm, dr) in ((pMa[g % 2], 0), (pMb[g % 2], 1)):
                        self.mm(pm[:, :], XTr[dr][hs, q, :, :].rearrange("p x c -> p (x c)"),
                                YMr[dr][hs, q, :, :].rearrange("p x c -> p (x c)"), True, False,
                                [XTr[dr], YMr[dr]], [pm], inc=False)
                        self.mm(pm[:, :], XTi[dr][hs, q, :, :].rearrange("p x c -> p (x c)"),
                                YMi[dr][hs, q, :, :].rearrange("p x c -> p (x c)"), False, True,
                                [XTi[dr], YMi[dr]], [pm], inc=True)
                    self.tt(S.dve, ma[:], pMa[g % 2][:, :], maskF[:].rearrange("p a b -> p (a b)"), ALU.mult,
                            [pMa[g % 2], maskF], [ma])
                    self.tt(S.dve, mb[:], pMb[g % 2][:, :], maskB[:].rearrange("p a b -> p (a b)"), ALU.mult,
                            [pMb[g % 2], maskB], [mb])
                    self.tt(S.dve, ma[:], ma[:], mb[:], ALU.add, [ma, mb], [ma])
                    self.stt(Mg[:, g, :], identf[:], Dcol[:, g:g + 1], ma[:], ALU.mult, ALU.add,
                             [identf, Dcol, ma], [Mg])
                S.barrier()
            if os.environ.get("S5_STOP") == "setup":
                return
            U = S.sb("U5", [128, 32, NU], BF16, ph)
            Yall = S.sb("Yall", [128, 32, NU], BF16, ph)
            p1 = ExitStack()
            pU = [S.ps("pU", [128, 512], F32, p1) for _ in range(2)]
            uts = [S.sb("ut5", [128, 512], BF16, p1) for _ in range(2)]
            MLs = [S.sb("ML", [128, 32, 8, 16], BF16, p1) for _ in range(2)]
            for t in range(NT):
                a = t % 2
                ut, ML, pu = uts[a], MLs[a], pU[a]
                S.sp.dma(ut[:], s["UTM"][t * 128:(t + 1) * 128, :], reads=[self.sb_["UTM"]], writes=[ut])
                self.tt(S.dve, AP(ML[:, 0, 0, 0:1], [[128, 32], [16, 8], [1, 16]]),
                        AP(ut[:, 0:1], [[16, 32], [0, 8], [1, 16]]), AP(mask8[:, 0:1], [[0, 32], [1, 8], [0, 16]]),
                        ALU.mult, [ut, mask8], [ML])
                for g in range(32):
                    self.mm(pu[:, g * 16:(g + 1) * 16], ML[:, g, :, :].rearrange("p j c -> p (j c)"), Sel[:], True, True,
                            [ML, Sel], [pu], inc=(g == 31))
                self.cp(S.act, U[:, :, t * 16:(t + 1) * 16], pu[:, :].rearrange("p (g n) -> p g n", g=32), [pu], [U])
            S.barrier()
            p1.close()
            if os.environ.get("S5_STOP") == "regroup":
                return
            ph_outer = ph
            ph = ExitStack()
            nl_i = S.sb("nl_i", [128, NU], I32, ph)
            S.pool.op(lambda: nc.gpsimd.iota(nl_i[:, 0:512], pattern=[[1, 512]], base=1, channel_multiplier=0), (), [nl_i])
            S.pool.op(lambda: nc.gpsimd.iota(nl_i[:, 512:NU].rearrange("p (a b) -> p a b", a=4),
                                             pattern=[[0, 4], [1, 32]], base=1, channel_multiplier=0), (), [nl_i])
            nloc1 = S.sb("nloc1", [128, NU], F32, ph)
            self.cp(S.dve, nloc1[:], nl_i[:], [nl_i], [nloc1])
            rmask = S.sb("rmask", [128, NU], F32, ph)
            self.memset(S.pool, rmask[:], 1.0, [rmask])
            self.memset(S.pool, AP(rmask[:, 512:513], [[32, 4]]), 0.0, [rmask])
            cosT = S.sb("cosT", [128, NU], F32, ph)
            sinT = S.sb("sinT", [128, NU], F32, ph)
            angT = S.sb("angT", [128, NU], F32, ph)
            Rrow = S.sb("Rrow", [128, NU], F32, ph)
            sin_tmps = [(S.sb("sr_tot", [128, NU], F32, ph), S.sb("sr_nf", [128, NU], F32, ph),
                         S.sb("sr_ni", [128, NU], I32, ph)) for _ in range(2)]
            pvs = [[S.ps("pv", [128, 512], F32, ph), S.ps("pv2", [128, 128], F32, ph)] for _ in range(2)]
            vv = [S.sb("vv", [128, NU], F32, ph) for _ in range(2)]
            zz = [S.sb("zz", [128, NU], F32, ph) for _ in range(2)]
            GG = [S.sb("GG", [128, NU], F32, ph) for _ in range(2)]
            wk = [S.sb("wk5", [128, NU], F32, ph) for _ in range(2)]
            Sst = [[S.sb("Sst", [128, NU], BF16, ph) for _ in range(2)] for _ in range(2)]
            fin = S.sb("fin5", [128, 2, 4], F32, ph)
            pY = [S.ps("pY5", [128, 512], F32, ph), S.ps("pY5b", [128, 128], F32, ph)]
            cranges = [(0, 512, 0), (512, 128, 1)]

            def P4(t_, first, n_, step):
                return AP(t_[:, first:first + 1], [[32, 4], [step, n_]])

            for q in range(16):
                for dr in range(2):
                    col = dr * 16 + q
                    rev = (dr == 1)
                    self.ts(S.dve, angT[:], nloc1[:], TH8[:, col:col + 1], None, ALU.mult, None, [nloc1, TH8], [angT])
                    self.sin_of(cosT[:], cosT, angT[:], math.pi / 2.0, [128, NU], None, [angT], tmps=sin_tmps[0])
                    self.sin_of(sinT[:], sinT, angT[:], 0.0, [128, NU], None, [angT], tmps=sin_tmps[1])
                    self.actf(Rrow[:], rmask[:], AF.Copy, [rmask, R8], [Rrow], scale=R8[:, col:col + 1])
                    for ri, Vt in ((0, VRe[dr]), (1, VIm[dr])):
                        for (c0, cn, pi_) in cranges:
                            for m in range(2):
                                g = 2 * q + m
                                self.mm(pvs[ri][pi_][64 * m:64 * m + 64, 0:cn], Vt[:, g, :], U[:, g, c0:c0 + cn],
                                        True, True, [Vt, U], [pvs[ri][pi_]], inc=(m == 1))
                            self.cp(S.act, vv[ri][:, c0:c0 + cn], pvs[ri][pi_][:, 0:cn], [pvs[ri][pi_]], [vv[ri]])
                    if not rev:
                        parts = [(lambda t_: t_[:, 0:NU], lambda t_: t_[:, 0:NU])]
                    else:
                        parts = [(lambda t_: t_[:, 0:512], lambda t_: AP(t_[:, 511:512], [[-1, 512]])),
                                 (lambda t_: P4(t_, 512, 32, 1), lambda t_: P4(t_, 543, 32, -1))]
                    for (dv_, sv_) in parts:
                        self.tt(S.dve, dv_(zz[0]), sv_(vv[0]), dv_(cosT), ALU.mult, [vv[0], cosT], [zz[0]])
                        self.tt(S.dve, dv_(wk[0]), sv_(vv[1]), dv_(sinT), ALU.mult, [vv[1], sinT], [wk[0]])
                        self.tt(S.dve, dv_(zz[1]), sv_(vv[1]), dv_(cosT), ALU.mult, [vv[1], cosT], [zz[1]])
                        self.tt(S.dve, dv_(wk[1]), sv_(vv[0]), dv_(sinT), ALU.mult, [vv[0], sinT], [wk[1]])
                    self.tt(S.dve, zz[0][:], zz[0][:], wk[0][:], ALU.add, [zz[0], wk[0]], [zz[0]], nosync=True)
                    self.tt(S.dve, zz[1][:], zz[1][:], wk[1][:], ALU.subtract, [zz[1], wk[1]], [zz[1]], nosync=True)
                    for ri in range(2):
                        S.dve.op(lambda: nc.vector.tensor_tensor_scan(
                            out=GG[ri][:], data0=Rrow[:], data1=zz[ri][:], initial=H0[ri][:, col:col + 1],
                            op0=ALU.mult, op1=ALU.add), [Rrow, zz[ri], H0[ri]], [GG[ri]], nosync=True)
                    self.tt(S.dve, wk[0][:], GG[0][:], cosT[:], ALU.mult, [GG[0], cosT], [wk[0]], nosync=True)
                    self.tt(S.dve, wk[1][:], GG[1][:], sinT[:], ALU.mult, [GG[1], sinT], [wk[1]], nosync=True)
                    self.tt(S.dve, zz[0][:], GG[0][:], sinT[:], ALU.mult, [GG[0], sinT], [zz[0]], nosync=True)
                    self.tt(S.dve, zz[1][:], GG[1][:], cosT[:], ALU.mult, [GG[1], cosT], [zz[1]], nosync=True)
                    for ri, (A_, B_, op_) in enumerate(((wk[0], wk[1], ALU.subtract), (zz[0], zz[1], ALU.add))):
                        St_ = Sst[dr][ri]
                        if not rev:
                            self.tt(S.dve, St_[:, 1:NU], A_[:, 0:NU - 1], B_[:, 0:NU - 1], op_, [A_, B_], [St_],
                                    nosync=True)
                            self.cp(S.act, St_[:, 0:1], H0[ri][:, col:col + 1], [H0[ri]], [St_])
                            self.memset(S.pool, AP(St_[:, 512:513], [[32, 4]]), 0.0, [St_])
                        else:
                            self.tt(S.dve, St_[:, 0:511], AP(A_[:, 510:511], [[-1, 511]]), AP(B_[:, 510:511], [[-1, 511]]),
                                    op_, [A_, B_], [St_])
                            self.tt(S.dve, P4(St_, 512, 31, 1), P4(A_, 542, 31, -1), P4(B_, 542, 31, -1), op_,
                                    [A_, B_], [St_])
                            self.cp(S.act, St_[:, 511:512], H0[ri][:, col:col + 1], [H0[ri]], [St_])
                            self.memset(S.pool, AP(St_[:, 543:544], [[32, 4]]), 0.0, [St_])
                        self.tt(S.dve, fin[:, ri, :], AP(A_[:, 543:544], [[32, 4]]), AP(B_[:, 543:544], [[32, 4]]), op_,
                                [A_, B_], [fin])
                        self.cp(S.dve, AP(FIN[ri][:, 0, col:col + 1], [[32, 4]]), fin[:, ri, :], [fin], [FIN[ri]])
                for m in range(2):
                    g = 2 * q + m
                    hs = slice(64 * m, 64 * m + 64)
                    for (c0, cn, pi_) in cranges:
                        py = pY[pi_]
                        self.mm(py[:, 0:cn], Mg[:, g, :], U[:, g, c0:c0 + cn], True, False, [Mg, U], [py], inc=False)
                        k = 0
                        for dr in range(2):
                            for (Ct, ri) in ((CRe[dr], 0), (CIm[dr], 1)):
                                k += 1
                                self.mm(py[:, 0:cn], Ct[hs, q, :], Sst[dr][ri][hs, c0:c0 + cn], False, k == 4,
                                        [Ct, Sst[dr][ri]], [py], inc=(k == 4))
                        self.cp(S.act, Yall[:, g, c0:c0 + cn], py[:, 0:cn], [py], [Yall])
            S.barrier()
            ph.close()
            if os.environ.get("S5_STOP") == "states":
                return
            ph = ExitStack()
            fo = S.sb("fo5", [32, 2, 4, 128], F32, ph)
            for ri, nm in ((0, "nsr"), (1, "nsi")):
                for sq in range(4):
                    self.tr(pT4[0:32, sq * 128:(sq + 1) * 128], FIN[ri][:, sq, :], identf[:], [FIN[ri], identf], [pT4],
                            inc=(sq == 3))
                self.cp(S.act, fo[:, ri, :, :].rearrange("p s c -> p (s c)"), pT4[0:32, :], [pT4], [fo])
                for sq in range(4):
                    S.pool.dma(bass.AP(o[nm].tensor, sq * 4096, [[128, 32], [1, 128]]), fo[:, ri, sq, :], reads=[fo],
                               writes=[self.ob_[nm]])
            Wg = S.sb("Wglu", [128, 4, 512], BF16, ph)
            wgst = [S.sb("wgst", [128, 512], F32, ph) for _ in range(2)]
            self.load_w_bf16(Wg, i["ssm_glu_w"][0], 512, 4, wgst)
            gbias = S.sb("gbias", [128, 512], F32, ph)
            S.sp.dma(gbias[:], self.bc(i["ssm_glu_b"][0], 128, 512), writes=[gbias])
            YE = [S.sb("YE", [128, 32, 16, 8], BF16, ph) for _ in range(2)]
            pTok = [S.ps("pTok", [128, 512], F32, ph) for _ in range(2)]
            yt_ = [S.sb("yt5", [128, 512], F32, ph) for _ in range(2)]
            x2_ = [S.sb("x25", [128, 512], F32, ph) for _ in range(2)]
            gl_ = [S.sb("gl5", [128, 512], F32, ph) for _ in range(2)]
            glb_ = [S.sb("glb5", [128, 512], BF16, ph) for _ in range(2)]
            gT_ = [S.sb("gT5", [128, 4, 128], BF16, ph) for _ in range(2)]
            pG_ = [S.ps("pG5", [128, 512], F32, ph) for _ in range(2)]
            pTb_ = [S.ps("pTb5", [128, 512], BF16, ph) for _ in range(2)]
            ocb_ = [S.sb("ocb5", [128, 512], BF16, ph) for _ in range(2)]
            stg = [S.sb("stgC", [128, 4, 512], BF16, ph) for _ in range(2)]

            def tok_gen(t):
                a = t % 2
                ye, pt = YE[a], pTok[a]
                yt, x2, gl, glb, gT, pG, pTb, ocb = yt_[a], x2_[a], gl_[a], glb_[a], gT_[a], pG_[a], pTb_[a], ocb_[a]
                self.tt(S.dve, AP(ye[:, 0, 0, 0:1], [[128, 32], [8, 16], [1, 8]]),
                        AP(Yall[:, 0, t * 16:t * 16 + 1], [[NU, 32], [1, 16], [0, 8]]),
                        AP(maskI[:, 0:1], [[0, 32], [0, 16], [1, 8]]), ALU.mult, [Yall, maskI], [ye])
                yield
                for g in range(32):
                    self.mm(pt[:, g * 16:(g + 1) * 16], ye[:, g, :, :].rearrange("p n i -> p (n i)"), Esel[:], True, True,
                            [ye, Esel], [pt], inc=(g == 31))
                yield
                self.cp(S.act, yt[:], pt[:, :], [pt], [yt])
                yield
                self.tt(S.dve, x2[:], yt[:], yt[:], ALU.mult, [yt], [x2])
                self.ts(S.dve, x2[:], x2[:], 0.044715, 1.0, ALU.mult, ALU.add, [x2], [x2])
                self.tt(S.dve, x2[:], x2[:], yt[:], ALU.mult, [x2, yt], [x2], nosync=True)
                yield
                self.actf(x2[:], x2[:], AF.Sigmoid, [x2], [x2], scale=1.5957691216)
                yield
                self.tt(S.dve, gl[:], x2[:], yt[:], ALU.mult, [x2, yt], [gl])
                yield
                self.cp(S.act, glb[:], gl[:], [gl], [glb])
                yield
                for c in range(4):
                    self.tr(pTb[:, c * 128:(c + 1) * 128], glb[:, c * 128:(c + 1) * 128], self.ident[:],
                            [glb, self.ident], [pTb], inc=(c == 3))
                yield
                self.cp(S.act, gT[:].rearrange("p c t -> p (c t)"), pTb[:, :], [pTb], [gT])
                yield
                for c in range(4):
                    self.mm(pG[:, :], gT[:, c, :], Wg[:, c, :], c == 0, c == 3, [gT, Wg], [pG])
                yield
                self.tt(S.dve, x2[:], pG[:, :], gbias[:], ALU.add, [pG, gbias], [x2])
                yield
                self.actf(x2[:], x2[:], AF.Sigmoid, [x2], [x2])
                yield
                self.tt(S.dve, ocb[:], x2[:], gl[:], ALU.mult, [x2, gl], [ocb])
                yield
                sub, sup = t % 4, t // 4
                for c in range(4):
                    self.tr(pTb[:, c * 128:(c + 1) * 128], ocb[:, c * 128:(c + 1) * 128], self.ident[:],
                            [ocb, self.ident], [pTb], inc=(c == 3))
                yield
                stt_ = stg[sup % 2]
                self.cp(S.act, stt_[:, :, sub * 128:(sub + 1) * 128], pTb[:, :].rearrange("p (c t) -> p c t", c=4),
                        [pTb], [stt_])
                if sub == 3:
                    S.pool.dma(s["OCT"][:, :, sup * 512:(sup + 1) * 512], stt_[:], reads=[stt_], writes=[self.sb_["OCT"]])

            for t0 in range(0, NT, 2):
                alive = [tok_gen(t0), tok_gen(t0 + 1)]
                while alive:
                    for g_ in list(alive):
                        try:
                            next(g_)
                        except StopIteration:
                            alive.remove(g_)
            S.barrier()
            ph.close()


W_NAMES = ["ada_w", "ada_b", "norm1_g", "norm2_g", "ffn_w1", "ffn_w3", "ffn_w2", "ab_w_in", "ab_w_out",
           "a_q_norm", "a_k_norm", "a_sink", "ret_decay", "ret_norm", "cd_w_in", "cd_w_out", "ssm_lambda_re",
           "ssm_lambda_im", "ssm_log_dt", "ssm_b_re", "ssm_b_im", "ssm_c_re", "ssm_c_im", "ssm_d", "ssm_glu_w",
           "ssm_glu_b", "d_q_norm", "d_k_norm", "d_lambda", "d_subln"]


def core_inputs(inp, b):
    f = lambda a: np.ascontiguousarray(a, dtype=np.float32)
    m = {}
    m["x"] = f(np.concatenate([inp["x_sample"][b].reshape(TS, D), inp["x_prompt"][4 * b:4 * b + 4].reshape(TP, D)], 0))
    m["cond"] = f(np.stack([inp["c"][b], inp["c_ctx"]], 0))
    m["cache_k_a"] = f(inp["cache_k_a"][b, 0].reshape(256, 128))
    m["cache_v_a"] = f(inp["cache_v_a"][b, 0].reshape(256, 128))
    m["state_ret"] = f(inp["state_ret"][b, 0])
    m["state_ssm_re"] = f(inp["state_ssm_re"][b, 0])
    m["state_ssm_im"] = f(inp["state_ssm_im"][b, 0])
    m["cache_k_d"] = f(inp["cache_k_d"][b, 0].reshape(256, 512))
    m["cache_v_d"] = f(inp["cache_v_d"][b, 0].reshape(256, 512))
    for nm in W_NAMES:
        m[nm] = f(inp[nm])
    return m


_NC_CACHE = {}


def kernel(**inp):
    inp = {k: np.asarray(v) for k, v in inp.items()}
    if "nc" not in _NC_CACHE:
        _NC_CACHE["nc"] = Builder().nc
    nc = _NC_CACHE["nc"]
    in_maps = [core_inputs(inp, b) for b in range(NCORES)]
    res = run_bass_kernel_spmd(nc, in_maps, core_ids=list(range(NCORES))).results
    cat = lambda k: [np.asarray(r[k]) for r in res]
    y = cat("y")
    y_sample = np.stack([a[:TS] for a in y], 0).reshape(8, 4096, D)
    y_prompt = np.concatenate([a[TS:].reshape(4, 256, D) for a in y], 0)
    nka = np.concatenate([a.reshape(4, 1, 256, 2, 64) for a in cat("nka")], 0)
    nva = np.concatenate([a.reshape(4, 1, 256, 2, 64) for a in cat("nva")], 0)
    nret = np.concatenate([a.reshape(4, 1, 2, 8, 64, 64) for a in cat("nret")], 0)
    nsr = np.concatenate([a.reshape(4, 1, 2, 32, 64) for a in cat("nsr")], 0)
    nsi = np.concatenate([a.reshape(4, 1, 2, 32, 64) for a in cat("nsi")], 0)
    nkd = np.concatenate([a.reshape(4, 1, 256, 4, 2, 64) for a in cat("nkd")], 0)
    nvd = np.concatenate([a.reshape(4, 1, 256, 4, 128) for a in cat("nvd")], 0)
    f = lambda a: np.ascontiguousarray(a, dtype=np.float32)
    return tuple(f(a) for a in (y_prompt, y_sample, nka, nva, nret, nsr, nsi, nkd, nvd))
```

```python
import math
import os
import numpy as np
from contextlib import ExitStack
import concourse.bass as bass
import concourse.mybir as mybir
from concourse.bass_utils import run_bass_kernel_spmd

F32 = mybir.dt.float32
BF16 = mybir.dt.bfloat16
I32 = mybir.dt.int32
AF = mybir.ActivationFunctionType
ALU = mybir.AluOpType
AX = mybir.AxisListType

NCORES = 8
D = 1024
TS = 4096
TP = 1024
TOK = TS + TP
NT = TOK // 128
NTS = TS // 128
FF = 2816
EPS = 1e-6


class Ev:
    __slots__ = ("sem", "sid", "val")

    def __init__(self, sem, sid, val):
        self.sem, self.sid, self.val = sem, sid, val


class Buf:
    __slots__ = ("name", "w", "r", "excl")

    def __init__(self, name="", excl=False):
        self.name = name
        self.w = None
        self.r = {}
        self.excl = excl


def _bufs(xs):
    out = []
    for x in xs:
        if x is None:
            continue
        out.append(x.b if isinstance(x, T) else x)
    return out


class Eng:
    def __init__(self, S, e, name):
        self.S, self.e, self.name = S, e, name
        self.sem = S.new_sem("e_" + name)
        self.sid = id(self.sem)
        self.cnt = 0
        self.waited = {}
        self.pending = []
        self.dsems = []
        self.dvals = []
        self.dnext = 0

    def wait(self, ev):
        if ev is None or ev.sid == self.sid:
            return
        assert ev.val is not None, "waiting on unresolved pending event"
        if self.waited.get(ev.sid, 0) >= ev.val:
            return
        self.e.wait_ge(ev.sem, ev.val)
        self.waited[ev.sid] = ev.val

    def deps(self, reads, writes, nosync=False):
        for b in reads:
            w = b.w
            if w is not None and w.sid == self.sid and nosync:
                continue
            if w is not None and w.sid == self.sid and self.name != "pe":
                if self.waited.get(w.sid, 0) < w.val:
                    self.e.wait_ge(w.sem, w.val)
                    self.waited[w.sid] = w.val
            else:
                self.wait(w)
            if b.excl:
                for r in b.r.values():
                    self.wait(r)
        for b in writes:
            self.wait(b.w)
            for r in b.r.values():
                self.wait(r)

    def mark(self, ev, reads, writes):
        for b in reads:
            b.r[ev.sid] = ev
        for b in writes:
            b.w = ev
            b.r = {}

    def op(self, ins_fn, reads=(), writes=(), inc=True, selfsync=False, nosync=False):
        reads, writes = _bufs(reads), _bufs(writes)
        self.deps(reads, writes, nosync)
        ins = ins_fn()
        ev = Ev(self.sem, self.sid, None)
        self.pending.append(ev)
        self.mark(ev, reads, writes)
        if inc:
            self.cnt += 1
            ins.then_inc(self.sem, 1)
            for p in self.pending:
                p.val = self.cnt
            self.pending = []
            if selfsync:
                self.e.wait_ge(self.sem, self.cnt)
        return ins

    def dma(self, out, in_, reads=(), writes=(), **kw):
        reads, writes = _bufs(reads), _bufs(writes)
        S = self.S
        if not self.dsems:
            for i in range(S.dma_pool):
                self.dsems.append(S.new_sem("d_%s%d" % (self.name, i)))
                self.dvals.append(0)
        k = self.dnext
        self.dnext = (k + 1) % len(self.dsems)
        sem = self.dsems[k]
        if self.dvals[k] > 0:
            self.wait(Ev(sem, id(sem), self.dvals[k]))
        self.deps(reads, writes)
        self.dvals[k] += 16
        self.e.dma_start(out=out, in_=in_, **kw).then_inc(sem, 16)
        ev = Ev(sem, id(sem), self.dvals[k])
        self.mark(ev, reads, writes)
        return ev


class T:
    def __init__(self, t, name=""):
        self.t = t
        self.b = Buf(name)

    def __getitem__(self, k):
        return self.t[k]


class Sched:
    def __init__(self, nc, dma_pool=16):
        self.nc = nc
        self.stack = ExitStack()
        self.dma_pool = dma_pool
        self.pe = Eng(self, nc.tensor, "pe")
        self.act = Eng(self, nc.scalar, "act")
        self.dve = Eng(self, nc.vector, "dve")
        self.pool = Eng(self, nc.gpsimd, "pool")
        self.sp = Eng(self, nc.sync, "sp")
        self.engs = [self.pe, self.act, self.dve, self.pool, self.sp]
        self.uid = 0

    def new_sem(self, name):
        return self.stack.enter_context(self.nc.semaphore(name))

    def sb(self, name, shape, dtype, stack=None):
        self.uid += 1
        t = (stack or self.stack).enter_context(
            self.nc.sbuf_tensor("%s_%d" % (name, self.uid), list(shape), dtype))
        return T(t, name)

    def ps(self, name, shape, dtype=F32, stack=None):
        self.uid += 1
        t = (stack or self.stack).enter_context(
            self.nc.psum_tensor("%s_%d" % (name, self.uid), list(shape), dtype))
        tt_ = T(t, name)
        tt_.b.excl = True
        return tt_

    def all_events(self):
        evs = []
        for E in self.engs:
            assert not E.pending, "pending events on %s" % E.name
            if E.cnt:
                evs.append(Ev(E.sem, E.sid, E.cnt))
            for s, v in zip(E.dsems, E.dvals):
                if v:
                    evs.append(Ev(s, id(s), v))
        return evs

    def barrier(self):
        evs = self.all_events()
        for E in self.engs:
            for ev in evs:
                E.wait(ev)

    def finish(self):
        for ev in self.all_events():
            self.sp.wait(ev)


def AP(base, dims):
    return bass.AP(base.tensor, base.offset, [list(base.ap[0])] + [list(d) for d in dims])


class StopBuild(Exception):
    pass


class Builder:
    def __init__(self, debug=False, stop_after=None):
        self.debug = debug
        self.stop_after = stop_after
        self.nc = nc = bass.Bass("TRN2", target_bir_lowering=False)
        self.S = S = Sched(nc)
        self.dbg_names = []
        self.declare_io()
        self.setup_consts()
        done = False
        try:
            self.run_phases()
        except StopBuild:
            return
        S.finish()
        S.stack.close()

    def run_phases(self):
        S = self.S
        stop_after = self.stop_after
        for name, fn in [("ada", self.phase_ada), ("l0in", self.l0_in), ("l0a", self.l0_attn_a),
                         ("l0b", self.l0_ret_b), ("l0ffn", lambda: self.out_ffn(0)),
                         ("l1in", self.l1_in), ("l1d", self.l1_attn_d), ("l1c", self.l1_s5_c),
                         ("l1ffn", lambda: self.out_ffn(1))]:
            if os.environ.get("S5_ONLY") and name != "l1c":
                continue
            fn()
            S.barrier()
            if stop_after == name:
                break

    def din(self, name, shape, dtype=F32):
        return self.nc.dram_tensor(name, list(shape), dtype, kind="ExternalInput").ap()

    def dout(self, name, shape, dtype=F32):
        return self.nc.dram_tensor(name, list(shape), dtype, kind="ExternalOutput").ap()

    def scr(self, name, shape, dtype=BF16):
        if self.debug:
            self.dbg_names.append(name)
            return self.nc.dram_tensor(name, list(shape), dtype, kind="ExternalOutput").ap()
        return self.nc.dram_tensor(name, list(shape), dtype).ap()

    def declare_io(self):
        i = self.i = {}
        i["x"] = self.din("x", [TOK, D])
        i["cond"] = self.din("cond", [2, D])
        i["cache_k_a"] = self.din("cache_k_a", [256, 128])
        i["cache_v_a"] = self.din("cache_v_a", [256, 128])
        i["state_ret"] = self.din("state_ret", [2, 8, 64, 64])
        i["state_ssm_re"] = self.din("state_ssm_re", [2, 32, 64])
        i["state_ssm_im"] = self.din("state_ssm_im", [2, 32, 64])
        i["cache_k_d"] = self.din("cache_k_d", [256, 512])
        i["cache_v_d"] = self.din("cache_v_d", [256, 512])
        for nm, shp in [("ada_w", [2, D, 6 * D]), ("ada_b", [2, 6 * D]), ("norm1_g", [2, D]), ("norm2_g", [2, D]),
                        ("ffn_w1", [2, D, FF]), ("ffn_w3", [2, D, FF]), ("ffn_w2", [2, FF, D]),
                        ("ab_w_in", [1, D, 2816]), ("ab_w_out", [1, D, D]), ("a_q_norm", [1, 64]),
                        ("a_k_norm", [1, 64]), ("a_sink", [1, 8]), ("ret_decay", [1, 2, 8]), ("ret_norm", [1, 64]),
                        ("cd_w_in", [1, D, 2048]), ("cd_w_out", [1, D, D]),
                        ("ssm_lambda_re", [1, 2, 32, 64]), ("ssm_lambda_im", [1, 2, 32, 64]),
                        ("ssm_log_dt", [1, 2, 32]), ("ssm_b_re", [1, 32, 64, 16]), ("ssm_b_im", [1, 32, 64, 16]),
                        ("ssm_c_re", [1, 32, 16, 64]), ("ssm_c_im", [1, 32, 16, 64]), ("ssm_d", [1, 512]),
                        ("ssm_glu_w", [1, 512, 512]), ("ssm_glu_b", [1, 512]), ("d_q_norm", [1, 64]),
                        ("d_k_norm", [1, 64]), ("d_lambda", [1, 4, 64]), ("d_subln", [1, 128])]:
            i[nm] = self.din(nm, shp)
        o = self.o = {}
        o["y"] = self.dout("y", [TOK, D])
        o["nka"] = self.dout("nka", [TP, 128])
        o["nva"] = self.dout("nva", [TP, 128])
        o["nret"] = self.dout("nret", [4, 2, 8, 64, 64])
        o["nsr"] = self.dout("nsr", [4, 2, 32, 64])
        o["nsi"] = self.dout("nsi", [4, 2, 32, 64])
        o["nkd"] = self.dout("nkd", [TP, 512])
        o["nvd"] = self.dout("nvd", [TP, 512])
        s = self.s = {}
        s["mod"] = self.scr("mod", [2, 2, 6 * D], F32)
        s["QAT"] = self.scr("QAT", [64, 8, TOK])
        s["KAT"] = self.scr("KAT", [64, 2, TOK])
        s["VA"] = self.scr("VA", [TOK, 128])
        s["QBT"] = self.scr("QBT", [64, 8, TOK])
        s["KBT"] = self.scr("KBT", [64, 8, TOK])
        s["KB"] = self.scr("KB", [TOK, 512])
        s["VB"] = self.scr("VB", [TOK, 512])
        s["GB"] = self.scr("GB", [TOK, 512])
        s["OAT"] = self.scr("OAT", [64, 8, TOK])
        s["OBT"] = self.scr("OBT", [128, 4, TOK])
        s["UTM"] = self.scr("UTM", [TOK, 512])
        s["QDT"] = self.scr("QDT", [64, 8, TOK])
        s["KDT"] = self.scr("KDT", [64, 8, TOK])
        s["VD"] = self.scr("VD", [TOK, 512])
        s["OCT"] = self.scr("OCT", [128, 4, TOK])
        s["ODT"] = self.scr("ODT", [128, 4, TOK])
        s["X1"] = self.scr("X1", [TOK, D], F32)
        s["X2"] = self.scr("X2", [TOK, D], F32)
        self.sb_ = {k: Buf(k) for k in s}
        self.ob_ = {k: Buf(k) for k in o}

    def tt(self, E, out, in0, in1, op, reads, writes, nosync=False):
        e = E.e
        return E.op(lambda: e.tensor_tensor(out=out, in0=in0, in1=in1, op=op), reads, writes, nosync=nosync)

    def ts(self, E, out, in0, s1, s2, op0, op1, reads, writes, accum_out=None):
        e = E.e
        if accum_out is not None:
            return E.op(lambda: e.tensor_scalar(out=out, in0=in0, scalar1=s1, scalar2=s2, op0=op0, op1=op1,
                                                accum_out=accum_out), reads, writes, selfsync=True)
        if op1 is None:
            return E.op(lambda: e.tensor_scalar(out=out, in0=in0, scalar1=s1, scalar2=None, op0=op0), reads, writes)
        return E.op(lambda: e.tensor_scalar(out=out, in0=in0, scalar1=s1, scalar2=s2, op0=op0, op1=op1),
                    reads, writes)

    def stt(self, out, in0, scalar, in1, op0, op1, reads, writes, accum_out=None):
        nc = self.nc
        if accum_out is not None:
            return self.S.dve.op(lambda: nc.vector.scalar_tensor_tensor(out=out, in0=in0, scalar=scalar, in1=in1,
                                                                        op0=op0, op1=op1, accum_out=accum_out),
                                 reads, writes, selfsync=True)
        return self.S.dve.op(lambda: nc.vector.scalar_tensor_tensor(out=out, in0=in0, scalar=scalar, in1=in1,
                                                                    op0=op0, op1=op1), reads, writes)

    def stt_any(self, E, out, in0, scalar, in1, op0, op1, reads, writes):
        if E is self.S.dve:
            return self.stt(out, in0, scalar, in1, op0, op1, reads, writes)
        self.ts(E, out, in0, scalar, None, op0, None, reads, writes)
        return self.tt(E, out, out, in1, op1, list(reads) + list(writes), writes)

    def cp(self, E, out, in_, reads, writes, nosync=False):
        e = E.e
        if E is self.S.act:
            return E.op(lambda: e.copy(out=out, in_=in_), reads, writes, nosync=nosync)
        return E.op(lambda: e.tensor_copy(out=out, in_=in_), reads, writes, nosync=nosync)

    def actf(self, out, in_, func, reads, writes, scale=1.0, bias=None):
        nc = self.nc
        if bias is None:
            return self.S.act.op(lambda: nc.scalar.activation(out=out, in_=in_, func=func, scale=scale),
                                 reads, writes)
        return self.S.act.op(lambda: nc.scalar.activation(out=out, in_=in_, func=func, scale=scale, bias=bias),
                             reads, writes)

    def mm(self, out, lhsT, rhs, start, stop, reads, writes, inc=None):
        nc = self.nc
        if inc is None:
            inc = stop
        return self.S.pe.op(lambda: nc.tensor.matmul(out, lhsT=lhsT, rhs=rhs, start=start, stop=stop),
                            reads, writes, inc=inc)

    def tr(self, out, in_, ident, reads, writes, inc=True):
        nc = self.nc
        return self.S.pe.op(lambda: nc.tensor.transpose(out, in_, ident), reads, writes, inc=inc)

    def memset(self, E, ap, val, writes):
        e = E.e
        return E.op(lambda: e.memset(ap, val), (), writes)

    def bcast_load(self, dst, src_vec_ap, n):
        src = bass.AP(src_vec_ap.tensor, src_vec_ap.offset, [[0, 128], [1, n]])
        return src

    def setup_consts(self):
        S, nc = self.S, self.nc
        self.ident = S.sb("ident", [128, 128], BF16)
        self.memset(S.pool, self.ident[:], 0.0, [self.ident])
        S.pool.op(lambda: nc.gpsimd.affine_select(out=self.ident[:], in_=self.ident[:], compare_op=ALU.not_equal,
                                                  fill=1.0, base=0, pattern=[[-1, 128]], channel_multiplier=1),
                  [self.ident], [self.ident])
        self.ones_bf = S.sb("ones_bf", [128, 128], BF16)
        self.memset(S.pool, self.ones_bf[:], 1.0, [self.ones_bf])
        self.mhalf = S.sb("mhalf", [128, 8], F32)
        self.memset(S.pool, self.mhalf[:], -0.5, [self.mhalf])
        self._shiftb = S.sb("shiftb", [128, 1], F32)
        self.memset(S.pool, self._shiftb[:], (math.pi / 2.0) * (1.0 - 1e-5), [self._shiftb])
        self.s["ROPE"] = self.scr("ROPE", [2, 128, 32, 32], F32)
        self.sb_["ROPE"] = Buf("ROPE")
        self.build_rope()

    def sin_of(self, out, outT, ang, shift, shape, ph, reads, tmps=None):
        S = self.S
        two_pi = 2.0 * math.pi
        if tmps is not None:
            tot, nf, ni = tmps
        else:
            tot = S.sb("sr_tot", shape, F32, ph)
            nf = S.sb("sr_nf", shape, F32, ph)
            ni = S.sb("sr_ni", shape, I32, ph)
        SC = 1.0 - 1e-5
        big = int(np.prod(shape[1:])) >= 256
        self.ts(S.dve, nf[:], ang, shift, 1.0 / two_pi, ALU.add, ALU.mult, reads, [nf])
        self.cp(S.dve, ni[:], nf[:], [nf], [ni], nosync=big)
        self.cp(S.dve, nf[:], ni[:], [ni], [nf], nosync=big)
        self.stt(tot[:], nf[:], -two_pi, ang, ALU.mult, ALU.add, [nf] + list(reads), [tot])
        if shift == 0.0:
            self.actf(out, tot[:], AF.Sin, [tot], [outT], scale=SC)
        else:
            self.actf(out, tot[:], AF.Sin, [tot, self._shiftb], [outT], scale=SC, bias=self._shiftb[:])

    def build_rope(self):
        S, nc = self.S, self.nc
        with ExitStack() as ph:
            rowi = S.sb("rowi", [128, 32], I32, ph)
            coli = S.sb("coli", [128, 1], I32, ph)
            for half in range(2):
                sl = slice(64 * half, 64 * half + 64)
                S.pool.op(lambda: nc.gpsimd.iota(rowi[sl, :], pattern=[[2, 32]], base=half, channel_multiplier=0),
                          (), [rowi])
                S.pool.op(lambda: nc.gpsimd.iota(coli[sl, :], pattern=[[1, 1]], base=0, channel_multiplier=1),
                          (), [coli])
            rowf = S.sb("rowf", [128, 32], F32, ph)
            colf = S.sb("colf", [128, 1], F32, ph)
            self.cp(S.dve, rowf[:], rowi[:], [rowi], [rowf])
            self.cp(S.dve, colf[:], coli[:], [coli], [colf])
            ang = S.sb("ang", [128, 32, 32], F32, ph)
            for f in range(16):
                inv = 10000.0 ** (-f / 16.0)
                self.ts(S.dve, ang[:, :, f], rowf[:], inv, None, ALU.mult, None, [rowf], [ang])
                self.ts(S.dve, ang[:, :, 16 + f], AP(colf[:], [[0, 32]]), inv, None, ALU.mult, None, [colf], [ang])
            tab = S.sb("ropetab", [128, 32, 32], F32, ph)
            for i, shift in enumerate((math.pi / 2.0, 0.0)):
                self.sin_of(tab[:], tab, ang[:], shift, [128, 32, 32], ph, [ang])
                S.sp.dma(self.s["ROPE"][i], tab[:], reads=[tab], writes=[self.sb_["ROPE"]])
            S.barrier()

    def load_rope(self, dst_cos, dst_sin, tile):
        S = self.S
        S.sp.dma(dst_cos[:], self.s["ROPE"][0, :, tile, :], reads=[self.sb_["ROPE"]], writes=[dst_cos])
        S.sp.dma(dst_sin[:], self.s["ROPE"][1, :, tile, :], reads=[self.sb_["ROPE"]], writes=[dst_sin])

    def load_bcast(self, dst, vec_ap, n, E=None):
        E = E or self.S.sp
        src = bass.AP(vec_ap.tensor, vec_ap.offset, [[0, dst.t.shape[0]], [1, n]])
        return src

    def bc(self, vec_ap, parts, n):
        return bass.AP(vec_ap.tensor, vec_ap.offset, [[0, parts], [1, n]])

    def load_w_steps(self, dst, src_rows_ap, ncols, kchunks, stage, rows=128, engs=None):
        S = self.S
        engs = engs or [S.act, S.dve]
        steps = []
        for k in range(kchunks):
            def step(k=k):
                st = stage[k % len(stage)]
                S.sp.dma(st[0:rows, 0:ncols], src_rows_ap[k * rows:(k + 1) * rows, :], writes=[st])
                self.cp(engs[k % len(engs)], dst[0:rows, k, 0:ncols], st[0:rows, 0:ncols], [st], [dst])
            steps.append(step)
        return steps

    def load_w_bf16(self, dst, src_rows_ap, ncols, kchunks, stage, rows=128, engs=None):
        for st in self.load_w_steps(dst, src_rows_ap, ncols, kchunks, stage, rows, engs):
            st()

    def phase_ada(self):
        S, nc, i, s = self.S, self.nc, self.i, self.s
        with ExitStack() as ph:
            ct = S.sb("ct", [128, 2, 8], F32, ph)
            for c in range(2):
                for k in range(8):
                    src = bass.AP(i["cond"].tensor, c * D + k * 128, [[1, 128], [1, 1]])
                    S.sp.dma(ct[:, c, k:k + 1], src, writes=[ct])
            cs = S.sb("cs", [128, 8, 2], BF16, ph)
            self.actf(cs[:].rearrange("p k c -> p c k"), ct[:], AF.Silu, [ct], [cs])
            wts = [S.sb("adaw", [128, 8, 1536], BF16, ph) for _ in range(2)]
            wstage = [S.sb("adast", [128, 1536], F32, ph) for _ in range(3)]
            pss = [S.ps("adaps", [2, 512], F32, ph) for _ in range(2)]
            bias = S.sb("adab", [2, 6 * D], F32, ph)
            msb = S.sb("adam", [2, 6 * D], F32, ph)
            q = 0
            for l in range(2):
                S.sp.dma(bias[:], self.bc(i["ada_b"][l], 2, 6 * D), writes=[bias])
                for part in range(4):
                    wt = wts[q % 2]
                    q += 1
                    self.load_w_bf16(wt, i["ada_w"][l][:, part * 1536:(part + 1) * 1536], 1536, 8, wstage,
                                     engs=[S.act, S.dve])
                    for cb in range(3):
                        pst = pss[cb % 2]
                        for k in range(8):
                            self.mm(pst[:, :], cs[:, k, :], wt[:, k, cb * 512:(cb + 1) * 512], k == 0, k == 7,
                                    [cs, wt], [pst])
                        c0 = part * 1536 + cb * 512
                        self.tt(S.dve, msb[:, c0:c0 + 512], pst[:, :], bias[:, c0:c0 + 512], ALU.add,
                                [pst, bias], [msb])
                S.sp.dma(s["mod"][l], msb[:], reads=[msb], writes=[self.sb_["mod"]])

    def load_mod(self, dst, l, cond, which):
        self.S.sp.dma(dst[:], self.bc(self.s["mod"][l, cond, which * D:(which + 1) * D], 128, D),
                      reads=[self.sb_["mod"]], writes=[dst])

    def make_norm_scale(self, G, SH, l, cond, norm_g_ap, which_sc, which_sh, tmp):
        S = self.S
        self.load_mod(tmp, l, cond, which_sc)
        S.sp.dma(G[:], self.bc(norm_g_ap, 128, D), writes=[G])
        self.stt(G[:], tmp[:], 1.0, G[:], ALU.add, ALU.mult, [tmp, G], [G])
        self.load_mod(SH, l, cond, which_sh)

    def rms_rstd(self, rstd, ss, n, width):
        S, nc = self.S, self.nc
        self.ts(S.dve, ss[:, 0:width], ss[:, 0:width], 1.0 / n, EPS, ALU.mult, ALU.add, [ss], [ss])
        S.pool.op(lambda: nc.gpsimd.tensor_tensor(out=rstd[:, 0:width], in0=ss[:, 0:width],
                                                  in1=self.mhalf[:, 0:width], op=ALU.pow),
                  [ss, self.mhalf], [rstd])

    def norm_tile(self, xt, G, SH, hf, hb, ss, rstd):
        S = self.S
        self.stt(hf[:], xt[:], 1.0, xt[:], ALU.mult, ALU.mult, [xt], [hf, ss], accum_out=ss[:, 0:1])
        self.rms_rstd(rstd, ss, float(D), 1)
        self.stt(hf[:], xt[:], rstd[:, 0:1], G[:], ALU.mult, ALU.mult, [xt, rstd, G], [hf])
        self.tt(S.dve, hb[:], hf[:], SH[:], ALU.add, [hf, SH], [hb])

    def norm_tile_g(self, xt, G, SH, hf, hb, ss, rstd):
        S, nc = self.S, self.nc
        self.stt(hf[:], xt[:], 1.0, xt[:], ALU.mult, ALU.mult, [xt], [hf, ss], accum_out=ss[:, 0:1])
        self.ts(S.dve, ss[:, 0:1], ss[:, 0:1], 1.0 / float(D), EPS, ALU.mult, ALU.add, [ss], [ss])
        yield
        S.pool.op(lambda: nc.gpsimd.tensor_tensor(out=rstd[:, 0:1], in0=ss[:, 0:1], in1=self.mhalf[:, 0:1],
                                                  op=ALU.pow), [ss, self.mhalf], [rstd])
        yield
        self.stt(hf[:], xt[:], rstd[:, 0:1], G[:], ALU.mult, ALU.mult, [xt, rstd, G], [hf])
        self.tt(S.dve, hb[:], hf[:], SH[:], ALU.add, [hf, SH], [hb], nosync=True)

    def head_norm_g(self, pb, H, gvec, out_f, sqt, ss, rstd):
        S, nc = self.S, self.nc
        W = H * 64
        self.actf(sqt[:, 0:W], pb[:, 0:W], AF.Square, [pb], [sqt])
        yield
        S.dve.op(lambda: nc.vector.tensor_reduce(out=ss[:, 0:H], in_=sqt[:, 0:W].rearrange("p (h d) -> p h d", h=H),
                                                 axis=AX.X, op=ALU.add), [sqt], [ss])
        self.ts(S.dve, ss[:, 0:H], ss[:, 0:H], 1.0 / 64.0, EPS, ALU.mult, ALU.add, [ss], [ss])
        yield
        S.pool.op(lambda: nc.gpsimd.tensor_tensor(out=rstd[:, 0:H], in0=ss[:, 0:H], in1=self.mhalf[:, 0:H],
                                                  op=ALU.pow), [ss, self.mhalf], [rstd])
        yield
        o3 = out_f[:, 0:W].rearrange("p (h d) -> p h d", h=H)
        self.tt(S.dve, o3, pb[:, 0:W].rearrange("p (h d) -> p h d", h=H), AP(rstd[:, 0:H], [[1, H], [0, 64]]),
                ALU.mult, [pb, rstd], [out_f])
        self.tt(S.dve, o3, o3, AP(gvec[:], [[0, H], [1, 64]]), ALU.mult, [out_f, gvec], [out_f], nosync=(H >= 8))

    def transpose_to(self, pT, hT_dst_fn, hb, nchunks, evac):
        for c in range(nchunks):
            self.tr(pT[:, c * 128:(c + 1) * 128], hb[:, c * 128:(c + 1) * 128], self.ident[:], [hb, self.ident],
                    [pT], inc=(c == nchunks - 1))
        evac()

    def head_norm(self, pb, H, gvec, out_f, sqt, ss, rstd):
        S, nc = self.S, self.nc
        W = H * 64
        self.actf(sqt[:, 0:W], pb[:, 0:W], AF.Square, [pb], [sqt])
        S.dve.op(lambda: nc.vector.tensor_reduce(out=ss[:, 0:H], in_=sqt[:, 0:W].rearrange("p (h d) -> p h d", h=H),
                                                 axis=AX.X, op=ALU.add), [sqt], [ss])
        self.rms_rstd(rstd, ss, 64.0, H)
        o3 = out_f[:, 0:W].rearrange("p (h d) -> p h d", h=H)
        self.tt(S.dve, o3, pb[:, 0:W].rearrange("p (h d) -> p h d", h=H), AP(rstd[:, 0:H], [[1, H], [0, 64]]),
                ALU.mult, [pb, rstd], [out_f])
        self.tt(S.dve, o3, o3, AP(gvec[:], [[0, H], [1, 64]]), ALU.mult, [out_f, gvec], [out_f])

    def rope_apply(self, out_b, x_f, H, cos, sin, t1, t2):
        S = self.S
        x3 = x_f[:, 0:H * 64].rearrange("p (h d) -> p h d", h=H)
        o3 = out_b[:, 0:H * 64].rearrange("p (h d) -> p h d", h=H)
        a3 = t1[:, 0:H * 32].rearrange("p (h d) -> p h d", h=H)
        b3 = t2[:, 0:H * 32].rearrange("p (h d) -> p h d", h=H)
        cb = AP(cos[:], [[0, H], [1, 32]])
        sb = AP(sin[:], [[0, H], [1, 32]])
        x1, x2 = x3[:, :, 0:32], x3[:, :, 32:64]
        self.tt(S.dve, a3, x1, cb, ALU.mult, [x_f, cos], [t1], nosync=(H >= 8))
        self.tt(S.dve, b3, x2, sb, ALU.mult, [x_f, sin], [t2], nosync=(H >= 8))
        self.tt(S.dve, o3[:, :, 0:32], a3, b3, ALU.subtract, [t1, t2], [out_b], nosync=(H >= 8))
        self.tt(S.dve, a3, x2, cb, ALU.mult, [x_f, cos], [t1], nosync=(H >= 8))
        self.tt(S.dve, b3, x1, sb, ALU.mult, [x_f, sin], [t2], nosync=(H >= 8))
        self.tt(S.dve, o3[:, :, 32:64], a3, b3, ALU.add, [t1, t2], [out_b], nosync=(H >= 8))

    def l0_in(self):
        S, nc, i, s, o = self.S, self.nc, self.i, self.s, self.o
        with ExitStack() as ph:
            W = S.sb("win", [128, 8, 2816], BF16, ph)
            wstage = [S.sb("winst", [128, 2816], F32, ph) for _ in range(2)]
            self.load_w_bf16(W, i["ab_w_in"][0], 2816, 8, wstage)
            G = S.sb("G", [128, D], F32, ph)
            SH = S.sb("SH", [128, D], F32, ph)
            qg = S.sb("qg", [128, 64], F32, ph)
            kg = S.sb("kg", [128, 64], F32, ph)
            S.sp.dma(qg[:], self.bc(i["a_q_norm"][0], 128, 64), writes=[qg])
            S.sp.dma(kg[:], self.bc(i["a_k_norm"][0], 128, 64), writes=[kg])
            xts = [S.sb("xt", [128, D], F32, ph) for _ in range(2)]
            hf = [S.sb("hf", [128, D], F32, ph) for _ in range(2)]
            hbs = [S.sb("hb", [128, D], BF16, ph) for _ in range(2)]
            hTs = [S.sb("hT", [128, 8, 128], BF16, ph) for _ in range(2)]
            ss = [S.sb("ss", [128, 8], F32, ph) for _ in range(2)]
            rstd = [S.sb("rstd", [128, 8], F32, ph) for _ in range(2)]
            ssh = [S.sb("ssh", [128, 8], F32, ph) for _ in range(2)]
            rsh = [S.sb("rsh", [128, 8], F32, ph) for _ in range(2)]
            sqt = [S.sb("sqt", [128, 512], F32, ph) for _ in range(2)]
            qf = [S.sb("qf", [128, 512], F32, ph) for _ in range(2)]
            kf = [S.sb("kf", [128, 128], F32, ph) for _ in range(2)]
            vf = [S.sb("vf", [128, 128], F32, ph) for _ in range(2)]
            t1 = [S.sb("t1", [128, 256], F32, ph) for _ in range(2)]
            t2 = [S.sb("t2", [128, 256], F32, ph) for _ in range(2)]
            cos = [S.sb("cos", [128, 32], F32, ph) for _ in range(2)]
            sin = [S.sb("sin", [128, 32], F32, ph) for _ in range(2)]
            qab = [S.sb("qab", [128, 512], BF16, ph) for _ in range(2)]
            kab = [S.sb("kab", [128, 128], BF16, ph) for _ in range(2)]
            vab = [S.sb("vab", [128, 128], BF16, ph) for _ in range(2)]
            qbb = [S.sb("qbb", [128, 512], BF16, ph) for _ in range(2)]
            kbb = [S.sb("kbb", [128, 512], BF16, ph) for _ in range(2)]
            vbb = [S.sb("vbb", [128, 512], BF16, ph) for _ in range(2)]
            gbb = [S.sb("gbb", [128, 512], BF16, ph) for _ in range(2)]
            stQA = [S.sb("stQA", [64, 8, 512], BF16, ph) for _ in range(2)]
            stKA = [S.sb("stKA", [64, 2, 512], BF16, ph) for _ in range(2)]
            stQB = [S.sb("stQB", [64, 8, 512], BF16, ph) for _ in range(2)]
            stKB = [S.sb("stKB", [64, 8, 512], BF16, ph) for _ in range(2)]
            pT = [S.ps("pT", [128, 1024], BF16, ph) for _ in range(2)]
            pbs = [S.ps("pb", [128, 512], F32, ph) for _ in range(4)]
            pOs = [S.ps("pO", [128, 1024], BF16, ph) for _ in range(2)]
            tmpm = S.sb("tmpm", [128, D], F32, ph)
            cnt = {"pb": 0, "po": 0, "cond": -1}

            def tile_gen(t):
                p = t % 2
                cond = 0 if t < NTS else 1
                sub, sup, r0 = t % 4, t // 4, t * 128
                xt, hb, hT = xts[p], hbs[p], hTs[p]
                if cond != cnt["cond"]:
                    self.make_norm_scale(G, SH, 0, cond, i["norm1_g"][0], 1, 0, tmpm)
                    cnt["cond"] = cond
                S.sp.dma(xt[:], i["x"][r0:r0 + 128, :], writes=[xt])
                if cond == 0:
                    self.load_rope(cos[p], sin[p], t)
                yield from self.norm_tile_g(xt, G, SH, hf[p], hb, ss[p], rstd[p])
                yield
                self.transpose_to(pT[p], None, hb, 8,
                                  lambda: self.cp(S.act, hT[:].rearrange("p k t -> p (k t)"), pT[p][:, :], [pT[p]], [hT]))
                yield

                def proj(c0, ncols):
                    pb = pbs[cnt["pb"] % 4]
                    cnt["pb"] += 1
                    for k in range(8):
                        self.mm(pb[:, 0:ncols], hT[:, k, :], W[:, k, c0:c0 + ncols], k == 0, k == 7, [hT, W], [pb])
                    return pb

                def to_featmajor(src_b, H, stage):
                    pO = pOs[cnt["po"] % 2]
                    cnt["po"] += 1
                    for h in range(H):
                        self.tr(pO[0:64, h * 128:(h + 1) * 128], src_b[:, h * 64:(h + 1) * 64], self.ident[:],
                                [src_b, self.ident], [pO], inc=(h == H - 1))
                    yield
                    self.cp(S.act, stage[:, :, sub * 128:(sub + 1) * 128],
                            pO[0:64, 0:H * 128].rearrange("p (h t) -> p h t", h=H), [pO], [stage])

                pb = proj(0, 512)
                yield
                yield from self.head_norm_g(pb, 8, qg, qf[p], sqt[p], ssh[p], rsh[p])
                yield
                qa = qab[p]
                if cond == 0:
                    self.rope_apply(qa, qf[p], 8, cos[p], sin[p], t1[p], t2[p])
                else:
                    self.cp(S.dve, qa[:], qf[p][:], [qf[p]], [qa])
                yield
                yield from to_featmajor(qa, 8, stQA[sup % 2])
                pb = proj(512, 256)
                yield
                yield from self.head_norm_g(pb, 2, kg, kf[p], sqt[p], ssh[p], rsh[p])
                yield
                ka = kab[p]
                va = vab[p]
                if cond == 0:
                    self.rope_apply(ka, kf[p], 2, cos[p], sin[p], t1[p], t2[p])
                    self.cp(S.act, va[:], pb[:, 128:256], [pb], [va])
                else:
                    self.cp(S.dve, ka[:], kf[p][:], [kf[p]], [ka])
                    self.cp(S.act, vf[p][:], pb[:, 128:256], [pb], [vf[p]])
                    self.cp(S.dve, va[:], vf[p][:], [vf[p]], [va])
                    pr = r0 - TS
                    S.pool.dma(o["nka"][pr:pr + 128, :], kf[p][:], reads=[kf[p]], writes=[self.ob_["nka"]])
                    S.pool.dma(o["nva"][pr:pr + 128, :], vf[p][:], reads=[vf[p]], writes=[self.ob_["nva"]])
                yield
                yield from to_featmajor(ka, 2, stKA[sup % 2])
                S.pool.dma(s["VA"][r0:r0 + 128, :], va[:], reads=[va], writes=[self.sb_["VA"]])
                pb = proj(768, 512)
                yield
                qb = qbb[p]
                self.actf(qb[:], pb[:, :], AF.Copy, [pb], [qb], scale=0.125)
                yield from to_featmajor(qb, 8, stQB[sup % 2])
                pb = proj(1280, 512)
                yield
                kb = kbb[p]
                self.cp(S.act, kb[:], pb[:, :], [pb], [kb])
                yield from to_featmajor(kb, 8, stKB[sup % 2])
                S.pool.dma(s["KB"][r0:r0 + 128, :], kb[:], reads=[kb], writes=[self.sb_["KB"]])
                pb = proj(1792, 512)
                yield
                vb = vbb[p]
                self.cp(S.dve, vb[:], pb[:, :], [pb], [vb])
                S.pool.dma(s["VB"][r0:r0 + 128, :], vb[:], reads=[vb], writes=[self.sb_["VB"]])
                pb = proj(2304, 512)
                yield
                gb = gbb[p]
                self.actf(gb[:], pb[:, :], AF.Silu, [pb], [gb])
                S.pool.dma(s["GB"][r0:r0 + 128, :], gb[:], reads=[gb], writes=[self.sb_["GB"]])
                if sub == 3:
                    c0 = sup * 512
                    for nm, st in (("QAT", stQA), ("KAT", stKA), ("QBT", stQB), ("KBT", stKB)):
                        S.act.dma(s[nm][:, :, c0:c0 + 512], st[sup % 2][:], reads=[st[sup % 2]],
                                  writes=[self.sb_[nm]])

            for t0 in range(0, NT, 2):
                alive = [tile_gen(t0), tile_gen(t0 + 1)]
                while alive:
                    for g_ in list(alive):
                        try:
                            next(g_)
                        except StopIteration:
                            alive.remove(g_)

    def l0_attn_a(self):
        S, nc, i, s, o = self.S, self.nc, self.i, self.s, self.o
        NEG = -30000.0
        with ExitStack() as ph:
            esk = S.sb("esk", [64, 8], F32, ph)
            S.sp.dma(esk[:], self.bc(i["a_sink"][0], 64, 8), writes=[esk])
            self.actf(esk[:], esk[:], AF.Exp, [esk], [esk])
            mprev = S.sb("mprev", [128, 4, 128], BF16, ph)
            mnext = S.sb("mnext", [128, 4, 128], BF16, ph)
            for m, pat, cm in ((mprev, [[0, 4], [-1, 128]], 1), (mnext, [[0, 4], [1, 128]], -1)):
                self.memset(S.pool, m[:], 0.0, [m])
                S.pool.op(lambda: nc.gpsimd.affine_select(out=m[:], in_=m[:], compare_op=ALU.is_ge, fill=NEG,
                                                          base=0, pattern=pat, channel_multiplier=cm), [m], [m])
            kc = S.sb("kc", [128, 2, 128], BF16, ph)
            vc = S.sb("vc", [128, 2, 128], BF16, ph)
            S.pool.dma(kc[:], i["cache_k_a"].rearrange("(n p) d -> p n d", p=128), writes=[kc])
            S.pool.dma(vc[:], i["cache_v_a"].rearrange("(n p) d -> p n d", p=128), writes=[vc])
            pO = S.ps("pOc", [128, 1024], BF16, ph)
            kcT = S.sb("kcT", [64, 2, 256], BF16, ph)
            for g in range(2):
                for n in range(2):
                    self.tr(pO[0:64, (g * 2 + n) * 128:(g * 2 + n + 1) * 128], kc[:, n, g * 64:(g + 1) * 64],
                            self.ident[:], [kc, self.ident], [pO], inc=(g == 1 and n == 1))
            self.cp(S.act, kcT[:].rearrange("p g t -> p (g t)"), pO[0:64, 0:512], [pO], [kcT])
            pS = [S.ps("pS", [128, 512], F32, ph) for _ in range(3)]
            pOa = [S.ps("pOa", [64, 512], F32, ph) for _ in range(2)]
            pZ = [S.ps("pZ", [64, 512], F32, ph) for _ in range(2)]
            PTs = [S.sb("PT", [128, 512], BF16, ph) for _ in range(3)]
            den = S.sb("den", [64, 512], F32, ph)
            rec = S.sb("rec", [64, 512], F32, ph)
            KT = S.sb("KT", [64, TS], BF16, ph)
            QT = S.sb("QT", [64, 4, TS], BF16, ph)
            Vt = S.sb("Vt", [128, 32, 64], BF16, ph)
            stg = [S.sb("stgA", [64, 4, 512], BF16, ph) for _ in range(2)]
            cnt = {"s": 0, "o": 0}

            def block(g, tiles, q_rhs, out_ap, out_T):
                ob = cnt["o"] % 2
                cnt["o"] += 1
                nt = len(tiles)
                rb = []
                for ti in range(nt + 1):
                    if ti < nt:
                        kl, vl, msk, rd = tiles[ti]
                        r = cnt["s"] % 3
                        cnt["s"] += 1
                        rb.append(r)
                        self.mm(pS[r][:, :], kl, q_rhs, True, msk is None, rd, [pS[r]])
                        if msk is not None:
                            self.mm(pS[r][:, :], self.ident[:], msk[:].rearrange("p r t -> p (r t)"), False, True,
                                    [self.ident, msk], [pS[r]])
                        self.actf(PTs[r][:], pS[r][:, :], AF.Exp, [pS[r]], [PTs[r]], scale=0.125)
                    if ti > 0:
                        tp_ = ti - 1
                        kl, vl, msk, rd = tiles[tp_]
                        r = rb[tp_]
                        last = tp_ == nt - 1
                        self.mm(pOa[ob][:, :], vl, PTs[r][:], tp_ == 0, last, rd + [PTs[r]], [pOa[ob]])
                        self.mm(pZ[ob][:, :], self.ones_bf[:, 0:64], PTs[r][:], tp_ == 0, last,
                                [self.ones_bf, PTs[r]], [pZ[ob]])
                self.tt(S.dve, den[:].rearrange("p (r t) -> p r t", r=4),
                        pZ[ob][:, :].rearrange("p (r t) -> p r t", r=4),
                        AP(esk[:, 4 * g:4 * g + 4], [[1, 4], [0, 128]]), ALU.add, [pZ[ob], esk], [den])
                S.dve.op(lambda: nc.vector.reciprocal(out=rec[:], in_=den[:]), [den], [rec])
                self.tt(S.dve, out_ap, pOa[ob][:, :].rearrange("p (r t) -> p r t", r=4),
                        rec[:].rearrange("p (r t) -> p r t", r=4), ALU.mult, [pOa[ob], rec], [out_T])

            for g in range(2):
                S.sp.dma(KT[:], s["KAT"][:, g, 0:TS], reads=[self.sb_["KAT"]], writes=[KT])
                S.sp.dma(QT[:], s["QAT"][:, 4 * g:4 * g + 4, 0:TS], reads=[self.sb_["QAT"]], writes=[QT])
                for part in range(4):
                    S.sp.dma(Vt[:, part * 8:(part + 1) * 8, :],
                             s["VA"][part * 1024:(part + 1) * 1024, g * 64:(g + 1) * 64].rearrange("(n p) d -> p n d", p=128),
                             reads=[self.sb_["VA"]], writes=[Vt])
                for bi in range(NTS):
                    tiles = [(kcT[:, g, n * 128:(n + 1) * 128], vc[:, n, g * 64:(g + 1) * 64], None, [kcT, vc, QT])
                             for n in range(2)]
                    for kt, msk in ((bi - 1, mprev), (bi, None), (bi + 1, mnext)):
                        if 0 <= kt < NTS:
                            tiles.append((KT[:, kt * 128:(kt + 1) * 128], Vt[:, kt, :], msk, [KT, Vt, QT]))
                    st = stg[(bi // 4) % 2]
                    block(g, tiles, QT[:, :, bi * 128:(bi + 1) * 128],
                          st[:, :, (bi % 4) * 128:(bi % 4 + 1) * 128], st)
                    if bi % 4 == 3:
                        c0 = (bi // 4) * 512
                        S.pool.dma(s["OAT"][:, 4 * g:4 * g + 4, c0:c0 + 512], st[:], reads=[st],
                                 writes=[self.sb_["OAT"]])
            KTp = S.sb("KTp", [64, 256], BF16, ph)
            QTp = S.sb("QTp", [64, 4, 256], BF16, ph)
            Vtp = S.sb("Vtp", [128, 2, 64], BF16, ph)
            q = 0
            for sq in range(4):
                base = TS + 256 * sq
                for g in range(2):
                    S.sp.dma(KTp[:], s["KAT"][:, g, base:base + 256], reads=[self.sb_["KAT"]], writes=[KTp])
                    S.sp.dma(QTp[:], s["QAT"][:, 4 * g:4 * g + 4, base:base + 256], reads=[self.sb_["QAT"]],
                             writes=[QTp])
                    S.sp.dma(Vtp[:], s["VA"][base:base + 256, g * 64:(g + 1) * 64].rearrange("(n p) d -> p n d", p=128),
                             reads=[self.sb_["VA"]], writes=[Vtp])
                    st = stg[q % 2]
                    q += 1
                    for qb in range(2):
                        tiles = [(KTp[:, n * 128:(n + 1) * 128], Vtp[:, n, :], None, [KTp, Vtp, QTp])
                                 for n in range(2)]
                        block(g, tiles, QTp[:, :, qb * 128:(qb + 1) * 128],
                              st[:, :, qb * 128:(qb + 1) * 128], st)
                    S.pool.dma(s["OAT"][:, 4 * g:4 * g + 4, base:base + 256], st[:, :, 0:256], reads=[st],
                             writes=[self.sb_["OAT"]])

    def l0_ret_b(self):
        S, nc, i, s, o = self.S, self.nc, self.i, self.s, self.o
        with ExitStack() as ph:
            lg = S.sb("lg", [128, 16], F32, ph)
            S.sp.dma(lg[:], self.bc(i["ret_decay"][0], 128, 16), writes=[lg])
            self.actf(lg[:], lg[:], AF.Exp, [lg], [lg], scale=-1.0)
            self.actf(lg[:], lg[:], AF.Ln, [lg], [lg], scale=1.0, bias=1.0)
            self.ts(S.dve, lg[:], lg[:], -1.0, None, ALU.mult, None, [lg], [lg])
            di = S.sb("di", [128, 128], I32, ph)
            S.pool.op(lambda: nc.gpsimd.iota(di[:], pattern=[[1, 128]], base=0, channel_multiplier=-1), (), [di])
            diff = S.sb("diff", [128, 128], F32, ph)
            self.cp(S.dve, diff[:], di[:], [di], [diff])
            posd = S.sb("posd", [128, 128], F32, ph)
            negd = S.sb("negd", [128, 128], F32, ph)
            mge = S.sb("mge", [128, 128], F32, ph)
            mle = S.sb("mle", [128, 128], F32, ph)
            self.ts(S.dve, posd[:], diff[:], 0.0, None, ALU.max, None, [diff], [posd])
            self.ts(S.dve, negd[:], diff[:], -1.0, 0.0, ALU.mult, ALU.max, [diff], [negd])
            self.ts(S.dve, mge[:], diff[:], 0.0, None, ALU.is_ge, None, [diff], [mge])
            self.ts(S.dve, mle[:], diff[:], 0.0, None, ALU.is_le, None, [diff], [mle])
            Dm = S.sb("Dm", [128, 8, 128], F32, ph)
            ta = S.sb("ta", [128, 128], F32, ph)
            tb = S.sb("tb", [128, 128], F32, ph)
            for h in range(8):
                self.actf(ta[:], posd[:], AF.Exp, [posd, lg], [ta], scale=lg[:, h:h + 1])
                self.actf(tb[:], negd[:], AF.Exp, [negd, lg], [tb], scale=lg[:, 8 + h:9 + h])
                self.tt(S.dve, ta[:], ta[:], mge[:], ALU.mult, [ta, mge], [ta])
                self.tt(S.dve, tb[:], tb[:], mle[:], ALU.mult, [tb, mle], [tb])
                self.tt(S.dve, Dm[:, h, :], ta[:], tb[:], ALU.add, [ta, tb], [Dm])
            jc = S.sb("jc", [128, 2], F32, ph)
            self.ts(S.dve, jc[:, 0:1], diff[:, 0:1], -1.0, None, ALU.mult, None, [diff], [jc])
            self.ts(S.dve, jc[:, 1:2], diff[:, 0:1], 1.0, 127.0, ALU.mult, ALU.add, [diff], [jc])
            kdf = S.sb("kdf", [128, 8], F32, ph)
            kdb = S.sb("kdb", [128, 8], F32, ph)
            self.actf(kdf[:], lg[:, 0:8], AF.Exp, [lg, jc], [kdf], scale=jc[:, 1:2])
            self.actf(kdb[:], lg[:, 8:16], AF.Exp, [lg, jc], [kdb], scale=jc[:, 0:1])
            ip1 = S.sb("ip1", [64, 128], F32, ph)
            imi = S.sb("imi", [64, 128], F32, ph)
            self.ts(S.dve, ip1[:], diff[0:64, :], jc[0:64, 0:1], 1.0, ALU.add, ALU.add, [diff, jc], [ip1])
            self.ts(S.dve, imi[:], ip1[:], -1.0, 129.0, ALU.mult, ALU.add, [ip1], [imi])
            qdf = S.sb("qdf", [64, 8, 128], F32, ph)
            qdb = S.sb("qdb", [64, 8, 128], F32, ph)
            for h in range(8):
                self.actf(qdf[:, h, :], ip1[:], AF.Exp, [ip1, lg], [qdf], scale=lg[0:64, h:h + 1])
                self.actf(qdb[:, h, :], imi[:], AF.Exp, [imi, lg], [qdb], scale=lg[0:64, 8 + h:9 + h])
            Gf = S.sb("Gf", [64, 8, 64], F32, ph)
            Gb = S.sb("Gb", [64, 8, 64], F32, ph)
            self.actf(Gf[:], AP(lg[0:64, 0:8], [[1, 8], [0, 64]]), AF.Exp, [lg], [Gf], scale=128.0)
            self.actf(Gb[:], AP(lg[0:64, 8:16], [[1, 8], [0, 64]]), AF.Exp, [lg], [Gb], scale=128.0)
            retg = S.sb("retg", [128, 64], F32, ph)
            S.sp.dma(retg[:], self.bc(i["ret_norm"][0], 128, 64), writes=[retg])

            SAll = [S.sb("SAll", [64, NT, 512], BF16, ph) for _ in range(2)]
            St = S.sb("St", [64, 512], F32, ph)
            KBt = [S.sb("KBt", [128, 512], BF16, ph) for _ in range(2)]
            VBt = [S.sb("VBt", [128, 512], BF16, ph) for _ in range(2)]
            Kd = [S.sb("Kd", [128, 512], BF16, ph) for _ in range(2)]
            pKV = [S.ps("pKV", [64, 512], F32, ph) for _ in range(2)]

            seqs = [(0, NTS, None)] + [(NTS + 2 * q, 2, q) for q in range(4)]
            it = 0
            for dr in range(2):
                kdec = kdf if dr == 0 else kdb
                Gd = Gf if dr == 0 else Gb
                for (c0, ncs, pq) in seqs:
                    if pq is None:
                        S.sp.dma(St[:].rearrange("d (h v) -> d h v", h=8),
                                 i["state_ret"][dr].rearrange("h d v -> d h v"), writes=[St])
                    else:
                        self.memset(S.pool, St[:], 0.0, [St])
                    order = range(c0, c0 + ncs) if dr == 0 else range(c0 + ncs - 1, c0 - 1, -1)
                    for n in order:
                        kb, vb, kd, pk = KBt[it % 2], VBt[it % 2], Kd[it % 2], pKV[it % 2]
                        it += 1
                        r0 = n * 128
                        S.sp.dma(kb[:], s["KB"][r0:r0 + 128, :], reads=[self.sb_["KB"]], writes=[kb])
                        S.sp.dma(vb[:], s["VB"][r0:r0 + 128, :], reads=[self.sb_["VB"]], writes=[vb])
                        self.cp(S.act, SAll[dr][:, n, :], St[:], [St], [SAll[dr]])
                        self.tt(S.dve, kd[:].rearrange("p (h d) -> p h d", h=8),
                                kb[:].rearrange("p (h d) -> p h d", h=8), AP(kdec[:], [[1, 8], [0, 64]]), ALU.mult,
                                [kb, kdec], [kd])
                        for h in range(8):
                            self.mm(pk[:, h * 64:(h + 1) * 64], kd[:, h * 64:(h + 1) * 64], vb[:, h * 64:(h + 1) * 64],
                                    True, True, [kd, vb], [pk], inc=(h == 7))
                        self.tt(S.dve, St[:], St[:], Gd[:].rearrange("d h v -> d (h v)"), ALU.mult, [St, Gd], [St])
                        self.tt(S.dve, St[:], pk[:, :], St[:], ALU.add, [pk, St], [St])
                    if pq is not None:
                        S.pool.dma(o["nret"][pq, dr].rearrange("h d v -> d h v"),
                                   St[:].rearrange("d (h v) -> d h v", h=8), reads=[St], writes=[self.ob_["nret"]])
            QTt = [S.sb("QTt", [64, 8, 128], BF16, ph) for _ in range(2)]
            KTt = [S.sb("KTt", [64, 8, 128], BF16, ph) for _ in range(2)]
            GBt = [S.sb("GBt", [128, 512], BF16, ph) for _ in range(2)]
            SD = [S.sb("SD", [128, 8, 128], BF16, ph) for _ in range(2)]
            Qdf = [S.sb("Qdf", [64, 8, 128], BF16, ph) for _ in range(2)]
            Qdb = [S.sb("Qdb", [64, 8, 128], BF16, ph) for _ in range(2)]
            pST = [S.ps("pST", [128, 1024], F32, ph) for _ in range(2)]
            pOb = S.ps("pOb", [128, 512], F32, ph)
            pTo = S.ps("pTo", [128, 512], BF16, ph)
            sqt = S.sb("sqtb", [128, 512], F32, ph)
            onf = S.sb("onf", [128, 512], F32, ph)
            obb = S.sb("obb", [128, 512], BF16, ph)
            ss = S.sb("ssb", [128, 8], F32, ph)
            rstd = S.sb("rstdb", [128, 8], F32, ph)
            stg = [S.sb("stgB", [128, 4, 512], BF16, ph) for _ in range(2)]
            for n in range(NT):
                r0 = n * 128
                a = n % 2
                S.sp.dma(QTt[a][:], s["QBT"][:, :, r0:r0 + 128], reads=[self.sb_["QBT"]], writes=[QTt[a]])
                S.sp.dma(KTt[a][:], s["KBT"][:, :, r0:r0 + 128], reads=[self.sb_["KBT"]], writes=[KTt[a]])
                S.sp.dma(VBt[a][:], s["VB"][r0:r0 + 128, :], reads=[self.sb_["VB"]], writes=[VBt[a]])
                S.sp.dma(GBt[a][:], s["GB"][r0:r0 + 128, :], reads=[self.sb_["GB"]], writes=[GBt[a]])
                for h in range(8):
                    self.mm(pST[a][:, h * 128:(h + 1) * 128], KTt[a][:, h, :], QTt[a][:, h, :], True, True,
                            [KTt[a], QTt[a]], [pST[a]], inc=(h == 7))
                for hh in range(2):
                    self.tt(S.dve, SD[a][:, 4 * hh:4 * hh + 4, :],
                            pST[a][:, hh * 512:(hh + 1) * 512].rearrange("p (h t) -> p h t", h=4),
                            Dm[:, 4 * hh:4 * hh + 4, :], ALU.mult, [pST[a], Dm], [SD[a]])
                self.tt(S.dve, Qdf[a][:], QTt[a][:], qdf[:], ALU.mult, [QTt[a], qdf], [Qdf[a]])
                self.tt(S.dve, Qdb[a][:], QTt[a][:], qdb[:], ALU.mult, [QTt[a], qdb], [Qdb[a]])
                for h in range(8):
                    hs = slice(h * 64, (h + 1) * 64)
                    self.mm(pOb[:, hs], SD[a][:, h, :], VBt[a][:, hs], True, False, [SD[a], VBt[a]], [pOb], inc=False)
                    self.mm(pOb[:, hs], Qdf[a][:, h, :], SAll[0][:, n, hs], False, False, [Qdf[a], SAll[0]], [pOb],
                            inc=False)
                    self.mm(pOb[:, hs], Qdb[a][:, h, :], SAll[1][:, n, hs], False, True, [Qdb[a], SAll[1]], [pOb],
                            inc=(h == 7))
                self.actf(sqt[:], pOb[:, :], AF.Square, [pOb], [sqt])
                S.dve.op(lambda: nc.vector.tensor_reduce(out=ss[:, 0:8], in_=sqt[:].rearrange("p (h d) -> p h d", h=8),
                                                         axis=AX.X, op=ALU.add), [sqt], [ss])
                self.rms_rstd(rstd, ss, 64.0, 8)
                o3 = onf[:].rearrange("p (h d) -> p h d", h=8)
                self.tt(S.dve, o3, pOb[:, :].rearrange("p (h d) -> p h d", h=8), AP(rstd[:, 0:8], [[1, 8], [0, 64]]),
                        ALU.mult, [pOb, rstd], [onf])
                self.tt(S.dve, o3, o3, AP(retg[:], [[0, 8], [1, 64]]), ALU.mult, [onf, retg], [onf])
                self.tt(S.dve, obb[:], onf[:], GBt[a][:], ALU.mult, [onf, GBt[a]], [obb])
                sub, sup = n % 4, n // 4
                for c in range(4):
                    self.tr(pTo[:, c * 128:(c + 1) * 128], obb[:, c * 128:(c + 1) * 128], self.ident[:],
                            [obb, self.ident], [pTo], inc=(c == 3))
                st = stg[sup % 2]
                self.cp(S.act, st[:, :, sub * 128:(sub + 1) * 128], pTo[:, :].rearrange("p (c t) -> p c t", c=4),
                        [pTo], [st])
                if sub == 3:
                    S.act.dma(s["OBT"][:, :, sup * 512:(sup + 1) * 512], st[:], reads=[st], writes=[self.sb_["OBT"]])

    def out_ffn(self, l):
        S, i = self.S, self.i
        NCH = FF // 128
        with ExitStack() as ph:
            W1 = S.sb("W1", [128, 8, FF], BF16, ph)
            W3 = S.sb("W3", [128, 8, FF], BF16, ph)
            W2 = S.sb("W2", [128, NCH, D], BF16, ph)

            def prefetch(stage):
                return (self.load_w_steps(W1, i["ffn_w1"][l], FF, 8, stage)
                        + self.load_w_steps(W3, i["ffn_w3"][l], FF, 8, stage)
                        + self.load_w_steps(W2, i["ffn_w2"][l], D, NCH, stage))
            self.out_proj(l, prefetch)
            S.barrier()
            self.ffn(l, (W1, W3, W2))
            S.barrier()

    def out_proj(self, l, prefetch=None):
        S, nc, i, s, o = self.S, self.nc, self.i, self.s, self.o
        if l == 0:
            wsrc, xin = i["ab_w_out"][0], i["x"]
            srcs = [("OAT", 64, 8, 0), ("OBT", 128, 4, 512)]
            xin_b = None
        else:
            wsrc, xin = i["cd_w_out"][0], s["X2"]
            srcs = [("OCT", 128, 4, 0), ("ODT", 128, 4, 512)]
            xin_b = self.sb_["X2"]
        with ExitStack() as ph:
            Ws = []
            wstage = [S.sb("wost", [128, FF], F32, ph) for _ in range(2)]
            for (nm, K, nch, row0) in srcs:
                Wt = S.sb("wo_" + nm, [K, nch, D], BF16, ph)
                self.load_w_bf16(Wt, wsrc[row0:row0 + K * nch, :], D, nch, wstage, rows=K)
                Ws.append(Wt)
            pf_steps = prefetch(wstage) if prefetch is not None else []
            gate = S.sb("gate1", [128, D], F32, ph)
            lts = [[S.sb("lt_" + nm, [K, nch, 128], BF16, ph) for _ in range(2)] for (nm, K, nch, row0) in srcs]
            xts = [S.sb("xo", [128, D], F32, ph) for _ in range(2)]
            tmp = S.sb("tmpo", [128, 512], F32, ph)
            pss = [S.ps("pso", [128, 512], F32, ph) for _ in range(3)]
            cur = -1
            pi = 0
            for t in range(NT):
                cond = 0 if t < NTS else 1
                if cond != cur:
                    self.load_mod(gate, l, cond, 2)
                    cur = cond
                r0 = t * 128
                xt = xts[t % 2]
                S.sp.dma(xt[:], xin[r0:r0 + 128, :], reads=[xin_b], writes=[xt])
                if pf_steps:
                    pf_steps.pop(0)()
                for si, (nm, K, nch, row0) in enumerate(srcs):
                    S.sp.dma(lts[si][t % 2][:], s[nm][:, :, r0:r0 + 128], reads=[self.sb_[nm]], writes=[lts[si][t % 2]])
                for cb in range(2):
                    ps = pss[pi % 3]
                    pi += 1
                    chain = [(lts[si][t % 2], Ws[si], c) for si, (nm, K, nch, row0) in enumerate(srcs) for c in range(nch)]
                    for ci, (lt, Wt, c) in enumerate(chain):
                        self.mm(ps[:, :], lt[:, c, :], Wt[:, c, cb * 512:(cb + 1) * 512], ci == 0, ci == len(chain) - 1,
                                [lt, Wt], [ps])
                    self.tt(S.dve, tmp[:], ps[:, :], gate[:, cb * 512:(cb + 1) * 512], ALU.mult, [ps, gate], [tmp])
                    self.tt(S.dve, xt[:, cb * 512:(cb + 1) * 512], tmp[:], xt[:, cb * 512:(cb + 1) * 512], ALU.add,
                            [tmp, xt], [xt])
                S.act.dma(s["X1"][r0:r0 + 128, :], xt[:], reads=[xt], writes=[self.sb_["X1"]])
            while pf_steps:
                pf_steps.pop(0)()

    def ffn(self, l, Ws):
        S, nc, i, s, o = self.S, self.nc, self.i, self.s, self.o
        dst = s["X2"] if l == 0 else o["y"]
        dst_b = self.sb_["X2"] if l == 0 else self.ob_["y"]
        NCH = FF // 128
        with ExitStack() as ph:
            W1, W3, W2 = Ws
            G = S.sb("G2", [128, D], F32, ph)
            SH = S.sb("SH2", [128, D], F32, ph)
            gate = S.sb("gate2", [128, D], F32, ph)
            xts = [S.sb("xf", [128, D], F32, ph) for _ in range(2)]
            hf = S.sb("hf2", [128, D], F32, ph)
            hb = S.sb("hb2", [128, D], BF16, ph)
            hT = S.sb("hT2", [128, 8, 512], BF16, ph)
            GT = S.sb("GT", [128, NCH, 512], BF16, ph)
            sas = [S.sb("sa", [128, 512], F32, ph) for _ in range(2)]
            tmp = S.sb("tmpf", [128, 512], F32, ph)
            ss = S.sb("ss2", [128, 8], F32, ph)
            rstd = S.sb("rstd2", [128, 8], F32, ph)
            pT = S.ps("pT2", [128, 1024], BF16, ph)
            pA = [S.ps("pA", [128, 512], F32, ph) for _ in range(2)]
            pB = [S.ps("pB", [128, 512], F32, ph) for _ in range(2)]
            pY = [S.ps("pY", [128, 512], F32, ph) for _ in range(2)]
            cur = -1
            xi = 0
            yi = 0
            for blk in range(TOK // 512):
                cond = 0 if blk < TS // 512 else 1
                if cond != cur:
                    self.make_norm_scale(G, SH, l, cond, i["norm2_g"][l], 4, 3, gate)
                    self.load_mod(gate, l, cond, 5)
                    cur = cond
                for sub in range(4):
                    r0 = blk * 512 + sub * 128
                    xt = xts[xi % 2]
                    xi += 1
                    S.sp.dma(xt[:], s["X1"][r0:r0 + 128, :], reads=[self.sb_["X1"]], writes=[xt])
                    self.norm_tile(xt, G, SH, hf, hb, ss, rstd)
                    self.transpose_to(pT, None, hb, 8,
                                      lambda: self.cp(S.act, hT[:, :, sub * 128:(sub + 1) * 128],
                                                      pT[:, :].rearrange("p (k t) -> p k t", k=8), [pT], [hT]))
                for hc in range(NCH):
                    a = hc % 2
                    cs = slice(hc * 128, (hc + 1) * 128)
                    for k in range(8):
                        self.mm(pA[a][:, :], W1[:, k, cs], hT[:, k, :], k == 0, k == 7, [W1, hT], [pA[a]])
                    for k in range(8):
                        self.mm(pB[a][:, :], W3[:, k, cs], hT[:, k, :], k == 0, k == 7, [W3, hT], [pB[a]])
                    self.actf(sas[a][:], pA[a][:, :], AF.Silu, [pA[a]], [sas[a]])
                    self.tt(S.dve, GT[:, hc, :], pB[a][:, :], sas[a][:], ALU.mult, [pB[a], sas[a]], [GT])
                for sub in range(4):
                    r0 = blk * 512 + sub * 128
                    xt = xts[xi % 2]
                    xi += 1
                    S.sp.dma(xt[:], s["X1"][r0:r0 + 128, :], reads=[self.sb_["X1"]], writes=[xt])
                    for cb in range(2):
                        py = pY[yi % 2]
                        yi += 1
                        for hc in range(NCH):
                            self.mm(py[:, :], GT[:, hc, sub * 128:(sub + 1) * 128], W2[:, hc, cb * 512:(cb + 1) * 512],
                                    hc == 0, hc == NCH - 1, [GT, W2], [py])
                        self.tt(S.dve, tmp[:], py[:, :], gate[:, cb * 512:(cb + 1) * 512], ALU.mult, [py, gate], [tmp])
                        self.tt(S.dve, xt[:, cb * 512:(cb + 1) * 512], tmp[:], xt[:, cb * 512:(cb + 1) * 512],
                                ALU.add, [tmp, xt], [xt])
                    S.pool.dma(dst[r0:r0 + 128, :], xt[:], reads=[xt], writes=[dst_b])

    def l1_in(self):
        S, nc, i, s, o = self.S, self.nc, self.i, self.s, self.o
        with ExitStack() as ph:
            W = S.sb("win1", [128, 8, 2048], BF16, ph)
            wstage = [S.sb("winst", [128, 2048], F32, ph) for _ in range(2)]
            self.load_w_bf16(W, i["cd_w_in"][0], 2048, 8, wstage)
            G = S.sb("G", [128, D], F32, ph)
            SH = S.sb("SH", [128, D], F32, ph)
            qg = S.sb("qg", [128, 64], F32, ph)
            kg = S.sb("kg", [128, 64], F32, ph)
            S.sp.dma(qg[:], self.bc(i["d_q_norm"][0], 128, 64), writes=[qg])
            S.sp.dma(kg[:], self.bc(i["d_k_norm"][0], 128, 64), writes=[kg])
            xts = [S.sb("xt", [128, D], F32, ph) for _ in range(2)]
            hf = [S.sb("hf", [128, D], F32, ph) for _ in range(2)]
            hbs = [S.sb("hb", [128, D], BF16, ph) for _ in range(2)]
            hTs = [S.sb("hT", [128, 8, 128], BF16, ph) for _ in range(2)]
            ss = [S.sb("ss", [128, 8], F32, ph) for _ in range(2)]
            rstd = [S.sb("rstd", [128, 8], F32, ph) for _ in range(2)]
            ssh = [S.sb("ssh", [128, 8], F32, ph) for _ in range(2)]
            rsh = [S.sb("rsh", [128, 8], F32, ph) for _ in range(2)]
            sqt = [S.sb("sqt", [128, 512], F32, ph) for _ in range(2)]
            qf = [S.sb("qf", [128, 512], F32, ph) for _ in range(2)]
            kf = [S.sb("kf", [128, 512], F32, ph) for _ in range(2)]
            vf = [S.sb("vf", [128, 512], F32, ph) for _ in range(2)]
            t1 = [S.sb("t1", [128, 256], F32, ph) for _ in range(2)]
            t2 = [S.sb("t2", [128, 256], F32, ph) for _ in range(2)]
            cos = [S.sb("cos", [128, 32], F32, ph) for _ in range(2)]
            sin = [S.sb("sin", [128, 32], F32, ph) for _ in range(2)]
            ub = [S.sb("ub", [128, 512], BF16, ph) for _ in range(2)]
            qdb = [S.sb("qdb", [128, 512], BF16, ph) for _ in range(2)]
            kdb = [S.sb("kdb", [128, 512], BF16, ph) for _ in range(2)]
            vdb = [S.sb("vdb", [128, 512], BF16, ph) for _ in range(2)]
            stQ = [S.sb("stQ", [64, 8, 512], BF16, ph) for _ in range(2)]
            stK = [S.sb("stK", [64, 8, 512], BF16, ph) for _ in range(2)]
            pT = [S.ps("pT", [128, 1024], BF16, ph) for _ in range(2)]
            pbs = [S.ps("pb", [128, 512], F32, ph) for _ in range(4)]
            pOs = [S.ps("pO", [128, 1024], BF16, ph) for _ in range(2)]
            tmpm = S.sb("tmpm", [128, D], F32, ph)
            cnt = {"pb": 0, "po": 0, "cond": -1}

            def tile_gen(t):
                p = t % 2
                cond = 0 if t < NTS else 1
                sub, sup, r0 = t % 4, t // 4, t * 128
                xt, hb, hT = xts[p], hbs[p], hTs[p]
                if cond != cnt["cond"]:
                    self.make_norm_scale(G, SH, 1, cond, i["norm1_g"][1], 1, 0, tmpm)
                    cnt["cond"] = cond
                S.sp.dma(xt[:], s["X2"][r0:r0 + 128, :], reads=[self.sb_["X2"]], writes=[xt])
                if cond == 0:
                    self.load_rope(cos[p], sin[p], t)
                yield from self.norm_tile_g(xt, G, SH, hf[p], hb, ss[p], rstd[p])
                yield
                self.transpose_to(pT[p], None, hb, 8,
                                  lambda: self.cp(S.act, hT[:].rearrange("p k t -> p (k t)"), pT[p][:, :], [pT[p]], [hT]))
                yield

                def proj(c0, ncols):
                    pb = pbs[cnt["pb"] % 4]
                    cnt["pb"] += 1
                    for k in range(8):
                        self.mm(pb[:, 0:ncols], hT[:, k, :], W[:, k, c0:c0 + ncols], k == 0, k == 7, [hT, W], [pb])
                    return pb

                def to_featmajor(src_b, H, stage):
                    pO = pOs[cnt["po"] % 2]
                    cnt["po"] += 1
                    for h in range(H):
                        self.tr(pO[0:64, h * 128:(h + 1) * 128], src_b[:, h * 64:(h + 1) * 64], self.ident[:],
                                [src_b, self.ident], [pO], inc=(h == H - 1))
                    yield
                    self.cp(S.act, stage[:, :, sub * 128:(sub + 1) * 128],
                            pO[0:64, 0:H * 128].rearrange("p (h t) -> p h t", h=H), [pO], [stage])

                pb = proj(0, 512)
                yield
                u = ub[p]
                self.cp(S.act, u[:], pb[:, :], [pb], [u])
                S.pool.dma(s["UTM"][r0:r0 + 128, :], u[:], reads=[u], writes=[self.sb_["UTM"]])
                pb = proj(512, 512)
                yield
                yield from self.head_norm_g(pb, 8, qg, qf[p], sqt[p], ssh[p], rsh[p])
                yield
                qd = qdb[p]
                if cond == 0:
                    self.rope_apply(qd, qf[p], 8, cos[p], sin[p], t1[p], t2[p])
                else:
                    self.cp(S.dve, qd[:], qf[p][:], [qf[p]], [qd])
                yield
                yield from to_featmajor(qd, 8, stQ[sup % 2])
                pb = proj(1024, 512)
                yield
                yield from self.head_norm_g(pb, 8, kg, kf[p], sqt[p], ssh[p], rsh[p])
                yield
                kd = kdb[p]
                if cond == 0:
                    self.rope_apply(kd, kf[p], 8, cos[p], sin[p], t1[p], t2[p])
                else:
                    self.cp(S.dve, kd[:], kf[p][:], [kf[p]], [kd])
                    pr = r0 - TS
                    S.pool.dma(o["nkd"][pr:pr + 128, :], kf[p][:], reads=[kf[p]], writes=[self.ob_["nkd"]])
                yield
                yield from to_featmajor(kd, 8, stK[sup % 2])
                pb = proj(1536, 512)
                yield
                vd = vdb[p]
                if cond == 0:
                    self.cp(S.act, vd[:], pb[:, :], [pb], [vd])
                else:
                    self.cp(S.act, vf[p][:], pb[:, :], [pb], [vf[p]])
                    self.cp(S.dve, vd[:], vf[p][:], [vf[p]], [vd])
                    pr = r0 - TS
                    S.pool.dma(o["nvd"][pr:pr + 128, :], vf[p][:], reads=[vf[p]], writes=[self.ob_["nvd"]])
                S.pool.dma(s["VD"][r0:r0 + 128, :], vd[:], reads=[vd], writes=[self.sb_["VD"]])
                if sub == 3:
                    c0 = sup * 512
                    for nm, st in (("QDT", stQ), ("KDT", stK)):
                        S.act.dma(s[nm][:, :, c0:c0 + 512], st[sup % 2][:], reads=[st[sup % 2]],
                                  writes=[self.sb_[nm]])

            for t0 in range(0, NT, 2):
                alive = [tile_gen(t0), tile_gen(t0 + 1)]
                while alive:
                    for g_ in list(alive):
                        try:
                            next(g_)
                        except StopIteration:
                            alive.remove(g_)

    def l1_attn_d(self):
        S, nc, i, s, o = self.S, self.nc, self.i, self.s, self.o
        lam_init = 0.8 - 0.6 * math.exp(-0.3 * 1)
        NKT = 2 + NTS
        with ExitStack() as ph:
            lp = S.sb("lp", [128, 4, 64], F32, ph)
            S.sp.dma(lp[:].rearrange("p a d -> p (a d)"), self.bc(i["d_lambda"][0], 128, 256), writes=[lp])
            pr = S.sb("lpr", [128, 2, 64], F32, ph)
            self.tt(S.dve, pr[:, 0, :], lp[:, 0, :], lp[:, 1, :], ALU.mult, [lp], [pr])
            self.tt(S.dve, pr[:, 1, :], lp[:, 2, :], lp[:, 3, :], ALU.mult, [lp], [pr])
            l2 = S.sb("l2", [128, 2], F32, ph)
            S.dve.op(lambda: nc.vector.tensor_reduce(out=l2[:], in_=pr[:], axis=AX.X, op=ALU.add), [pr], [l2])
            self.actf(l2[:], l2[:], AF.Exp, [l2], [l2])
            nlam = S.sb("nlam", [128, 1], F32, ph)
            self.stt(nlam[:], l2[:, 0:1], -1.0, l2[:, 1:2], ALU.mult, ALU.add, [l2], [nlam])
            self.ts(S.dve, nlam[:], nlam[:], -lam_init, None, ALU.add, None, [nlam], [nlam])
            sgc = S.sb("sgc", [128, 1], F32, ph)
            S.sp.dma(sgc[:], bass.AP(i["d_subln"].tensor, 0, [[1, 128], [1, 1]]), writes=[sgc])
            self.ts(S.dve, sgc[:], sgc[:], 1.0 - lam_init, None, ALU.mult, None, [sgc], [sgc])
            kcT = S.sb("kcTd", [128, 4, 256], BF16, ph)
            pS = [[S.ps("pSd", [128, 512], F32, ph) for _ in range(2)] for _ in range(2)]
            pO = [S.ps("pOd", [128, 512], F32, ph) for _ in range(2)]
            pZ = [S.ps("pZd", [128, 512], F32, ph) for _ in range(2)]
            PT = [[S.sb("PTd", [128, 512], BF16, ph) for _ in range(2)] for _ in range(2)]
            QT = S.sb("QTd", [128, TS], BF16, ph)
            KT = S.sb("KTd", [128, TS], BF16, ph)
            Vt = S.sb("Vtd", [128, NKT, 128], BF16, ph)
            rz = [S.sb("rzd", [128, 512], F32, ph) for _ in range(2)]
            zacc = [S.sb("zacc", [128, 512], F32, ph) for _ in range(2)]
            zacc2 = S.sb("zacc2", [128, 512], F32, ph)
            zhi = [S.sb("zhi", [128, 512], BF16, ph) for _ in range(2)]
            zlo = [S.sb("zlo", [128, 512], BF16, ph) for _ in range(2)]
            tq = [S.sb("tqd", [128, 512], F32, ph) for _ in range(2)]
            od = S.sb("odd", [128, 512], F32, ph)
            sqb = S.sb("sqbd", [128, 512], BF16, ph)
            rs = S.sb("rsd", [128, 512], F32, ph)
            stg = [S.sb("stgD", [128, 512], BF16, ph) for _ in range(2)]
            cnt = {"it": 0, "st": 0}
            kcf = S.sb("kcf", [128, 2, 512], F32, ph)
            identf = S.sb("identfD", [128, 128], F32, ph)
            self.cp(S.dve, identf[:], self.ident[:], [self.ident], [identf])
            S.sp.dma(kcf[:], i["cache_k_d"].rearrange("(n p) d -> p n d", p=128), writes=[kcf])
            for n in range(2):
                for h in range(4):
                    self.tr(pO[0][:, h * 128:(h + 1) * 128], kcf[:, n, h * 128:(h + 1) * 128], identf[:],
                            [kcf, identf], [pO[0]], inc=(h == 3))
                self.cp(S.act, kcT[:, :, n * 128:(n + 1) * 128], pO[0][:, :].rearrange("p (h t) -> p h t", h=4),
                        [pO[0]], [kcT])

            def attend(qT, q0, nq, ktiles, rd, store):
                nk = len(ktiles)
                bufs = []
                for kti in range(nk + 1):
                    if kti < nk:
                        kt_, va = ktiles[kti]
                        a = cnt["it"] % 2
                        cnt["it"] += 1
                        bufs.append(a)
                        for c in range(2):
                            hs = slice(64 * c, 64 * c + 64)
                            self.mm(pS[c][a][:, 0:nq], kt_[hs, :], qT[hs, q0:q0 + nq], True, True, rd, [pS[c][a]])
                        for c in range(2):
                            self.actf(PT[c][a][:, 0:nq], pS[c][a][:, 0:nq], AF.Exp, [pS[c][a]], [PT[c][a]], scale=0.125)
                    if kti > 0:
                        kp = kti - 1
                        a = bufs[kp]
                        va = ktiles[kp][1]
                        first, last = kp == 0, kp == nk - 1
                        for c in range(2):
                            self.mm(pO[c][:, 0:nq], va, PT[c][a][:, 0:nq], first, last, rd + [PT[c][a]], [pO[c]],
                                    inc=True)
                            if c == 0:
                                if first:
                                    self.cp(S.dve, zacc[0][:, 0:nq], PT[0][a][:, 0:nq], [PT[0][a]], [zacc[0]])
                                else:
                                    self.tt(S.dve, zacc[0][:, 0:nq], zacc[0][:, 0:nq], PT[0][a][:, 0:nq], ALU.add,
                                            [zacc[0], PT[0][a]], [zacc[0]], nosync=True)
                            else:
                                self.mm(pZ[1][:, 0:nq], self.ones_bf[:], PT[1][a][:, 0:nq], first, last,
                                        [self.ones_bf, PT[1][a]], [pZ[1]], inc=True)
                for c in range(1):
                    self.cp(S.pool, zhi[c][:, 0:nq], zacc[c][:, 0:nq], [zacc[c]], [zhi[c]])
                    self.tt(S.pool, zlo[c][:, 0:nq], zacc[c][:, 0:nq], zhi[c][:, 0:nq], ALU.subtract,
                            [zacc[c], zhi[c]], [zlo[c]])
                    self.mm(pZ[c][:, 0:nq], self.ones_bf[:], zhi[c][:, 0:nq], True, False, [self.ones_bf, zhi[c]],
                            [pZ[c]], inc=False)
                    self.mm(pZ[c][:, 0:nq], self.ones_bf[:], zlo[c][:, 0:nq], False, True, [self.ones_bf, zlo[c]],
                            [pZ[c]], inc=True)
                st = stg[cnt["st"] % 2]
                cnt["st"] += 1
                for c in range(2):
                    S.dve.op(lambda: nc.vector.reciprocal(out=rz[c][:, 0:nq], in_=pZ[c][:, 0:nq]), [pZ[c]], [rz[c]])
                    self.tt(S.dve, tq[c][:, 0:nq], pO[c][:, 0:nq], rz[c][:, 0:nq], ALU.mult, [pO[c], rz[c]], [tq[c]])
                self.stt(od[:, 0:nq], tq[1][:, 0:nq], nlam[:, 0:1], tq[0][:, 0:nq], ALU.mult, ALU.add,
                         [tq[0], tq[1], nlam], [od])
                self.tt(S.pool, sqb[:, 0:nq], od[:, 0:nq], od[:, 0:nq], ALU.mult, [od], [sqb])
                a = cnt["it"] % 2
                self.mm(pS[0][a][:, 0:nq], self.ones_bf[:], sqb[:, 0:nq], True, True, [self.ones_bf, sqb], [pS[0][a]])
                self.ts(S.dve, rs[:, 0:nq], pS[0][a][:, 0:nq], 1.0 / 128.0, EPS, ALU.mult, ALU.add, [pS[0][a]], [rs])
                self.actf(rs[:, 0:nq], rs[:, 0:nq], AF.Ln, [rs], [rs])
                self.actf(rs[:, 0:nq], rs[:, 0:nq], AF.Exp, [rs], [rs], scale=-0.5)
                self.stt(st[:, 0:nq], od[:, 0:nq], sgc[:, 0:1], rs[:, 0:nq], ALU.mult, ALU.mult, [od, sgc, rs], [st])
                store(st)

            for h in range(4):
                for c in range(2):
                    S.sp.dma(QT[64 * c:64 * c + 64, :], s["QDT"][:, 2 * h + c, 0:TS], reads=[self.sb_["QDT"]], writes=[QT])
                    S.sp.dma(KT[64 * c:64 * c + 64, :], s["KDT"][:, 2 * h + c, 0:TS], reads=[self.sb_["KDT"]], writes=[KT])
                S.pool.dma(Vt[:, 0:2, :], i["cache_v_d"][:, h * 128:(h + 1) * 128].rearrange("(n p) d -> p n d", p=128),
                           writes=[Vt])
                for part in range(4):
                    S.sp.dma(Vt[:, 2 + part * 8:2 + (part + 1) * 8, :],
                             s["VD"][part * 1024:(part + 1) * 1024, h * 128:(h + 1) * 128].rearrange("(n p) d -> p n d", p=128),
                             reads=[self.sb_["VD"]], writes=[Vt])
                ktiles = [(kcT[:, h, n * 128:(n + 1) * 128], Vt[:, n, :]) for n in range(2)]
                ktiles += [(KT[:, n * 128:(n + 1) * 128], Vt[:, 2 + n, :]) for n in range(NTS)]
                for qs in range(TS // 512):
                    def store(st, h=h, qs=qs):
                        S.pool.dma(s["ODT"][:, h, qs * 512:(qs + 1) * 512], st[:], reads=[st], writes=[self.sb_["ODT"]])
                    attend(QT, qs * 512, 512, ktiles, [QT, KT, Vt, kcT], store)
            QTp = S.sb("QTdp", [128, 256], BF16, ph)
            KTp = S.sb("KTdp", [128, 256], BF16, ph)
            Vtp = S.sb("Vtdp", [128, 2, 128], BF16, ph)
            for sqi in range(4):
                base = TS + 256 * sqi
                for h in range(4):
                    for c in range(2):
                        S.sp.dma(QTp[64 * c:64 * c + 64, :], s["QDT"][:, 2 * h + c, base:base + 256],
                                 reads=[self.sb_["QDT"]], writes=[QTp])
                        S.sp.dma(KTp[64 * c:64 * c + 64, :], s["KDT"][:, 2 * h + c, base:base + 256],
                                 reads=[self.sb_["KDT"]], writes=[KTp])
                    S.sp.dma(Vtp[:], s["VD"][base:base + 256, h * 128:(h + 1) * 128].rearrange("(n p) d -> p n d", p=128),
                             reads=[self.sb_["VD"]], writes=[Vtp])
                    ktiles = [(KTp[:, n * 128:(n + 1) * 128], Vtp[:, n, :]) for n in range(2)]

                    def store(st, h=h, base=base):
                        S.pool.dma(s["ODT"][:, h, base:base + 256], st[:, 0:256], reads=[st], writes=[self.sb_["ODT"]])
                    attend(QTp, 0, 256, ktiles, [QTp, KTp, Vtp], store)

    def l1_s5_c(self):
        S, nc, i, s, o = self.S, self.nc, self.i, self.s, self.o

        def stop(tag):
            if os.environ.get("S5_STOP") == tag:
                S.barrier()
                S.finish()
                self.leak = ExitStack()
                self.leak.push(lambda *a: True)
                raise StopBuild()
        NU = 640
        dvp = [S.dve, S.pool]
        with ExitStack() as ph:
            identf = S.sb("identf", [128, 128], F32, ph)
            self.cp(S.dve, identf[:], self.ident[:], [self.ident], [identf])
            VRe = [S.sb("VRe", [128, 32, 64], BF16, ph) for _ in range(2)]
            VIm = [S.sb("VIm", [128, 32, 64], BF16, ph) for _ in range(2)]
            CRe = [S.sb("CRe", [128, 16, 128], BF16, ph) for _ in range(2)]
            CIm = [S.sb("CIm", [128, 16, 128], BF16, ph) for _ in range(2)]
            Mg = S.sb("Mg", [128, 32, 128], BF16, ph)
            R8 = S.sb("R8", [128, 32], F32, ph)
            TH8 = S.sb("TH8", [128, 32], F32, ph)
            H0 = [S.sb("H0", [128, 32], F32, ph) for _ in range(2)]
            mask8 = S.sb("mask8", [128, 8], BF16, ph)
            Sel = S.sb("Sel", [128, 16], BF16, ph)
            maskI = S.sb("maskI", [128, 8], BF16, ph)
            Esel = S.sb("Esel", [128, 16], BF16, ph)
            FIN = [S.sb("FIN", [128, 4, 32], F32, ph) for _ in range(2)]
            self.memset(S.pool, mask8[:], 0.0, [mask8])
            self.memset(S.pool, Esel[:], 0.0, [Esel])
            for n in range(16):
                S.pool.op(lambda: nc.gpsimd.affine_select(out=mask8[:], in_=mask8[:], compare_op=ALU.not_equal, fill=1.0,
                                                          base=-8 * n, pattern=[[-1, 8]], channel_multiplier=1),
                          [mask8], [mask8])
            for n in range(8):
                S.pool.op(lambda: nc.gpsimd.affine_select(out=Esel[:], in_=Esel[:], compare_op=ALU.not_equal, fill=1.0,
                                                          base=-16 * n, pattern=[[-1, 16]], channel_multiplier=1),
                          [Esel], [Esel])
            for (m_, step, hi) in ((Sel, 8, 7), (maskI, 16, 15)):
                n_ = m_.t.shape[1]
                self.memset(S.pool, m_[:], 1.0, [m_])
                S.pool.op(lambda: nc.gpsimd.affine_select(out=m_[:], in_=m_[:], compare_op=ALU.is_ge, fill=0.0, base=0,
                                                          pattern=[[-step, n_]], channel_multiplier=1), [m_], [m_])
                S.pool.op(lambda: nc.gpsimd.affine_select(out=m_[:], in_=m_[:], compare_op=ALU.is_ge, fill=0.0, base=hi,
                                                          pattern=[[step, n_]], channel_multiplier=-1), [m_], [m_])
            stop("masks")
            pT4 = S.ps("pT4", [128, 512], F32, ph)
            with ExitStack() as st:
                raw = S.sb("raw", [32, 4, 128], F32, st)
                for k, nm in enumerate(("ssm_lambda_re", "ssm_lambda_im")):
                    S.sp.dma(raw[:, k, :], bass.AP(i[nm].tensor, 0, [[128, 32], [1, 128]]), writes=[raw])
                for k, nm in enumerate(("state_ssm_re", "state_ssm_im")):
                    S.sp.dma(raw[:, 2 + k, :], bass.AP(i[nm].tensor, 0, [[128, 32], [1, 128]]), writes=[raw])
                LR = S.sb("LR", [128, 32], F32, st)
                LI = S.sb("LI", [128, 32], F32, st)
                for k, dstt in enumerate((LR, LI, H0[0], H0[1])):
                    self.tr(pT4[:, k * 32:(k + 1) * 32], raw[:, k, :], identf[0:32, 0:32], [raw, identf], [pT4])
                    self.cp(S.act, dstt[:], pT4[:, k * 32:(k + 1) * 32], [pT4], [dstt])
                stop("lam")
                DT = S.sb("DT", [128, 32], F32, st)
                ldt = S.sb("ldt", [128, 64], F32, st)
                S.sp.dma(ldt[:], self.bc(i["ssm_log_dt"][0], 128, 64), writes=[ldt])
                for m in range(2):
                    self.actf(DT[64 * m:64 * m + 64, :], AP(ldt[64 * m:64 * m + 64, m:m + 1], [[2, 32]]), AF.Exp,
                              [ldt], [DT])
                LRDT = S.sb("LRDT", [128, 32], F32, st)
                TH = S.sb("TH", [128, 32], F32, st)
                self.tt(S.dve, LRDT[:], LR[:], DT[:], ALU.mult, [LR, DT], [LRDT])
                self.tt(S.dve, TH[:], LI[:], DT[:], ALU.mult, [LI, DT], [TH])
                self.ts(S.dve, TH8[:], TH[:], 8.0, None, ALU.mult, None, [TH], [TH8])
                stop("dt")
                MAG = S.sb("MAG", [128, 16, 32], F32, st)
                ANG = S.sb("ANG", [128, 16, 32], F32, st)
                PWr = S.sb("PWr", [128, 16, 32], F32, st)
                PWi = S.sb("PWi", [128, 16, 32], F32, st)
                for kk in range(16):
                    self.actf(MAG[:, kk, :], LRDT[:], AF.Exp, [LRDT], [MAG], scale=float(kk - 7))
                    self.ts(S.dve, ANG[:, kk, :], TH[:], float(kk - 7), None, ALU.mult, None, [TH], [ANG])
                self.cp(S.act, R8[:], MAG[:, 15, :], [MAG], [R8])
                self.sin_of(PWr[:].rearrange("p k c -> p (k c)"), PWr, ANG[:].rearrange("p k c -> p (k c)"),
                            math.pi / 2.0, [128, 512], st, [ANG])
                self.sin_of(PWi[:].rearrange("p k c -> p (k c)"), PWi, ANG[:].rearrange("p k c -> p (k c)"),
                            0.0, [128, 512], st, [ANG])
                self.tt(S.dve, PWr[:], PWr[:], MAG[:], ALU.mult, [PWr, MAG], [PWr])
                self.tt(S.dve, PWi[:], PWi[:], MAG[:], ALU.mult, [PWi, MAG], [PWi])
                stop("pw")
                den = S.sb("den5", [128, 32], F32, st)
                am1 = S.sb("am1", [128, 32], F32, st)
                fr = S.sb("fr", [128, 32], F32, st)
                fi = S.sb("fi", [128, 32], F32, st)
                tq = S.sb("tq", [128, 32], F32, st)
                self.tt(S.dve, den[:], LR[:], LR[:], ALU.mult, [LR], [den])
                self.tt(S.dve, tq[:], LI[:], LI[:], ALU.mult, [LI], [tq])
                self.tt(S.dve, den[:], den[:], tq[:], ALU.add, [den, tq], [den])
                S.dve.op(lambda: nc.vector.reciprocal(out=den[:], in_=den[:]), [den], [den])
                self.ts(S.dve, am1[:], PWr[:, 8, :], -1.0, None, ALU.add, None, [PWr], [am1])
                self.tt(S.dve, fr[:], am1[:], LR[:], ALU.mult, [am1, LR], [fr])
                self.tt(S.dve, tq[:], PWi[:, 8, :], LI[:], ALU.mult, [PWi, LI], [tq])
                self.tt(S.dve, fr[:], fr[:], tq[:], ALU.add, [fr, tq], [fr])
                self.tt(S.dve, fr[:], fr[:], den[:], ALU.mult, [fr, den], [fr])
                self.tt(S.dve, fi[:], PWi[:, 8, :], LR[:], ALU.mult, [PWi, LR], [fi])
                self.tt(S.dve, tq[:], am1[:], LI[:], ALU.mult, [am1, LI], [tq])
                self.tt(S.dve, fi[:], fi[:], tq[:], ALU.subtract, [fi, tq], [fi])
                self.tt(S.dve, fi[:], fi[:], den[:], ALU.mult, [fi, den], [fi])
                stop("f")
                Bre = S.sb("Bre", [128, 16, 16], F32, st)
                Bim = S.sb("Bim", [128, 16, 16], F32, st)
                for nm, dstt in (("ssm_b_re", Bre), ("ssm_b_im", Bim)):
                    for part in range(4):
                        S.sp.dma(dstt[:, part * 4:(part + 1) * 4, :],
                                 bass.AP(i[nm].tensor, part * 4 * 2048, [[16, 128], [2048, 4], [1, 16]]), writes=[dstt])
                FBr = [S.sb("FBr", [128, 16, 16], F32, st) for _ in range(2)]
                FBi = [S.sb("FBi", [128, 16, 16], F32, st) for _ in range(2)]
                tw = [S.sb("tw", [128, 16, 16], F32, st) for _ in range(4)]
                for dr in range(2):
                    frb = AP(fr[:, dr * 16:(dr + 1) * 16], [[1, 16], [0, 16]])
                    fib = AP(fi[:, dr * 16:(dr + 1) * 16], [[1, 16], [0, 16]])
                    self.tt(S.dve, tw[0][:], Bre[:], frb, ALU.mult, [Bre, fr], [tw[0]])
                    self.tt(S.dve, tw[1][:], Bim[:], fib, ALU.mult, [Bim, fi], [tw[1]])
                    self.tt(S.dve, FBr[dr][:], tw[0][:], tw[1][:], ALU.subtract, [tw[0], tw[1]], [FBr[dr]])
                    self.tt(S.dve, tw[0][:], Bim[:], frb, ALU.mult, [Bim, fr], [tw[0]])
                    self.tt(S.dve, tw[1][:], Bre[:], fib, ALU.mult, [Bre, fi], [tw[1]])
                    self.tt(S.dve, FBi[dr][:], tw[0][:], tw[1][:], ALU.add, [tw[0], tw[1]], [FBi[dr]])
                stop("b")
                CT = [S.sb("CT", [128, 16, 16], F32, st) for _ in range(2)]
                craw = S.sb("craw", [128, 4, 2, 64], F32, st)
                for k, nm in enumerate(("ssm_c_re", "ssm_c_im")):
                    for blk in range(2):
                        S.sp.dma(craw[:, :, blk, :], bass.AP(i[nm].tensor, 0, [[64, 128], [8192, 4], [1, 64]]),
                                 reads=[CT[0], CT[1]], writes=[craw])
                    for t4 in range(4):
                        self.tr(pT4[:, t4 * 128:(t4 + 1) * 128], craw[:, t4, :, :].rearrange("p b c -> p (b c)"),
                                identf[:], [craw, identf], [pT4], inc=(t4 == 3))
                    for m in range(2):
                        src = AP(pT4[64 * m:64 * m + 64, 16 * m:16 * m + 16], [[128, 4], [32, 4], [1, 16]])
                        self.cp(S.act, CT[k][64 * m:64 * m + 64, :, :].rearrange("p (a b) c -> p a b c", a=4), src,
                                [pT4], [CT[k]])
                stop("c")
                cnt = {"e": 0}

                def cpow(outr, outi, Ar, Ai, dr, efn, neg_im):
                    for x in range(8):
                        kk = efn(x) + 7
                        pr_ = AP(PWr[:, kk, dr * 16:(dr + 1) * 16], [[1, 16], [0, 16]])
                        pi_ = AP(PWi[:, kk, dr * 16:(dr + 1) * 16], [[1, 16], [0, 16]])
                        E = S.dve
                        cnt["e"] += 1
                        a, b = (tw[0], tw[1]) if E is S.dve else (tw[2], tw[3])
                        self.tt(E, a[:], Ar[:], pr_, ALU.mult, [Ar, PWr], [a])
                        self.tt(E, b[:], Ai[:], pi_, ALU.mult, [Ai, PWi], [b])
                        self.tt(E, outr[:, :, x, :], a[:], b[:], ALU.subtract, [a, b], [outr])
                        self.tt(E, a[:], Ar[:], pi_, ALU.mult, [Ar, PWi], [a])
                        self.tt(E, b[:], Ai[:], pr_, ALU.mult, [Ai, PWr], [b])
                        if neg_im:
                            self.stt_any(E, outi[:, :, x, :], a[:], -1.0, b[:], ALU.mult, ALU.subtract, [a, b], [outi])
                        else:
                            self.tt(E, outi[:, :, x, :], a[:], b[:], ALU.add, [a, b], [outi])

                XTr = [S.sb("XTr", [128, 16, 8, 16], F32, st) for _ in range(2)]
                XTi = [S.sb("XTi", [128, 16, 8, 16], F32, st) for _ in range(2)]
                cpow(XTr[0], XTi[0], FBr[0], FBi[0], 0, lambda j: 7 - j, False)
                cpow(XTr[1], XTi[1], FBr[1], FBi[1], 1, lambda j: j, False)
                YMr = [S.sb("YMr", [128, 16, 8, 16], F32, st) for _ in range(2)]
                YMi = [S.sb("YMi", [128, 16, 8, 16], F32, st) for _ in range(2)]
                cpow(YMr[0], YMi[0], CT[0], CT[1], 0, lambda x: x - 7, True)
                cpow(YMr[1], YMi[1], CT[0], CT[1], 1, lambda x: -x, True)
                YCr = S.sb("YCr", [128, 16, 8, 16], F32, st)
                YCi = S.sb("YCi", [128, 16, 8, 16], F32, st)
                for dr, efn in ((0, lambda x: x + 1), (1, lambda x: 8 - x)):
                    cpow(YCr, YCi, CT[0], CT[1], dr, efn, True)
                    self.cp(S.act, CRe[dr][:], YCr[:].rearrange("p q x c -> p q (x c)"), [YCr], [CRe[dr]])
                    self.cp(S.act, CIm[dr][:], YCi[:].rearrange("p q x c -> p q (x c)"), [YCi], [CIm[dr]])
                stop("cpow")
                pT5 = S.ps("pT5", [128, 512], F32, st)
                for dr in range(2):
                    for (src_, dst_) in ((XTr[dr], VRe[dr]), (XTi[dr], VIm[dr])):
                        for q8 in range(2):
                            for m, bank in ((0, pT4), (1, pT5)):
                                for qq in range(8):
                                    q = q8 * 8 + qq
                                    self.tr(bank[:, qq * 64:(qq + 1) * 64],
                                            src_[64 * m:64 * m + 64, q, :, :].rearrange("p x c -> p (x c)"),
                                            identf[64 * m:64 * m + 64, 64 * m:64 * m + 64], [src_, identf], [bank],
                                            inc=(qq == 7))
                            for m, bank in ((0, pT4), (1, pT5)):
                                self.cp(S.act, AP(dst_[:, q8 * 16 + m, 0:1], [[128, 8], [1, 64]]),
                                        bank[:, :].rearrange("p (g c) -> p g c", g=8), [bank], [dst_])
                stop("vtr")
                maskF = S.sb("maskF", [128, 8, 16], F32, st)
                maskB = S.sb("maskB", [128, 8, 16], F32, st)
                self.memset(S.pool, maskF[:], 1.0, [maskF])
                self.memset(S.pool, maskB[:], 1.0, [maskB])
                S.pool.op(lambda: nc.gpsimd.affine_select(out=maskF[:], in_=maskF[:], compare_op=ALU.is_ge, fill=0.0,
                                                          base=15, pattern=[[16, 8], [0, 16]], channel_multiplier=-1),
                          [maskF], [maskF])
                S.pool.op(lambda: nc.gpsimd.affine_select(out=maskB[:], in_=maskB[:], compare_op=ALU.is_ge, fill=0.0,
                                                          base=0, pattern=[[-16, 8], [0, 16]], channel_multiplier=1),
                          [maskB], [maskB])
                Dcol = S.sb("Dcol", [128, 32], F32, st)
                for j in range(8):
                    S.sp.dma(Dcol[16 * j:16 * j + 16, :], bass.AP(i["ssm_d"].tensor, 0, [[1, 16], [16, 32]]),
                             writes=[Dcol], allow_slow_non_contiguous=True)
                pMa = [S.ps("pMa", [128, 128], F32, st) for _ in range(2)]
                pMb = [S.ps("pMb", [128, 128], F32, st) for _ in range(2)]
                ma = S.sb("ma", [128, 128], F32, st)
                mb = S.sb("mb", [128, 128], F32, st)
                for g in range(32):
                    q, m = g // 2, g % 2
                    hs = slice(64 * m, 64 * m + 64)
                    for (pm, dr) in ((pMa[g % 2], 0), (pMb[g % 2], 1)):
                        self.mm(pm[:, :], XTr[dr][hs, q, :, :].rearrange("p x c -> p (x c)"),
                                YMr[dr][hs, q, :, :].rearrange("p x c -> p (x c)"), True, False,
                                [XTr[dr], YMr[dr]], [pm], inc=False)
                        self.mm(pm[:, :], XTi[dr][hs, q, :, :].rearrange("p x c -> p (x c)"),
                                YMi[dr][hs, q, :, :].rearrange("p x c -> p (x c)"), False, True,
                                [XTi[dr], YMi[dr]], [pm], inc=True)
                    self.tt(S.dve, ma[:], pMa[g % 2][:, :], maskF[:].rearrange("p a b -> p (a b)"), ALU.mult,
                            [pMa[g % 2], maskF], [ma])
                    self.tt(S.dve, mb[:], pMb[g % 2][:, :], maskB[:].rearrange("p a b -> p (a b)"), ALU.mult,
                            [pMb[g % 2], maskB], [mb])
                    self.tt(S.dve, ma[:], ma[:], mb[:], ALU.add, [ma, mb], [ma])
                    self.stt(Mg[:, g, :], identf[:], Dcol[:, g:g + 1], ma[:], ALU.mult, ALU.add,
                             [identf, Dcol, ma], [Mg])
                S.barrier()
            if os.environ.get("S5_STOP") == "setup":
                return
            U = S.sb("U5", [128, 32, NU], BF16, ph)
            Yall = S.sb("Yall", [128, 32, NU], BF16, ph)
            p1 = ExitStack()
            pU = [S.ps("pU", [128, 512], F32, p1) for _ in range(2)]
            uts = [S.sb("ut5", [128, 512], BF16, p1) for _ in range(2)]
            MLs = [S.sb("ML", [128, 32, 8, 16], BF16, p1) for _ in range(2)]
            for t in range(NT):
                a = t % 2
                ut, ML, pu = uts[a], MLs[a], pU[a]
                S.sp.dma(ut[:], s["UTM"][t * 128:(t + 1) * 128, :], reads=[self.sb_["UTM"]], writes=[ut])
                self.tt(S.dve, AP(ML[:, 0, 0, 0:1], [[128, 32], [16, 8], [1, 16]]),
                        AP(ut[:, 0:1], [[16, 32], [0, 8], [1, 16]]), AP(mask8[:, 0:1], [[0, 32], [1, 8], [0, 16]]),
                        ALU.mult, [ut, mask8], [ML])
                for g in range(32):
                    self.mm(pu[:, g * 16:(g + 1) * 16], ML[:, g, :, :].rearrange("p j c -> p (j c)"), Sel[:], True, True,
                            [ML, Sel], [pu], inc=(g == 31))
                self.cp(S.act, U[:, :, t * 16:(t + 1) * 16], pu[:, :].rearrange("p (g n) -> p g n", g=32), [pu], [U])
            S.barrier()
            p1.close()
            if os.environ.get("S5_STOP") == "regroup":
                return
            ph_outer = ph
            ph = ExitStack()
            nl_i = S.sb("nl_i", [128, NU], I32, ph)
            S.pool.op(lambda: nc.gpsimd.iota(nl_i[:, 0:512], pattern=[[1, 512]], base=1, channel_multiplier=0), (), [nl_i])
            S.pool.op(lambda: nc.gpsimd.iota(nl_i[:, 512:NU].rearrange("p (a b) -> p a b", a=4),
                                             pattern=[[0, 4], [1, 32]], base=1, channel_multiplier=0), (), [nl_i])
            nloc1 = S.sb("nloc1", [128, NU], F32, ph)
            self.cp(S.dve, nloc1[:], nl_i[:], [nl_i], [nloc1])
            rmask = S.sb("rmask", [128, NU], F32, ph)
            self.memset(S.pool, rmask[:], 1.0, [rmask])
            self.memset(S.pool, AP(rmask[:, 512:513], [[32, 4]]), 0.0, [rmask])
            cosT = S.sb("cosT", [128, NU], F32, ph)
            sinT = S.sb("sinT", [128, NU], F32, ph)
            angT = S.sb("angT", [128, NU], F32, ph)
            Rrow = S.sb("Rrow", [128, NU], F32, ph)
            sin_tmps = [(S.sb("sr_tot", [128, NU], F32, ph), S.sb("sr_nf", [128, NU], F32, ph),
                         S.sb("sr_ni", [128, NU], I32, ph)) for _ in range(2)]
            pvs = [[S.ps("pv", [128, 512], F32, ph), S.ps("pv2", [128, 128], F32, ph)] for _ in range(2)]
            vv = [S.sb("vv", [128, NU], F32, ph) for _ in range(2)]
            zz = [S.sb("zz", [128, NU], F32, ph) for _ in range(2)]
            GG = [S.sb("GG", [128, NU], F32, ph) for _ in range(2)]
            wk = [S.sb("wk5", [128, NU], F32, ph) for _ in range(2)]
            Sst = [[S.sb("Sst", [128, NU], BF16, ph) for _ in range(2)] for _ in range(2)]
            fin = S.sb("fin5", [128, 2, 4], F32, ph)
            pY = [S.ps("pY5", [128, 512], F32, ph), S.ps("pY5b", [128, 128], F32, ph)]
            cranges = [(0, 512, 0), (512, 128, 1)]

            def P4(t_, first, n_, step):
                return AP(t_[:, first:first + 1], [[32, 4], [step, n_]])

            for q in range(16):
                for dr in range(2):
                    col = dr * 16 + q
                    rev = (dr == 1)
                    self.ts(S.dve, angT[:], nloc1[:], TH8[:, col:col + 1], None, ALU.mult, None, [nloc1, TH8], [angT])
                    self.sin_of(cosT[:], cosT, angT[:], math.pi / 2.0, [128, NU], None, [angT], tmps=sin_tmps[0])
                    self.sin_of(sinT[:], sinT, angT[:], 0.0, [128, NU], None, [angT], tmps=sin_tmps[1])
                    self.actf(Rrow[:], rmask[:], AF.Copy, [rmask, R8], [Rrow], scale=R8[:, col:col + 1])
                    for ri, Vt in ((0, VRe[dr]), (1, VIm[dr])):
                        for (c0, cn, pi_) in cranges:
                            for m in range(2):
                                g = 2 * q + m
                                self.mm(pvs[ri][pi_][64 * m:64 * m + 64, 0:cn], Vt[:, g, :], U[:, g, c0:c0 + cn],
                                        True, True, [Vt, U], [pvs[ri][pi_]], inc=(m == 1))
                            self.cp(S.act, vv[ri][:, c0:c0 + cn], pvs[ri][pi_][:, 0:cn], [pvs[ri][pi_]], [vv[ri]])
                    if not rev:
                        parts = [(lambda t_: t_[:, 0:NU], lambda t_: t_[:, 0:NU])]
                    else:
                        parts = [(lambda t_: t_[:, 0:512], lambda t_: AP(t_[:, 511:512], [[-1, 512]])),
                                 (lambda t_: P4(t_, 512, 32, 1), lambda t_: P4(t_, 543, 32, -1))]
                    for (dv_, sv_) in parts:
                        self.tt(S.dve, dv_(zz[0]), sv_(vv[0]), dv_(cosT), ALU.mult, [vv[0], cosT], [zz[0]])
                        self.tt(S.dve, dv_(wk[0]), sv_(vv[1]), dv_(sinT), ALU.mult, [vv[1], sinT], [wk[0]])
                        self.tt(S.dve, dv_(zz[1]), sv_(vv[1]), dv_(cosT), ALU.mult, [vv[1], cosT], [zz[1]])
                        self.tt(S.dve, dv_(wk[1]), sv_(vv[0]), dv_(sinT), ALU.mult, [vv[0], sinT], [wk[1]])
                    self.tt(S.dve, zz[0][:], zz[0][:], wk[0][:], ALU.add, [zz[0], wk[0]], [zz[0]], nosync=True)
                    self.tt(S.dve, zz[1][:], zz[1][:], wk[1][:], ALU.subtract, [zz[1], wk[1]], [zz[1]], nosync=True)
                    for ri in range(2):
                        S.dve.op(lambda: nc.vector.tensor_tensor_scan(
                            out=GG[ri][:], data0=Rrow[:], data1=zz[ri][:], initial=H0[ri][:, col:col + 1],
                            op0=ALU.mult, op1=ALU.add), [Rrow, zz[ri], H0[ri]], [GG[ri]], nosync=True)
                    self.tt(S.dve, wk[0][:], GG[0][:], cosT[:], ALU.mult, [GG[0], cosT], [wk[0]], nosync=True)
                    self.tt(S.dve, wk[1][:], GG[1][:], sinT[:], ALU.mult, [GG[1], sinT], [wk[1]], nosync=True)
                    self.tt(S.dve, zz[0][:], GG[0][:], sinT[:], ALU.mult, [GG[0], sinT], [zz[0]], nosync=True)
                    self.tt(S.dve, zz[1][:], GG[1][:], cosT[:], ALU.mult, [GG[1], cosT], [zz[1]], nosync=True)
                    for ri, (A_, B_, op_) in enumerate(((wk[0], wk[1], ALU.subtract), (zz[0], zz[1], ALU.add))):
                        St_ = Sst[dr][ri]
                        if not rev:
                            self.tt(S.dve, St_[:, 1:NU], A_[:, 0:NU - 1], B_[:, 0:NU - 1], op_, [A_, B_], [St_],
                                    nosync=True)
                            self.cp(S.act, St_[:, 0:1], H0[ri][:, col:col + 1], [H0[ri]], [St_])
                            self.memset(S.pool, AP(St_[:, 512:513], [[32, 4]]), 0.0, [St_])
                        else:
                            self.tt(S.dve, St_[:, 0:511], AP(A_[:, 510:511], [[-1, 511]]), AP(B_[:, 510:511], [[-1, 511]]),
                                    op_, [A_, B_], [St_])
                            self.tt(S.dve, P4(St_, 512, 31, 1), P4(A_, 542, 31, -1), P4(B_, 542, 31, -1), op_,
                                    [A_, B_], [St_])
                            self.cp(S.act, St_[:, 511:512], H0[ri][:, col:col + 1], [H0[ri]], [St_])
                            self.memset(S.pool, AP(St_[:, 543:544], [[32, 4]]), 0.0, [St_])
                        self.tt(S.dve, fin[:, ri, :], AP(A_[:, 543:544], [[32, 4]]), AP(B_[:, 543:544], [[32, 4]]), op_,
                                [A_, B_], [fin])
                        self.cp(S.dve, AP(FIN[ri][:, 0, col:col + 1], [[32, 4]]), fin[:, ri, :], [fin], [FIN[ri]])
                for m in range(2):
                    g = 2 * q + m
                    hs = slice(64 * m, 64 * m + 64)
                    for (c0, cn, pi_) in cranges:
                        py = pY[pi_]
                        self.mm(py[:, 0:cn], Mg[:, g, :], U[:, g, c0:c0 + cn], True, False, [Mg, U], [py], inc=False)
                        k = 0
                        for dr in range(2):
                            for (Ct, ri) in ((CRe[dr], 0), (CIm[dr], 1)):
                                k += 1
                                self.mm(py[:, 0:cn], Ct[hs, q, :], Sst[dr][ri][hs, c0:c0 + cn], False, k == 4,
                                        [Ct, Sst[dr][ri]], [py], inc=(k == 4))
                        self.cp(S.act, Yall[:, g, c0:c0 + cn], py[:, 0:cn], [py], [Yall])
            S.barrier()
            ph.close()
            if os.environ.get("S5_STOP") == "states":
                return
            ph = ExitStack()
            fo = S.sb("fo5", [32, 2, 4, 128], F32, ph)
            for ri, nm in ((0, "nsr"), (1, "nsi")):
                for sq in range(4):
                    self.tr(pT4[0:32, sq * 128:(sq + 1) * 128], FIN[ri][:, sq, :], identf[:], [FIN[ri], identf], [pT4],
                            inc=(sq == 3))
                self.cp(S.act, fo[:, ri, :, :].rearrange("p s c -> p (s c)"), pT4[0:32, :], [pT4], [fo])
                for sq in range(4):
                    S.pool.dma(bass.AP(o[nm].tensor, sq * 4096, [[128, 32], [1, 128]]), fo[:, ri, sq, :], reads=[fo],
                               writes=[self.ob_[nm]])
            Wg = S.sb("Wglu", [128, 4, 512], BF16, ph)
            wgst = [S.sb("wgst", [128, 512], F32, ph) for _ in range(2)]
            self.load_w_bf16(Wg, i["ssm_glu_w"][0], 512, 4, wgst)
            gbias = S.sb("gbias", [128, 512], F32, ph)
            S.sp.dma(gbias[:], self.bc(i["ssm_glu_b"][0], 128, 512), writes=[gbias])
            YE = [S.sb("YE", [128, 32, 16, 8], BF16, ph) for _ in range(2)]
            pTok = [S.ps("pTok", [128, 512], F32, ph) for _ in range(2)]
            yt_ = [S.sb("yt5", [128, 512], F32, ph) for _ in range(2)]
            x2_ = [S.sb("x25", [128, 512], F32, ph) for _ in range(2)]
            gl_ = [S.sb("gl5", [128, 512], F32, ph) for _ in range(2)]
            glb_ = [S.sb("glb5", [128, 512], BF16, ph) for _ in range(2)]
            gT_ = [S.sb("gT5", [128, 4, 128], BF16, ph) for _ in range(2)]
            pG_ = [S.ps("pG5", [128, 512], F32, ph) for _ in range(2)]
            pTb_ = [S.ps("pTb5", [128, 512], BF16, ph) for _ in range(2)]
            ocb_ = [S.sb("ocb5", [128, 512], BF16, ph) for _ in range(2)]
            stg = [S.sb("stgC", [128, 4, 512], BF16, ph) for _ in range(2)]

            def tok_gen(t):
                a = t % 2
                ye, pt = YE[a], pTok[a]
                yt, x2, gl, glb, gT, pG, pTb, ocb = yt_[a], x2_[a], gl_[a], glb_[a], gT_[a], pG_[a], pTb_[a], ocb_[a]
                self.tt(S.dve, AP(ye[:, 0, 0, 0:1], [[128, 32], [8, 16], [1, 8]]),
                        AP(Yall[:, 0, t * 16:t * 16 + 1], [[NU, 32], [1, 16], [0, 8]]),
                        AP(maskI[:, 0:1], [[0, 32], [0, 16], [1, 8]]), ALU.mult, [Yall, maskI], [ye])
                yield
                for g in range(32):
                    self.mm(pt[:, g * 16:(g + 1) * 16], ye[:, g, :, :].rearrange("p n i -> p (n i)"), Esel[:], True, True,
                            [ye, Esel], [pt], inc=(g == 31))
                yield
                self.cp(S.act, yt[:], pt[:, :], [pt], [yt])
                yield
                self.tt(S.dve, x2[:], yt[:], yt[:], ALU.mult, [yt], [x2])
                self.ts(S.dve, x2[:], x2[:], 0.044715, 1.0, ALU.mult, ALU.add, [x2], [x2])
                self.tt(S.dve, x2[:], x2[:], yt[:], ALU.mult, [x2, yt], [x2], nosync=True)
                yield
                self.actf(x2[:], x2[:], AF.Sigmoid, [x2], [x2], scale=1.5957691216)
                yield
                self.tt(S.dve, gl[:], x2[:], yt[:], ALU.mult, [x2, yt], [gl])
                yield
                self.cp(S.act, glb[:], gl[:], [gl], [glb])
                yield
                for c in range(4):
                    self.tr(pTb[:, c * 128:(c + 1) * 128], glb[:, c * 128:(c + 1) * 128], self.ident[:],
                            [glb, self.ident], [pTb], inc=(c == 3))
                yield
                self.cp(S.act, gT[:].rearrange("p c t -> p (c t)"), pTb[:, :], [pTb], [gT])
                yield
                for c in range(4):
                    self.mm(pG[:, :], gT[:, c, :], Wg[:, c, :], c == 0, c == 3, [gT, Wg], [pG])
                yield
                self.tt(S.dve, x2[:], pG[:, :], gbias[:], ALU.add, [pG, gbias], [x2])
                yield
                self.actf(x2[:], x2[:], AF.Sigmoid, [x2], [x2])
                yield
                self.tt(S.dve, ocb[:], x2[:], gl[:], ALU.mult, [x2, gl], [ocb])
                yield
                sub, sup = t % 4, t // 4
                for c in range(4):
                    self.tr(pTb[:, c * 128:(c + 1) * 128], ocb[:, c * 128:(c + 1) * 128], self.ident[:],
                            [ocb, self.ident], [pTb], inc=(c == 3))
                yield
                stt_ = stg[sup % 2]
                self.cp(S.act, stt_[:, :, sub * 128:(sub + 1) * 128], pTb[:, :].rearrange("p (c t) -> p c t", c=4),
                        [pTb], [stt_])
                if sub == 3:
                    S.pool.dma(s["OCT"][:, :, sup * 512:(sup + 1) * 512], stt_[:], reads=[stt_], writes=[self.sb_["OCT"]])

            for t0 in range(0, NT, 2):
                alive = [tok_gen(t0), tok_gen(t0 + 1)]
                while alive:
                    for g_ in list(alive):
                        try:
                            next(g_)
                        except StopIteration:
                            alive.remove(g_)
            S.barrier()
            ph.close()


W_NAMES = ["ada_w", "ada_b", "norm1_g", "norm2_g", "ffn_w1", "ffn_w3", "ffn_w2", "ab_w_in", "ab_w_out",
           "a_q_norm", "a_k_norm", "a_sink", "ret_decay", "ret_norm", "cd_w_in", "cd_w_out", "ssm_lambda_re",
           "ssm_lambda_im", "ssm_log_dt", "ssm_b_re", "ssm_b_im", "ssm_c_re", "ssm_c_im", "ssm_d", "ssm_glu_w",
           "ssm_glu_b", "d_q_norm", "d_k_norm", "d_lambda", "d_subln"]


def core_inputs(inp, b):
    f = lambda a: np.ascontiguousarray(a, dtype=np.float32)
    m = {}
    m["x"] = f(np.concatenate([inp["x_sample"][b].reshape(TS, D), inp["x_prompt"][4 * b:4 * b + 4].reshape(TP, D)], 0))
    m["cond"] = f(np.stack([inp["c"][b], inp["c_ctx"]], 0))
    m["cache_k_a"] = f(inp["cache_k_a"][b, 0].reshape(256, 128))
    m["cache_v_a"] = f(inp["cache_v_a"][b, 0].reshape(256, 128))
    m["state_ret"] = f(inp["state_ret"][b, 0])
    m["state_ssm_re"] = f(inp["state_ssm_re"][b, 0])
    m["state_ssm_im"] = f(inp["state_ssm_im"][b, 0])
    m["cache_k_d"] = f(inp["cache_k_d"][b, 0].reshape(256, 512))
    m["cache_v_d"] = f(inp["cache_v_d"][b, 0].reshape(256, 512))
    for nm in W_NAMES:
        m[nm] = f(inp[nm])
    return m


_NC_CACHE = {}


def kernel(**inp):
    inp = {k: np.asarray(v) for k, v in inp.items()}
    if "nc" not in _NC_CACHE:
        _NC_CACHE["nc"] = Builder().nc
    nc = _NC_CACHE["nc"]
    in_maps = [core_inputs(inp, b) for b in range(NCORES)]
    res = run_bass_kernel_spmd(nc, in_maps, core_ids=list(range(NCORES))).results
    cat = lambda k: [np.asarray(r[k]) for r in res]
    y = cat("y")
    y_sample = np.stack([a[:TS] for a in y], 0).reshape(8, 4096, D)
    y_prompt = np.concatenate([a[TS:].reshape(4, 256, D) for a in y], 0)
    nka = np.concatenate([a.reshape(4, 1, 256, 2, 64) for a in cat("nka")], 0)
    nva = np.concatenate([a.reshape(4, 1, 256, 2, 64) for a in cat("nva")], 0)
    nret = np.concatenate([a.reshape(4, 1, 2, 8, 64, 64) for a in cat("nret")], 0)
    nsr = np.concatenate([a.reshape(4, 1, 2, 32, 64) for a in cat("nsr")], 0)
    nsi = np.concatenate([a.reshape(4, 1, 2, 32, 64) for a in cat("nsi")], 0)
    nkd = np.concatenate([a.reshape(4, 1, 256, 4, 2, 64) for a in cat("nkd")], 0)
    nvd = np.concatenate([a.reshape(4, 1, 256, 4, 128) for a in cat("nvd")], 0)
    f = lambda a: np.ascontiguousarray(a, dtype=np.float32)
    return tuple(f(a) for a in (y_prompt, y_sample, nka, nva, nret, nsr, nsi, nkd, nvd))
```
